# Optimizing a Trainium2 kernel written in Bass

```python
import math
import jax, jax.numpy as jnp
from jax import lax
import numpy as np

D_MODEL = 4096
BATCH = 1
SEQ = 8192
DEPTH = 1

GRID_W = 64
CTX_LEN = 256
MLA_HEADS = 16
QK_NOPE = 128
QK_ROPE = 64
V_HEAD = 128
Q_LORA = 768
KV_LORA = 512
ROPE_THETA = 10000.0
Q_BLOCK = 128
SOFTMAX_SCALE = 1.0 / math.sqrt(QK_NOPE + QK_ROPE)
CONV_CH = 2048
CONV_GROUPS = 16
CONV_K = 3
PEER_HEADS = 8
N_KEYS = 128
N_EXPERTS = N_KEYS * N_KEYS
PEER_TOPK = 16
PEER_DK = 256
PEER_BLOCK = 32
MLA_WIDTH = MLA_HEADS * V_HEAD
MIX_WIDTH = MLA_WIDTH + CONV_CH
O_CQ = 0
O_CKV = Q_LORA
O_KR = Q_LORA + KV_LORA
O_CONV = O_KR + QK_ROPE
IN_COLS = O_CONV + 3 * CONV_CH
N_MOD = 6
EPS = 1e-6

kernel_name = "hybrid_mla_shortconv_peer_dit"


def rmsnorm(x, g):
    xf = x.astype(jnp.float32)
    y = xf * lax.rsqrt(jnp.mean(xf * xf, axis=-1, keepdims=True) + EPS)
    return (y * g.astype(jnp.float32)).astype(x.dtype)


def adaln(cond, w, b):
    m = jax.nn.silu(cond) @ w + b
    return m.reshape(*cond.shape[:-1], N_MOD, D_MODEL)


def modulate(h, shift, scale):
    return h * (1 + scale) + shift


def axial_tables(T, dtype):
    rows = T // GRID_W
    row = jnp.repeat(jnp.arange(rows), GRID_W).astype(jnp.float32)
    col = jnp.tile(jnp.arange(GRID_W), rows).astype(jnp.float32)
    half = QK_ROPE // 2
    freqs = ROPE_THETA ** (-jnp.arange(0, half, 2, dtype=jnp.float32) / half)
    ang_r = row[:, None] * freqs
    ang_c = col[:, None] * freqs
    return tuple(t.astype(dtype) for t in (jnp.cos(ang_r), jnp.sin(ang_r), jnp.cos(ang_c), jnp.sin(ang_c)))


def rotate(x, cos, sin):
    x1, x2 = jnp.split(x, 2, axis=-1)
    return jnp.concatenate([x1 * cos - x2 * sin, x2 * cos + x1 * sin], axis=-1)


def axial_rope(x, tabs):
    cr, sr, cc, sc = tabs
    xr, xc = jnp.split(x, 2, axis=-1)
    return jnp.concatenate([rotate(xr, cr, sr), rotate(xc, cc, sc)], axis=-1)


def mla_queries(cq, q_norm_g, w_uq):
    B, S, _ = cq.shape
    q = (rmsnorm(cq, q_norm_g) @ w_uq).reshape(B, S, MLA_HEADS, QK_NOPE + QK_ROPE)
    return q[..., :QK_NOPE], q[..., QK_NOPE:]


def mla_keys(ckv, kv_norm_g, w_ukv):
    B, S, _ = ckv.shape
    kv = (rmsnorm(ckv, kv_norm_g) @ w_ukv).reshape(B, S, MLA_HEADS, QK_NOPE + V_HEAD)
    return kv[..., :QK_NOPE], kv[..., QK_NOPE:]


def attend(q_nope, q_rope, k_nope, k_rope, v):
    B, S, H, _ = q_nope.shape
    nb = S // Q_BLOCK
    qn = q_nope.reshape(B, nb, Q_BLOCK, H, QK_NOPE).transpose(1, 0, 2, 3, 4)
    qr = q_rope.reshape(B, nb, Q_BLOCK, H, QK_ROPE).transpose(1, 0, 2, 3, 4)

    def block(args):
        qn_b, qr_b = args
        s = (jnp.einsum('bqhd,bkhd->bhqk', qn_b, k_nope)
             + jnp.einsum('bqhr,bkr->bhqk', qr_b, k_rope)).astype(jnp.float32) * SOFTMAX_SCALE
        p = jax.nn.softmax(s, axis=-1).astype(v.dtype)
        return jnp.einsum('bhqk,bkhd->bqhd', p, v)

    o = lax.map(block, (qn, qr))
    return o.transpose(1, 0, 2, 3, 4).reshape(B, S, H * V_HEAD)


def short_conv(pconv, w, b):
    xin, gb, gc = jnp.split(pconv, 3, axis=-1)
    u = jnp.pad(gc * xin, ((0, 0), (1, 1), (0, 0)))
    y = u[:, :-2] * w[0] + u[:, 1:-1] * w[1] + u[:, 2:] * w[2] + b
    return gb * y


def peer(h, w_q, keys, u, v):
    B, S, D = h.shape
    q = (h @ w_q).reshape(B, S, PEER_HEADS, 2, PEER_DK // 2)
    s = jnp.einsum('bshpd,hpkd->bshpk', q, keys)
    s1, i1 = lax.top_k(s[..., 0, :], PEER_TOPK)
    s2, i2 = lax.top_k(s[..., 1, :], PEER_TOPK)
    cand = (s1[..., :, None] + s2[..., None, :]).reshape(B, S, PEER_HEADS, PEER_TOPK * PEER_TOPK)
    top, flat = lax.top_k(cand, PEER_TOPK)
    e = (jnp.take_along_axis(i1, flat // PEER_TOPK, axis=-1) * N_KEYS
         + jnp.take_along_axis(i2, flat % PEER_TOPK, axis=-1))
    g = jax.nn.softmax(top.astype(jnp.float32), axis=-1).astype(h.dtype)
    nb = (B * S) // PEER_BLOCK
    hb = h.reshape(nb, PEER_BLOCK, D)
    eb = e.reshape(nb, PEER_BLOCK, PEER_HEADS * PEER_TOPK)
    gb = g.reshape(nb, PEER_BLOCK, PEER_HEADS * PEER_TOPK)

    def block(args):
        h_b, e_b, g_b = args
        act = jnp.einsum('td,tkd->tk', h_b, u[e_b])
        wgt = g_b * jax.nn.gelu(act, approximate=False)
        return jnp.einsum('tk,tkd->td', wgt, v[e_b])

    return lax.map(block, (hb, eb, gb)).reshape(B, S, D)


def setup_inputs(seed: int = 0) -> dict:
    key = jax.random.key(seed)
    ks = jax.random.split(key, 21)
    f32 = jnp.float32

    def nrm(k, shape, s):
        return jax.random.normal(k, shape, f32) * s

    D = D_MODEL
    return {
        "x": nrm(ks[0], (BATCH, SEQ, D), 1.0),
        "c": nrm(ks[1], (BATCH, D), 1.0),
        "ctx": nrm(ks[2], (BATCH, CTX_LEN, D), 1.0),
        "c_ctx": nrm(ks[3], (D,), 1.0),
        "w_ada": nrm(ks[4], (DEPTH, D, N_MOD * D), D ** -0.5),
        "b_ada": nrm(ks[5], (DEPTH, N_MOD * D), 0.01),
        "norm1_g": 1.0 + nrm(ks[6], (DEPTH, D), 0.02),
        "w_in": nrm(ks[7], (DEPTH, D, IN_COLS), D ** -0.5),
        "q_norm_g": 1.0 + nrm(ks[8], (DEPTH, Q_LORA), 0.02),
        "w_uq": nrm(ks[9], (DEPTH, Q_LORA, MLA_HEADS * (QK_NOPE + QK_ROPE)), Q_LORA ** -0.5),
        "kv_norm_g": 1.0 + nrm(ks[10], (DEPTH, KV_LORA), 0.02),
        "w_ukv": nrm(ks[11], (DEPTH, KV_LORA, MLA_HEADS * (QK_NOPE + V_HEAD)), KV_LORA ** -0.5),
        "conv_w": nrm(ks[12], (DEPTH, CONV_K, CONV_CH), CONV_K ** -0.5),
        "conv_b": nrm(ks[13], (DEPTH, CONV_CH), 0.01),
        "w_o": nrm(ks[14], (DEPTH, MIX_WIDTH, D), MIX_WIDTH ** -0.5),
        "norm2_g": 1.0 + nrm(ks[15], (DEPTH, D), 0.02),
        "peer_wq": nrm(ks[16], (DEPTH, D, PEER_HEADS * PEER_DK), D ** -0.5),
        "peer_keys": nrm(ks[17], (DEPTH, PEER_HEADS, 2, N_KEYS, PEER_DK // 2), (PEER_DK // 2) ** -0.5),
        "peer_u": nrm(ks[18], (DEPTH, N_EXPERTS, D), D ** -0.5),
        "peer_v": nrm(ks[19], (DEPTH, N_EXPERTS, D), PEER_HEADS ** -0.5),
        "final_norm_g": 1.0 + nrm(ks[20], (D,), 0.02),
    }


def reference(x, c, ctx, c_ctx, w_ada, b_ada, norm1_g, w_in, q_norm_g, w_uq, kv_norm_g,
              w_ukv, conv_w, conv_b, w_o, norm2_g, peer_wq, peer_keys, peer_u, peer_v,
              final_norm_g):
    T = x.shape[1]
    tabs_k = axial_tables(T, x.dtype)
    tabs_q = tuple(t[:, None, :] for t in tabs_k)
    xc = ctx
    for i in range(DEPTH):
        last = i == DEPTH - 1
        m_x = adaln(c, w_ada[i], b_ada[i])[:, None]
        m_c = adaln(c_ctx, w_ada[i], b_ada[i])

        hx = modulate(rmsnorm(x, norm1_g[i]), m_x[..., 0, :], m_x[..., 1, :])
        hc = modulate(rmsnorm(xc, norm1_g[i]), m_c[0], m_c[1])
        px = hx @ w_in[i]
        if last:
            pc = hc @ w_in[i][:, O_CKV:O_CONV]
            ckv_c, kr_c = pc[..., :KV_LORA], pc[..., KV_LORA:]
        else:
            pc = hc @ w_in[i]
            ckv_c, kr_c = pc[..., O_CKV:O_KR], pc[..., O_KR:O_CONV]

        kn_c, v_c = mla_keys(ckv_c, kv_norm_g[i], w_ukv[i])
        kn_x, v_x = mla_keys(px[..., O_CKV:O_KR], kv_norm_g[i], w_ukv[i])
        kr_x = axial_rope(px[..., O_KR:O_CONV], tabs_k)
        qn_x, qr_x = mla_queries(px[..., O_CQ:O_CKV], q_norm_g[i], w_uq[i])
        qr_x = axial_rope(qr_x, tabs_q)
        att_x = attend(qn_x, qr_x,
                       jnp.concatenate([kn_x, kn_c], axis=1),
                       jnp.concatenate([kr_x, kr_c], axis=1),
                       jnp.concatenate([v_x, v_c], axis=1))
        conv_x = short_conv(px[..., O_CONV:], conv_w[i], conv_b[i])
        x = x + m_x[..., 2, :] * (jnp.concatenate([att_x, conv_x], axis=-1) @ w_o[i])

        h2 = modulate(rmsnorm(x, norm2_g[i]), m_x[..., 3, :], m_x[..., 4, :])
        x = x + m_x[..., 5, :] * peer(h2, peer_wq[i], peer_keys[i], peer_u[i], peer_v[i])

        if not last:
            qn_c, qr_c = mla_queries(pc[..., O_CQ:O_CKV], q_norm_g[i], w_uq[i])
            att_c = attend(qn_c, qr_c, kn_c, kr_c, v_c)
            conv_c = short_conv(pc[..., O_CONV:], conv_w[i], conv_b[i])
            xc = xc + m_c[2] * (jnp.concatenate([att_c, conv_c], axis=-1) @ w_o[i])
            h2c = modulate(rmsnorm(xc, norm2_g[i]), m_c[3], m_c[4])
            xc = xc + m_c[5] * peer(h2c, peer_wq[i], peer_keys[i], peer_u[i], peer_v[i])
    return rmsnorm(x, final_norm_g)
```

```python
import math
from contextlib import ExitStack

import numpy as np
import concourse.bass as bass
import concourse.mybir as mybir
from concourse.bass_utils import run_bass_kernel_spmd

F32 = mybir.dt.float32
BF16 = mybir.dt.bfloat16
AF = mybir.ActivationFunctionType
ALU = mybir.AluOpType
AX = mybir.AxisListType

NCORES = 8
EPS = 1e-6
NEG = -1.0e30

CFG_FULL = dict(D=4096, T=8192, TC=256, H=16, QL=768, KVL=512, CC=2048, PH=8)


class Buf:
    __slots__ = ("w", "r", "excl")

    def __init__(self, excl=False):
        self.w = None
        self.r = []
        self.excl = excl


class Sched:
    NS = 8
    EPOCH = 60000

    def __init__(self, nc, stack):
        self.nc = nc
        self.stack = stack
        self.eng = {"pe": nc.tensor, "act": nc.scalar, "dve": nc.vector, "pool": nc.gpsimd, "sp": nc.sync}
        self.sem = {}
        self.cnt = {}
        self.nsem = 0
        self.seen = {k: {} for k in self.eng}
        self.all_sems = {}
        for k in self.eng:
            self._new_eng_sem(k)
        self.dsem = {}
        self.dval = {}
        self.dslot = {}
        for q in ("sp", "pool", "act"):
            self.dsem[q] = [self._mksem(f"d_{q}{i}") for i in range(self.NS)]
            self.dval[q] = [0] * self.NS
            self.dslot[q] = 0
        self.n_instr = 0

    def _mksem(self, name):
        s = self.stack.enter_context(self.nc.semaphore(f"{name}_{self.nsem}"))
        self.nsem += 1
        self.all_sems[id(s)] = s
        return s

    def _new_eng_sem(self, k):
        self.sem[k] = self._mksem(f"e_{k}")
        self.cnt[k] = 0

    def _wait(self, e, ev):
        sem, val, src = ev
        if e == "pe" and src == "pe":
            return
        seen = self.seen[e]
        key = id(sem)
        if seen.get(key, 0) >= val:
            return
        self.eng[e].wait_ge(sem, val)
        seen[key] = val

    def _deps(self, e, reads, writes):
        for b in reads:
            if b.w is not None:
                self._wait(e, b.w)
            if b.excl:
                for ev in b.r:
                    self._wait(e, ev)
        for b in writes:
            if b.w is not None:
                self._wait(e, b.w)
            for ev in b.r:
                self._wait(e, ev)

    def _record(self, ev, reads, writes):
        for b in reads:
            if b.excl:
                b.w = ev
                b.r = []
            else:
                b.r.append(ev)
                if len(b.r) > 24:
                    b.r = b.r[-24:]
        for b in writes:
            b.w = ev
            b.r = []

    def op(self, e, fn, reads=(), writes=()):
        if self.cnt[e] >= self.EPOCH:
            self._new_eng_sem(e)
        self._deps(e, reads, writes)
        ins = fn(self.eng[e])
        self.cnt[e] += 1
        ins.then_inc(self.sem[e], 1)
        self.n_instr += 1
        ev = (self.sem[e], self.cnt[e], e)
        self._record(ev, reads, writes)
        return ev

    def dma(self, q, out, in_, reads=(), writes=(), **kw):
        self._deps(q, reads, writes)
        slot = self.dslot[q]
        self.dslot[q] = (slot + 1) % self.NS
        sem = self.dsem[q][slot]
        prev = self.dval[q][slot]
        if prev > 0:
            self._wait(q, (sem, prev, "dma"))
        ins = self.eng[q].dma_start(out=out, in_=in_, **kw)
        ins.then_inc(sem, 16)
        self.dval[q][slot] = prev + 16
        self.n_instr += 1
        ev = (sem, prev + 16, "dma")
        self._record(ev, reads, writes)
        return ev

    def barrier(self):
        evs = []
        for k in self.eng:
            if self.cnt[k] > 0:
                evs.append((self.sem[k], self.cnt[k], k))
        for q in self.dsem:
            for i in range(self.NS):
                if self.dval[q][i] > 0:
                    evs.append((self.dsem[q][i], self.dval[q][i], "dma"))
        for e in self.eng:
            for ev in evs:
                if ev[2] == e and e != "pe":
                    continue
                sem, val, src = ev
                seen = self.seen[e]
                if seen.get(id(sem), 0) >= val:
                    continue
                self.eng[e].wait_ge(sem, val)
                seen[id(sem)] = val

    def final_wait(self, e="sp"):
        for q in self.dsem:
            for i in range(self.NS):
                if self.dval[q][i] > 0:
                    self._wait(e, (self.dsem[q][i], self.dval[q][i], "dma"))


def build(cfg, stop=None):
    D = cfg["D"]; T = cfg["T"]; TC = cfg["TC"]; H = cfg["H"]; QL = cfg["QL"]; KVL = cfg["KVL"]
    CC = cfg["CC"]; PH = cfg["PH"]
    TL = T // NCORES
    DC = D // 128
    NT = T // 128
    NTC = TC // 128
    NTL = TL // 128
    KT = T + TC
    NKC = KT // 128
    CCH = CC // 128
    MIX = H * 128 + CC
    MC = MIX // 128
    INC = QL + KVL + 64 + 3 * CC
    O_CKV = QL
    O_CONV = QL + KVL + 64
    QC = QL // 128
    KC4 = KVL // 128
    NE = 16384
    HP = PH * 2
    QG = min(512, TL)
    NQG = TL // QG
    DW = min(512, D)
    NDW = D // DW
    SCALE = 1.0 / math.sqrt(192.0)
    assert H % 2 == 0 and KVL <= 512 and 6 * DC * 2 <= 512

    nc = bass.Bass("TRN2", target_bir_lowering=False)

    def din(name, shape, dt=F32):
        return nc.dram_tensor(name, list(shape), dt, kind="ExternalInput").ap()

    x_full = din("x_full", [T, D]); x_own = din("x_own", [TL, D]); x_halo = din("x_halo", [2, D])
    hmask = din("hmask", [128, 2]); ctx = din("ctx", [TC, D])
    c_in = din("c", [D]); cc_in = din("c_ctx", [D])
    w_ada = din("w_ada", [D, 6 * D]); b_ada = din("b_ada", [6 * D])
    norm1_g = din("norm1_g", [D]); w_in = din("w_in", [D, INC]); q_norm_g = din("q_norm_g", [QL])
    w_uq = din("w_uq", [QL, H * 192]); kv_norm_g = din("kv_norm_g", [KVL]); w_ukv = din("w_ukv", [KVL, H * 256])
    conv_w = din("conv_w", [3, CC]); conv_b = din("conv_b", [CC]); w_o = din("w_o", [MIX, D])
    norm2_g = din("norm2_g", [D]); peer_wq = din("peer_wq", [D, HP * 128])
    peer_keys = din("peer_keys", [PH, 2, 128, 128]); peer_u = din("peer_u", [NE, D]); peer_v = din("peer_v", [NE, D])
    final_g = din("final_norm_g", [D]); rope_all = din("rope_all", [T, 64]); rope_own = din("rope_own", [TL, 64])
    ident_in = din("ident", [128, 128])
    out_d = nc.dram_tensor("out", [TL, D], F32, kind="ExternalOutput").ap()
    x1d = nc.dram_tensor("x1d", [TL, D], F32).ap()
    convd = nc.dram_tensor("convd", [CC, TL], BF16).ap()
    attd = nc.dram_tensor("attd", [H * 128, TL], BF16).ap()
    WdT = nc.dram_tensor("WdT", [NE, TL], BF16).ap()
    WgT = nc.dram_tensor("WgT", [NE, TL], BF16).ap()

    with ExitStack() as top:
        S = Sched(nc, top)
        top.enter_context(nc.allow_non_contiguous_dma(reason="small per-feature vectors"))
        top.enter_context(nc.allow_low_precision(reason="bf16 matmul operands"))

        def sb(stack, name, shape, dt):
            t = stack.enter_context(nc.sbuf_tensor(name, list(shape), dt))
            return t, Buf()

        PSF = []; PSB = []
        for i in range(8):
            t = top.enter_context(nc.psum_tensor(f"ps{i}", [128, 512], F32))
            PSF.append(t); PSB.append(Buf(excl=True))
        ps_rr = [0]
        ps_res = set()

        def psum(reserve=False):
            assert len(ps_res) < 8, "all PSUM banks reserved"
            while True:
                i = ps_rr[0]; ps_rr[0] = (i + 1) % 8
                if i not in ps_res:
                    break
            if reserve:
                ps_res.add(i)
            return PSF[i], PSB[i]

        dbg_stack = ExitStack()
        mid = ExitStack()
        kvs = ExitStack()

        def finish(tile_ap=None, buf=None, n=None, is_bf=False, close=()):
            if tile_ap is not None:
                if is_bf:
                    tmp, tb = sb(dbg_stack, "dbgtmp", [128, n], F32)
                    S.op("dve", lambda e: e.tensor_copy(out=tmp[:], in_=tile_ap), reads=[buf], writes=[tb])
                    S.dma("sp", out_d[0:128, 0:n], tmp[:], reads=[tb])
                else:
                    S.dma("sp", out_d[0:128, 0:n], tile_ap, reads=[buf])
            S.final_wait("sp")
            S.barrier()
            print("built (stop=%s): instrs %d sems %d" % (stop, S.n_instr, S.nsem))
            dbg_stack.close()
            for st_ in close:
                st_.close()
            kvs.close(); mid.close()
            return nc

        def finish_dram(src_ap, n, is_bf=False, close=()):
            t_, tb_ = sb(dbg_stack, "dbgd", [128, n], BF16 if is_bf else F32)
            S.dma("sp", t_[:], src_ap, writes=[tb_])
            return finish(t_[:], tb_, n, is_bf=is_bf, close=close)

        def r3(ap, b=128):
            return ap.rearrange("p (a b) -> p a b", b=b)

        def pbf(pt):
            return pt[:].bitcast(BF16)

        ident_f, b_idf = sb(top, "ident_f", [128, 128], F32)
        ident_b, b_idb = sb(top, "ident_b", [128, 128], BF16)
        ones_f, b_ones = sb(top, "ones_f", [128, 128], F32)
        modx, b_modx = sb(top, "modx", [128, 6 * DC], F32)
        modc, b_modc = sb(top, "modc", [128, 2 * DC], F32)
        gs1, b_gs1 = sb(top, "gs1", [128, DC], F32)
        gs1c, b_gs1c = sb(top, "gs1c", [128, DC], F32)
        gs2, b_gs2 = sb(top, "gs2", [128, DC], F32)
        gq, b_gq = sb(top, "gq", [128, QC], F32)
        gkv, b_gkv = sb(top, "gkv", [128, KC4], F32)
        gfin, b_gfin = sb(top, "gfin", [128, DC], F32)
        epsb, b_eps = sb(top, "epsb", [128, 1], F32)

        S.dma("sp", ident_f[:], ident_in[:, :], writes=[b_idf])
        S.op("dve", lambda e: e.tensor_copy(out=ident_b[:], in_=ident_f[:]), reads=[b_idf], writes=[b_idb])
        S.op("dve", lambda e: e.memset(ones_f[:], 1.0), writes=[b_ones])
        S.op("dve", lambda e: e.memset(epsb[:], EPS), writes=[b_eps])

        def vec_pc(dst, buf, src, n):
            S.dma("sp", dst, src.rearrange("(c p) -> p c", p=128), writes=[buf])

        vec_pc(gq[:], b_gq, q_norm_g, QC)
        vec_pc(gkv[:], b_gkv, kv_norm_g, KC4)
        vec_pc(gfin[:], b_gfin, final_g, DC)

        def rstd_from_ss(ss, b_ss, rs, b_rs, n):
            S.op("act", lambda e: e.activation(out=rs, in_=ss, func=AF.Sqrt, bias=epsb[:ss.shape[0], :], scale=1.0 / n),
                 reads=[b_ss, b_eps], writes=[b_rs])
            S.op("dve", lambda e: e.reciprocal(out=rs, in_=rs), reads=[b_rs], writes=[b_rs])

        def bcast_rows(stack, name, vec, b_vec):
            dst, b_dst = sb(stack, name, [128, D], F32)
            with ExitStack() as st:
                dg, b_dg = sb(st, name + "_dg", [128, 2, 128], F32)
                for dc in range(DC):
                    k = dc % 2
                    S.op("dve", lambda e: e.tensor_scalar(out=dg[:, k, :], in0=ident_f[:], scalar1=vec[:, dc:dc + 1],
                                                          scalar2=None, op0=ALU.mult),
                         reads=[b_idf, b_vec], writes=[b_dg])
                    pt, pb = psum()
                    S.op("pe", lambda e: e.matmul(pt[:, 0:128], lhsT=ones_f[:], rhs=dg[:, k, :], start=True, stop=True),
                         reads=[b_ones, b_dg], writes=[pb])
                    S.op("act", lambda e: e.copy(out=dst[:, dc * 128:(dc + 1) * 128], in_=pt[:, 0:128]),
                         reads=[pb], writes=[b_dst])
                S.barrier()
            return dst, b_dst

        with ExitStack() as ph:
            cin, b_cin = sb(ph, "cin", [128, 2, DC], F32)
            scv, b_scv = sb(ph, "scv", [128, DC, 2], F32)
            bt, b_bt = sb(ph, "bt", [128, 6 * DC], F32)
            g1t, b_g1t = sb(ph, "g1t", [128, DC], F32)
            g2t, b_g2t = sb(ph, "g2t", [128, DC], F32)
            BW = 256
            NBLK = 6 * D // BW
            wad = [sb(ph, f"wad{i}", [128, DC, BW], F32) for i in range(3)]
            wab = [sb(ph, f"wab{i}", [128, DC, BW], BF16) for i in range(2)]
            scb, b_scb = sb(ph, "scb", [128, DC, 2], BF16)
            S.dma("sp", cin[:, 0, :], c_in.rearrange("(c p) -> p c", p=128), writes=[b_cin])
            S.dma("sp", cin[:, 1, :], cc_in.rearrange("(c p) -> p c", p=128), writes=[b_cin])
            for j in range(6):
                S.dma("sp", bt[:, j * DC:(j + 1) * DC], b_ada[j * D:(j + 1) * D].rearrange("(c p) -> p c", p=128),
                      writes=[b_bt])
            vec_pc(g1t[:], b_g1t, norm1_g, DC)
            vec_pc(g2t[:], b_g2t, norm2_g, DC)
            for i in range(2):
                S.op("act", lambda e: e.activation(out=scv[:, :, i], in_=cin[:, i, :], func=AF.Silu),
                     reads=[b_cin], writes=[b_scv])
            S.op("dve", lambda e: e.tensor_copy(out=scb[:], in_=scv[:]), reads=[b_scv], writes=[b_scb])
            mps, mpb = psum()
            HDC = max(1, DC // 2)
            for blk in range(NBLK):
                wt, wb = wad[blk % 3]
                wtb, wbb = wab[blk % 2]
                S.dma("sp", wt[:], w_ada[:, blk * BW:(blk + 1) * BW].rearrange("(kc p) n -> p kc n", p=128), writes=[wb])
                if DC > 1:
                    S.op("dve", lambda e: e.tensor_copy(out=wtb[:, 0:HDC, :], in_=wt[:, 0:HDC, :]), reads=[wb], writes=[wbb])
                    S.op("act", lambda e: e.copy(out=wtb[:, HDC:DC, :], in_=wt[:, HDC:DC, :]), reads=[wb], writes=[wbb])
                else:
                    S.op("dve", lambda e: e.tensor_copy(out=wtb[:], in_=wt[:]), reads=[wb], writes=[wbb])
                for j in range(BW // 128):
                    col = blk * (BW // 128) + j
                    for kc in range(DC):
                        S.op("pe", lambda e: e.matmul(mps[:, col * 2:col * 2 + 2], lhsT=wtb[:, kc, j * 128:(j + 1) * 128],
                                                      rhs=scb[:, kc, :], start=(kc == 0), stop=(kc == DC - 1)),
                             reads=[wbb, b_scb], writes=[mpb])
            mview = mps[:, 0:12 * DC].rearrange("p (c two) -> p c two", two=2)
            S.op("dve", lambda e: e.tensor_tensor(out=modx[:], in0=mview[:, :, 0], in1=bt[:], op=ALU.add),
                 reads=[mpb, b_bt], writes=[b_modx])
            S.op("dve", lambda e: e.tensor_tensor(out=modc[:], in0=mview[:, 0:2 * DC, 1], in1=bt[:, 0:2 * DC], op=ALU.add),
                 reads=[mpb, b_bt], writes=[b_modc])
            S.op("dve", lambda e: e.scalar_tensor_tensor(out=gs1[:], in0=modx[:, DC:2 * DC], scalar=1.0, in1=g1t[:],
                                                         op0=ALU.add, op1=ALU.mult), reads=[b_modx, b_g1t], writes=[b_gs1])
            S.op("dve", lambda e: e.scalar_tensor_tensor(out=gs1c[:], in0=modc[:, DC:2 * DC], scalar=1.0, in1=g1t[:],
                                                         op0=ALU.add, op1=ALU.mult), reads=[b_modc, b_g1t], writes=[b_gs1c])
            S.op("dve", lambda e: e.scalar_tensor_tensor(out=gs2[:], in0=modx[:, 4 * DC:5 * DC], scalar=1.0, in1=g2t[:],
                                                         op0=ALU.add, op1=ALU.mult), reads=[b_modx, b_g2t], writes=[b_gs2])
            S.barrier()
        sh1 = modx[:, 0:DC]; sh1c = modc[:, 0:DC]; sh2 = modx[:, 3 * DC:4 * DC]
        gate1 = modx[:, 2 * DC:3 * DC]; gate2 = modx[:, 5 * DC:6 * DC]
        if stop == "A":
            return finish(modx[:, :], b_modx, 6 * DC)

        class NormT:
            def __init__(self, stack, tag):
                self.xt = [sb(stack, f"{tag}_x{i}", [128, D], F32) for i in range(2)]
                self.xs = [sb(stack, f"{tag}_xs0", [128, D], BF16)] * 2
                self.junk = self.xs[0]
                self.ss = [sb(stack, f"{tag}_ss{i}", [128, 1], F32) for i in range(2)]
                self.rs = [sb(stack, f"{tag}_rs{i}", [128, 1], F32) for i in range(2)]
                self.k = 0

            def run(self, src, n, gsv, b_gsv, shv, b_shv, dst, b_dst):
                k = self.k; self.k ^= 1
                xt, bx = self.xt[k]; xs, bxs = self.xs[k]; jk, bj = self.junk
                ss, bss = self.ss[k]; rs, brs = self.rs[k]
                S.dma("sp", xt[:n, :], src, writes=[bx])
                S.op("act", lambda e: e.activation(out=jk[:n, :], in_=xt[:n, :], func=AF.Square, accum_out=ss[:n, :]),
                     reads=[bx], writes=[bj, bss])
                rstd_from_ss(ss[:n, :], bss, rs[:n, :], brs, D)
                S.op("dve", lambda e: e.tensor_scalar(out=xs[:n, :], in0=xt[:n, :], scalar1=rs[:n, 0:1], scalar2=None,
                                                      op0=ALU.mult), reads=[bx, brs], writes=[bxs])
                for g0 in range(0, DC, 8):
                    g1 = min(DC, g0 + 8)
                    pt, pb = psum()
                    pv = pbf(pt)
                    for dc in range(g0, g1):
                        S.op("pe", lambda e: e.transpose(out=pv[:, (dc - g0) * 128:(dc - g0) * 128 + n],
                                                         in_=xs[:n, dc * 128:(dc + 1) * 128], identity=ident_b[:n, :n]),
                             reads=[bxs, b_idb], writes=[pb])
                    for dc in range(g0, g1):
                        src_ps = pv[:, (dc - g0) * 128:(dc - g0) * 128 + n]
                        if dc % 2 == 0:
                            S.op("act", lambda e: e.activation(out=dst[:, dc, :], in_=src_ps, func=AF.Identity,
                                                               bias=shv[:, dc:dc + 1], scale=gsv[:, dc:dc + 1]),
                                 reads=[pb, b_gsv, b_shv], writes=[b_dst])
                        else:
                            S.op("dve", lambda e: e.tensor_scalar(out=dst[:, dc, :], in0=src_ps, scalar1=gsv[:, dc:dc + 1],
                                                                  scalar2=shv[:, dc:dc + 1], op0=ALU.mult, op1=ALU.add),
                                 reads=[pb, b_gsv, b_shv], writes=[b_dst])

        def cast_load(dst, b_dst, src):
            S.dma("pool", dst, src, writes=[b_dst])

        def mm_acc(pt_ap, pb, pairs, reads):
            n = len(pairs)
            for i, (l, r) in enumerate(pairs):
                S.op("pe", lambda e: e.matmul(pt_ap, lhsT=l, rhs=r, start=(i == 0), stop=(i == n - 1)),
                     reads=reads, writes=[pb])

        cqT, b_cqT = sb(kvs, "cqT", [128, QC, TL], BF16)

        with ExitStack() as ph:
            hxo, b_hxo = sb(ph, "hxo", [128, DC, TL], BF16)
            hxh, b_hxh = sb(ph, "hxh", [128, DC, 2], BF16)
            with ExitStack() as st:
                nt_ = NormT(st, "nC")
                for tt in range(NTL):
                    nt_.run(x_own[tt * 128:(tt + 1) * 128, :], 128, gs1, b_gs1, sh1, b_modx,
                            hxo[:, :, tt * 128:(tt + 1) * 128], b_hxo)
                nt_.run(x_halo[:, :], 2, gs1, b_gs1, sh1, b_modx, hxh[:, :, :], b_hxh)
                S.barrier()
            with ExitStack() as st:
                wqa, b_wqa = sb(st, "wqa", [128, DC, QL], BF16)
                for dc in range(DC):
                    cast_load(wqa[:, dc, :], b_wqa, w_in[dc * 128:(dc + 1) * 128, 0:QL])
                cqn, b_cqn = sb(st, "cqn", [128, QL], BF16)
                jq, b_jq = sb(st, "jq", [128, 512], BF16)
                ssq, b_ssq = sb(st, "ssq", [128, 4], F32)
                rsq, b_rsq = sb(st, "rsq", [128, 1], F32)
                pieces = [(o, min(512, QL - o)) for o in range(0, QL, 512)]
                for tt in range(NTL):
                    pts = []
                    for pi, (o, w) in enumerate(pieces):
                        pt, pb = psum()
                        mm_acc(pt[:, 0:w], pb, [(hxo[:, dc, tt * 128:(tt + 1) * 128], wqa[:, dc, o:o + w]) for dc in range(DC)],
                               [b_hxo, b_wqa])
                        S.op("act", lambda e: e.activation(out=jq[:, 0:w], in_=pt[:, 0:w], func=AF.Square,
                                                           accum_out=ssq[:, pi:pi + 1]), reads=[pb], writes=[b_jq, b_ssq])
                        pts.append((pt, pb))
                    if len(pieces) > 1:
                        S.op("dve", lambda e: e.tensor_reduce(out=ssq[:, 3:4], in_=ssq[:, 0:len(pieces)], axis=AX.X, op=ALU.add),
                             reads=[b_ssq], writes=[b_ssq])
                        sst = ssq[:, 3:4]
                    else:
                        sst = ssq[:, 0:1]
                    rstd_from_ss(sst, b_ssq, rsq[:], b_rsq, QL)
                    for (o, w), (pt, pb) in zip(pieces, pts):
                        S.op("dve", lambda e: e.tensor_scalar(out=cqn[:, o:o + w], in0=pt[:, 0:w], scalar1=rsq[:, 0:1],
                                                              scalar2=None, op0=ALU.mult), reads=[pb, b_rsq], writes=[b_cqn])
                    pt, pb = psum()
                    pv = pbf(pt)
                    for cc in range(QC):
                        S.op("pe", lambda e: e.transpose(out=pv[:, cc * 128:(cc + 1) * 128], in_=cqn[:, cc * 128:(cc + 1) * 128],
                                                         identity=ident_b[:]), reads=[b_cqn, b_idb], writes=[pb])
                    for cc in range(QC):
                        S.op("act", lambda e: e.activation(out=cqT[:, cc, tt * 128:(tt + 1) * 128], in_=pv[:, cc * 128:(cc + 1) * 128],
                                                           func=AF.Identity, scale=gq[:, cc:cc + 1]),
                             reads=[pb, b_gq], writes=[b_cqT])
                S.barrier()
            with ExitStack() as st:
                cw, b_cw = sb(st, "cw", [128, 3, CCH], F32)
                cb, b_cb = sb(st, "cb", [128, CCH], F32)
                hm, b_hm = sb(st, "hm", [128, 2], F32)
                for r in range(3):
                    S.dma("sp", cw[:, r, :], conv_w[r, :].rearrange("(j p) -> p j", p=128), writes=[b_cw])
                S.dma("sp", cb[:], conv_b.rearrange("(j p) -> p j", p=128), writes=[b_cb])
                S.dma("sp", hm[:], hmask[:, :], writes=[b_hm])
                wc = [[sb(st, f"wc{k}_{r}", [128, DC, 128], BF16) for r in range(3)] for k in range(2)]
                u_t, b_u = sb(st, "u_t", [128, TL + 2], F32)
                xin_s, b_xin = sb(st, "xin_s", [128, TL + 2], F32)
                gb_s, b_gbs = sb(st, "gb_s", [128, TL], F32)
                y_t, b_y = sb(st, "y_t", [128, TL], F32)
                cv = [sb(st, f"cv{k}", [128, TL], BF16) for k in range(2)]
                for j in range(CCH):
                    ws = wc[j % 2]
                    for r in range(3):
                        c0 = O_CONV + r * CC + j * 128
                        cast_load(ws[r][0][:], ws[r][1], w_in[:, c0:c0 + 128].rearrange("(dc p) n -> p dc n", p=128))
                    for tg in range(NQG):
                        pt, pb = psum()
                        mm_acc(pt[:, 0:QG], pb, [(ws[0][0][:, dc, :], hxo[:, dc, tg * QG:(tg + 1) * QG]) for dc in range(DC)],
                               [ws[0][1], b_hxo])
                        S.op("act", lambda e: e.copy(out=xin_s[:, 1 + tg * QG:1 + (tg + 1) * QG], in_=pt[:, 0:QG]),
                             reads=[pb], writes=[b_xin])
                    pt, pb = psum()
                    mm_acc(pt[:, 0:2], pb, [(ws[0][0][:, dc, :], hxh[:, dc, :]) for dc in range(DC)], [ws[0][1], b_hxh])
                    S.op("act", lambda e: e.copy(out=xin_s[:, 0:1], in_=pt[:, 0:1]), reads=[pb], writes=[b_xin])
                    S.op("act", lambda e: e.copy(out=xin_s[:, TL + 1:TL + 2], in_=pt[:, 1:2]), reads=[pb], writes=[b_xin])
                    for tg in range(NQG):
                        pt, pb = psum()
                        mm_acc(pt[:, 0:QG], pb, [(ws[1][0][:, dc, :], hxo[:, dc, tg * QG:(tg + 1) * QG]) for dc in range(DC)],
                               [ws[1][1], b_hxo])
                        S.op("act", lambda e: e.copy(out=gb_s[:, tg * QG:(tg + 1) * QG], in_=pt[:, 0:QG]),
                             reads=[pb], writes=[b_gbs])
                    for tg in range(NQG):
                        pt, pb = psum()
                        mm_acc(pt[:, 0:QG], pb, [(ws[2][0][:, dc, :], hxo[:, dc, tg * QG:(tg + 1) * QG]) for dc in range(DC)],
                               [ws[2][1], b_hxo])
                        S.op("dve", lambda e: e.tensor_tensor(out=u_t[:, 1 + tg * QG:1 + (tg + 1) * QG], in0=pt[:, 0:QG],
                                                              in1=xin_s[:, 1 + tg * QG:1 + (tg + 1) * QG], op=ALU.mult),
                             reads=[pb, b_xin], writes=[b_u])
                    pt, pb = psum()
                    mm_acc(pt[:, 0:2], pb, [(ws[2][0][:, dc, :], hxh[:, dc, :]) for dc in range(DC)], [ws[2][1], b_hxh])
                    for (hc, uc) in ((0, 0), (1, TL + 1)):
                        S.op("dve", lambda e: e.scalar_tensor_tensor(out=u_t[:, uc:uc + 1], in0=pt[:, hc:hc + 1],
                                                                     scalar=hm[:, hc:hc + 1], in1=xin_s[:, uc:uc + 1],
                                                                     op0=ALU.mult, op1=ALU.mult),
                             reads=[pb, b_hm, b_xin], writes=[b_u])
                    S.op("dve", lambda e: e.tensor_scalar(out=y_t[:], in0=u_t[:, 1:TL + 1], scalar1=cw[:, 1, j:j + 1],
                                                          scalar2=cb[:, j:j + 1], op0=ALU.mult, op1=ALU.add),
                         reads=[b_u, b_cw, b_cb], writes=[b_y])
                    S.op("dve", lambda e: e.scalar_tensor_tensor(out=y_t[:], in0=u_t[:, 0:TL], scalar=cw[:, 0, j:j + 1],
                                                                 in1=y_t[:], op0=ALU.mult, op1=ALU.add),
                         reads=[b_u, b_cw, b_y], writes=[b_y])
                    S.op("dve", lambda e: e.scalar_tensor_tensor(out=y_t[:], in0=u_t[:, 2:TL + 2], scalar=cw[:, 2, j:j + 1],
                                                                 in1=y_t[:], op0=ALU.mult, op1=ALU.add),
                         reads=[b_u, b_cw, b_y], writes=[b_y])
                    cvt, cvb = cv[j % 2]
                    S.op("dve", lambda e: e.tensor_tensor(out=cvt[:], in0=y_t[:], in1=gb_s[:], op=ALU.mult),
                         reads=[b_y, b_gbs], writes=[cvb])
                    S.dma("sp", convd[j * 128:(j + 1) * 128, :], cvt[:], reads=[cvb])
                S.barrier()
            S.barrier()

        if stop == "C":
            return finish(cqT[:, 0, 0:128], b_cqT, 128, is_bf=True)
        ckvT, b_ckvT = sb(kvs, "ckvT", [128, KC4, KT], BF16)
        krT, b_krT = sb(kvs, "krT", [128, KT], BF16)
        with ExitStack() as ph:
            nt_ = NormT(ph, "nB")
            wkv, b_wkv = sb(ph, "wkv", [128, DC, KVL + 64], BF16)
            for dc in range(DC):
                cast_load(wkv[:, dc, :], b_wkv, w_in[dc * 128:(dc + 1) * 128, O_CKV:O_CKV + KVL + 64])
            hxt = [sb(ph, f"hxt{i}", [128, DC, 128], BF16) for i in range(2)]
            jk2, b_jk2 = sb(ph, "jk2", [128, 512], BF16)
            ss2 = [sb(ph, f"ss2_{i}", [128, 1], F32) for i in range(2)]
            rs2 = [sb(ph, f"rs2_{i}", [128, 1], F32) for i in range(2)]
            ckn = [sb(ph, f"ckn{i}", [128, KVL], BF16) for i in range(2)]
            krs = [sb(ph, f"krs{i}", [128, 64], F32) for i in range(2)]
            rtb = [sb(ph, f"rtb{i}", [128, 64], F32) for i in range(2)]
            tmpr = [sb(ph, f"tmpr{i}", [128, 4, 32], F32) for i in range(2)]
            krd = [sb(ph, f"krd{i}", [128, 128], BF16) for i in range(2)]
            def front(i):
                k = i % 2
                is_x = i < NT
                src = x_full[i * 128:(i + 1) * 128, :] if is_x else ctx[(i - NT) * 128:(i - NT + 1) * 128, :]
                hx, bhx = hxt[k]
                if is_x:
                    nt_.run(src, 128, gs1, b_gs1, sh1, b_modx, hx[:, :, :], bhx)
                else:
                    nt_.run(src, 128, gs1c, b_gs1c, sh1c, b_modc, hx[:, :, :], bhx)

            front(0)
            for i in range(NT + NTC):
                k = i % 2
                is_x = i < NT
                hx, bhx = hxt[k]
                if i + 1 < NT + NTC:
                    front(i + 1)
                pa, pab = psum()
                mm_acc(pa[:, 0:KVL], pab, [(hx[:, dc, :], wkv[:, dc, 0:KVL]) for dc in range(DC)], [bhx, b_wkv])
                pk, pkb = psum()
                mm_acc(pk[:, 0:64], pkb, [(hx[:, dc, :], wkv[:, dc, KVL:KVL + 64]) for dc in range(DC)], [bhx, b_wkv])
                ss, bss = ss2[k]; rs, brs = rs2[k]; cn, bcn = ckn[k]
                S.op("act", lambda e: e.activation(out=jk2[:, 0:KVL], in_=pa[:, 0:KVL], func=AF.Square, accum_out=ss[:]),
                     reads=[pab], writes=[b_jk2, bss])
                rstd_from_ss(ss[:], bss, rs[:], brs, KVL)
                S.op("dve", lambda e: e.tensor_scalar(out=cn[:], in0=pa[:, 0:KVL], scalar1=rs[:, 0:1], scalar2=None,
                                                      op0=ALU.mult), reads=[pab, brs], writes=[bcn])
                kd, bkd = krd[k]
                if is_x:
                    kr, bkr = krs[k]; rt, brt = rtb[k]; tp, btp = tmpr[k]
                    S.dma("sp", rt[:], rope_all[i * 128:(i + 1) * 128, :], writes=[brt])
                    S.op("act", lambda e: e.copy(out=kr[:], in_=pk[:, 0:64]), reads=[pkb], writes=[bkr])
                    rope_apply(S, kr, bkr, rt, brt, tp, btp, kd[:, 0:64], bkd)
                    S.op("dve", lambda e: e.tensor_copy(out=kd[:, 64:128], in_=kd[:, 0:64]), reads=[bkd], writes=[bkd])
                else:
                    S.op("act", lambda e: e.copy(out=kd[:, 0:64], in_=pk[:, 0:64]), reads=[pkb], writes=[bkd])
                    S.op("act", lambda e: e.copy(out=kd[:, 64:128], in_=pk[:, 0:64]), reads=[pkb], writes=[bkd])
                pt, pb = psum()
                pv = pbf(pt)
                for cc in range(KC4):
                    S.op("pe", lambda e: e.transpose(out=pv[:, cc * 128:(cc + 1) * 128], in_=cn[:, cc * 128:(cc + 1) * 128],
                                                     identity=ident_b[:]), reads=[bcn, b_idb], writes=[pb])
                S.op("pe", lambda e: e.transpose(out=pv[:, KC4 * 128:(KC4 + 1) * 128], in_=kd[:], identity=ident_b[:]),
                     reads=[bkd, b_idb], writes=[pb])
                for cc in range(KC4):
                    S.op("act", lambda e: e.activation(out=ckvT[:, cc, i * 128:(i + 1) * 128], in_=pv[:, cc * 128:(cc + 1) * 128],
                                                       func=AF.Identity, scale=gkv[:, cc:cc + 1]),
                         reads=[pb, b_gkv], writes=[b_ckvT])
                S.op("dve", lambda e: e.tensor_copy(out=krT[:, i * 128:(i + 1) * 128], in_=pv[:, KC4 * 128:(KC4 + 1) * 128]),
                     reads=[pb], writes=[b_krT])
            S.barrier()

        if stop == "B":
            return finish(ckvT[:, 0, 0:128], b_ckvT, 128, is_bf=True)
        with ExitStack() as ph:
            attT, b_attT = sb(ph, "attT", [128, H, TL], BF16)
            ones_b, b_onb = sb(ph, "ones_b", [128, 128], BF16)
            S.op("dve", lambda e: e.memset(ones_b[:], 1.0), writes=[b_onb])
            wuk, b_wuk = sb(ph, "wuk", [128, KC4, 128], BF16)
            wuv, b_wuv = sb(ph, "wuv", [128, KC4, 128], BF16)
            wqn, b_wqn = sb(ph, "wqn", [128, QC, 128], BF16)
            wqr, b_wqr = sb(ph, "wqr", [128, QC, 128], BF16)
            knh, b_knh = sb(ph, "knh", [128, KT], BF16)
            vh, b_vh = sb(ph, "vh", [128, NKC, 128], BF16)
            qnh, b_qnh = sb(ph, "qnh", [128, TL], BF16)
            qrT, b_qrT = sb(ph, "qrT", [128, TL], BF16)
            qrs, b_qrs = sb(ph, "qrs", [128, 128], F32)
            qrr, b_qrr = sb(ph, "qrr", [128, 128], BF16)
            rto, b_rto = sb(ph, "rto", [128, NTL, 64], F32)
            tpq, b_tpq = sb(ph, "tpq", [128, 4, 32], F32)
            NPB = 3
            Pt = [sb(ph, f"Pt{i}", [128, QG], BF16) for i in range(NPB)]
            accs, b_accs = sb(ph, "accs", [128, QG], F32)
            rsum, b_rsum = sb(ph, "rsum", [128, QG], F32)
            for tt in range(NTL):
                S.dma("sp", rto[:, tt, :], rope_own[tt * 128:(tt + 1) * 128, :], writes=[b_rto])
            SB = [psum(reserve=True) for _ in range(NPB)]
            OB = psum(reserve=True)
            for h in range(H):
                cast_load(wuk[:], b_wuk, w_ukv[:, h * 256:h * 256 + 128].rearrange("(c p) n -> p c n", p=128))
                cast_load(wuv[:], b_wuv, w_ukv[:, h * 256 + 128:h * 256 + 256].rearrange("(c p) n -> p c n", p=128))
                cast_load(wqn[:], b_wqn, w_uq[:, h * 192:h * 192 + 128].rearrange("(c p) n -> p c n", p=128))
                if h % 2 == 0:
                    for hh in range(2):
                        c0 = (h + hh) * 192 + 128
                        cast_load(wqr[:, :, hh * 64:(hh + 1) * 64], b_wqr,
                                  w_uq[:, c0:c0 + 64].rearrange("(c p) n -> p c n", p=128))
                    for tt in range(NTL):
                        pt, pb = psum()
                        mm_acc(pt[:, 0:128], pb, [(cqT[:, cc, tt * 128:(tt + 1) * 128], wqr[:, cc, :]) for cc in range(QC)],
                               [b_cqT, b_wqr])
                        S.op("act", lambda e: e.copy(out=qrs[:], in_=pt[:, 0:128]), reads=[pb], writes=[b_qrs])
                        for hh in range(2):
                            rope_apply(S, qrs, b_qrs, rto, b_rto, tpq, b_tpq, qrr[:, hh * 64:(hh + 1) * 64], b_qrr,
                                       src_off=hh * 64, rt_idx=tt)
                        pt2, pb2 = psum()
                        pv2 = pbf(pt2)
                        S.op("pe", lambda e: e.transpose(out=pv2[:, 0:128], in_=qrr[:], identity=ident_b[:]),
                             reads=[b_qrr, b_idb], writes=[pb2])
                        S.op("act", lambda e: e.activation(out=qrT[:, tt * 128:(tt + 1) * 128], in_=pv2[:, 0:128], func=AF.Copy,
                                                           scale=SCALE), reads=[pb2], writes=[b_qrT])
                for tg in range(NQG):
                    pt, pb = psum()
                    mm_acc(pt[:, 0:QG], pb, [(wqn[:, cc, :], cqT[:, cc, tg * QG:(tg + 1) * QG]) for cc in range(QC)],
                           [b_wqn, b_cqT])
                    S.op("act", lambda e: e.activation(out=qnh[:, tg * QG:(tg + 1) * QG], in_=pt[:, 0:QG], func=AF.Copy,
                                                       scale=SCALE), reads=[pb], writes=[b_qnh])
                for k0 in range(0, KT, 512):
                    w = min(512, KT - k0)
                    pt, pb = psum()
                    mm_acc(pt[:, 0:w], pb, [(wuk[:, cc, :], ckvT[:, cc, k0:k0 + w]) for cc in range(KC4)], [b_wuk, b_ckvT])
                    if (k0 // 512) % 2 == 0:
                        S.op("dve", lambda e: e.tensor_copy(out=knh[:, k0:k0 + w], in_=pt[:, 0:w]), reads=[pb], writes=[b_knh])
                    else:
                        S.op("act", lambda e: e.copy(out=knh[:, k0:k0 + w], in_=pt[:, 0:w]), reads=[pb], writes=[b_knh])
                for kc0 in range(0, NKC, 4):
                    n4 = min(4, NKC - kc0)
                    pt, pb = psum()
                    for q4 in range(n4):
                        kc = kc0 + q4
                        mm_acc(pt[:, q4 * 128:(q4 + 1) * 128], pb,
                               [(ckvT[:, cc, kc * 128:(kc + 1) * 128], wuv[:, cc, :]) for cc in range(KC4)], [b_ckvT, b_wuv])
                    if (kc0 // 4) % 2 == 0:
                        S.op("act", lambda e: e.copy(out=vh[:, kc0:kc0 + n4, :], in_=r3(pt[:, 0:n4 * 128])), reads=[pb], writes=[b_vh])
                    else:
                        S.op("dve", lambda e: e.tensor_copy(out=vh[:, kc0:kc0 + n4, :], in_=r3(pt[:, 0:n4 * 128])),
                             reads=[pb], writes=[b_vh])
                r0 = 64 * (h % 2)
                for qg in range(NQG):
                    qs = slice(qg * QG, (qg + 1) * QG)

                    def s_mm(kc):
                        st_, sbb = SB[kc % NPB]
                        S.op("pe", lambda e: e.matmul(st_[:, 0:QG], lhsT=knh[:, kc * 128:(kc + 1) * 128], rhs=qnh[:, qs],
                                                      start=True, stop=False), reads=[b_knh, b_qnh], writes=[sbb])
                        S.op("pe", lambda e: e.matmul(st_[:, 0:QG], lhsT=krT[r0:r0 + 64, kc * 128:(kc + 1) * 128],
                                                      rhs=qrT[r0:r0 + 64, qs], start=False, stop=True),
                             reads=[b_krT, b_qrT], writes=[sbb])

                    s_mm(0)
                    if NKC > 1:
                        s_mm(1)
                    for kc in range(NKC):
                        st_, sbb = SB[kc % NPB]
                        p_t, p_b = Pt[kc % NPB]
                        S.op("act", lambda e: e.activation(out=p_t[:], in_=st_[:, 0:QG], func=AF.Exp), reads=[sbb], writes=[p_b])
                        S.op("pe", lambda e: e.matmul(OB[0][:, 0:QG], lhsT=vh[:, kc, :], rhs=p_t[:], start=(kc == 0),
                                                      stop=(kc == NKC - 1)), reads=[b_vh, p_b], writes=[OB[1]])
                        if kc == 0:
                            S.op("dve", lambda e: e.tensor_copy(out=accs[:], in_=p_t[:]), reads=[p_b], writes=[b_accs])
                        else:
                            S.op("dve", lambda e: e.tensor_tensor(out=accs[:], in0=accs[:], in1=p_t[:], op=ALU.add),
                                 reads=[p_b, b_accs], writes=[b_accs])
                        if kc + 2 < NKC:
                            s_mm(kc + 2)
                    pt, pb = psum()
                    S.op("pe", lambda e: e.matmul(pt[:, 0:QG], lhsT=ones_f[:], rhs=accs[:], start=True, stop=True),
                         reads=[b_ones, b_accs], writes=[pb])
                    S.op("dve", lambda e: e.reciprocal(out=rsum[:], in_=pt[:, 0:QG]), reads=[pb], writes=[b_rsum])
                    S.op("dve", lambda e: e.tensor_tensor(out=attT[:, h, qs], in0=OB[0][:, 0:QG], in1=rsum[:], op=ALU.mult),
                         reads=[OB[1], b_rsum], writes=[b_attT])
            S.dma("sp", attd.rearrange("(h p) t -> p h t", p=128), attT[:], reads=[b_attT])
            S.barrier()
            ps_res.clear()
            if stop == "D":
                return finish(attT[:, 0, 0:128], b_attT, 128, is_bf=True, close=[ph])
        S.barrier()
        kvs.close()

        with ExitStack() as ph:
            g1b, b_g1b = bcast_rows(ph, "g1b", gate1, b_modx)
            if stop == "E0":
                return finish(g1b[:, 0:D], b_g1b, D, close=[ph])
            mixT, b_cvT = sb(ph, "mixT", [128, MC, TL], BF16)
            b_attT = b_cvT
            S.dma("sp", mixT[:, 0:H, :], attd.rearrange("(h p) t -> p h t", p=128), writes=[b_cvT])
            S.dma("sp", mixT[:, H:MC, :], convd.rearrange("(j p) t -> p j t", p=128), writes=[b_cvT])
            cvT = mixT[:, H:MC, :]
            if stop == "E1":
                return finish(cvT[:, 0, 0:128], b_cvT, 128, is_bf=True, close=[ph])
            wo = [sb(ph, f"wo{i}", [128, MC, DW], BF16) for i in range(2)]
            xp = [sb(ph, f"xp{i}", [128, DW], F32) for i in range(3)]
            tq = [sb(ph, f"tq{i}", [128, DW], F32) for i in range(3)]
            it = 0
            for dt in range(NDW):
                wt, wb = wo[dt % 2]
                for mc in range(MC):
                    cast_load(wt[:, mc, :], wb, w_o[mc * 128:(mc + 1) * 128, dt * DW:(dt + 1) * DW])
                for tt in range(NTL):
                    ts_ = slice(tt * 128, (tt + 1) * 128)
                    pt, pb = psum()
                    pairs = [(mixT[:, mc, ts_], wt[:, mc, :]) for mc in range(MC)]
                    mm_acc(pt[:, 0:DW], pb, pairs, [b_attT, b_cvT, wb])
                    xt_, xb_ = xp[it % 3]; tt_, tb_ = tq[it % 3]; it += 1
                    S.dma("sp", xt_[:], x_own[ts_, dt * DW:(dt + 1) * DW], writes=[xb_])
                    S.op("dve", lambda e: e.tensor_tensor(out=tt_[:], in0=pt[:, 0:DW], in1=g1b[:, dt * DW:(dt + 1) * DW], op=ALU.mult),
                         reads=[pb, b_g1b], writes=[tb_])
                    S.op("dve", lambda e: e.tensor_tensor(out=tt_[:], in0=tt_[:], in1=xt_[:], op=ALU.add),
                         reads=[tb_, xb_], writes=[tb_])
                    S.dma("sp", x1d[ts_, dt * DW:(dt + 1) * DW], tt_[:], reads=[tb_])
            S.barrier()
        S.barrier()

        if stop == "E":
            return finish_dram(x1d[0:128, 0:DW], DW)
        mid.close()
        with ExitStack() as peer:
            h2T, b_h2T = sb(peer, "h2T", [128, DC, TL], BF16)
            with ExitStack() as st:
                nt_ = NormT(st, "nF")
                for tt in range(NTL):
                    nt_.run(x1d[tt * 128:(tt + 1) * 128, :], 128, gs2, b_gs2, sh2, b_modx, h2T[:, :, tt * 128:(tt + 1) * 128], b_h2T)
                S.barrier()
            if stop == "F":
                return finish(h2T[:, 0, 0:128], b_h2T, 128, is_bf=True, close=[peer])
            with ExitStack() as st:
                kT, b_kT = sb(st, "kT", [128, HP, 128], BF16)
                qT, b_qT = sb(st, "qT", [128, HP, TL], BF16)
                with ExitStack() as st2:
                    kf, b_kf = sb(st2, "kf", [128, HP, 128], F32)
                    kb_, b_kb = sb(st2, "kb", [128, HP, 128], BF16)
                    S.dma("sp", kf[:], peer_keys.rearrange("h p k d -> k (h p) d"), writes=[b_kf])
                    S.op("dve", lambda e: e.tensor_copy(out=kb_[:], in_=kf[:]), reads=[b_kf], writes=[b_kb])
                    for g0 in range(0, HP, 8):
                        pt, pb = psum()
                        pv = pbf(pt)
                        for hp in range(g0, min(HP, g0 + 8)):
                            S.op("pe", lambda e: e.transpose(out=pv[:, (hp - g0) * 128:(hp - g0 + 1) * 128], in_=kb_[:, hp, :],
                                                             identity=ident_b[:]), reads=[b_kb, b_idb], writes=[pb])
                        n8 = min(HP, g0 + 8) - g0
                        S.op("act", lambda e: e.copy(out=kT[:, g0:g0 + n8, :], in_=r3(pv[:, 0:n8 * 128])), reads=[pb], writes=[b_kT])
                    wq = [sb(st2, f"wq{i}", [128, DC, 128], BF16) for i in range(2)]
                    for hp in range(HP):
                        wt, wb = wq[hp % 2]
                        cast_load(wt[:], wb, peer_wq[:, hp * 128:(hp + 1) * 128].rearrange("(dc p) n -> p dc n", p=128))
                        for tg in range(NQG):
                            pt, pb = psum()
                            mm_acc(pt[:, 0:QG], pb, [(wt[:, dc, :], h2T[:, dc, tg * QG:(tg + 1) * QG]) for dc in range(DC)], [wb, b_h2T])
                            S.op("act", lambda e: e.copy(out=qT[:, hp, tg * QG:(tg + 1) * QG], in_=pt[:, 0:QG]), reads=[pb], writes=[b_qT])
                    S.barrier()
                ub = [sb(st, f"ub{i}", [128, D], BF16) for i in range(2)]
                uT = [sb(st, f"uT{i}", [128, DC, 128], BF16) for i in range(2)]
                wgo = [sb(st, f"wgo{i}", [128, TL], BF16) for i in range(2)]
                LW = min(2048, D)

                NEC = NE // 128

                def u_load(ec):
                    ut, utb = ub[ec % 2]
                    cast_load(ut[:].rearrange("p (a n) -> p a n", n=LW), utb,
                              peer_u[ec * 128:(ec + 1) * 128, :].rearrange("p (a n) -> p a n", n=LW))

                tr_banks = {}
                mm_banks = {}
                T_BANKS = [psum(reserve=True) for _ in range(-(-DC // 8))]
                M_BANKS = [psum(reserve=True) for _ in range(NQG)]

                def T_pe(ec):
                    if ec + 1 < NEC:
                        u_load(ec + 1)
                    ut, utb = ub[ec % 2]
                    tr_banks[ec] = []
                    for g0 in range(0, DC, 8):
                        g1 = min(DC, g0 + 8)
                        pt, pb = T_BANKS[g0 // 8]
                        pv = pbf(pt)
                        for dc in range(g0, g1):
                            S.op("pe", lambda e: e.transpose(out=pv[:, (dc - g0) * 128:(dc - g0 + 1) * 128],
                                                             in_=ut[:, dc * 128:(dc + 1) * 128], identity=ident_b[:]),
                                 reads=[utb, b_idb], writes=[pb])
                        tr_banks[ec].append((g0, g1, pv, pb))

                def T_act(ec):
                    uTt, uTb = uT[ec % 2]
                    for (g0, g1, pv, pb) in tr_banks.pop(ec):
                        S.op("act", lambda e: e.copy(out=uTt[:, g0:g1, :], in_=r3(pv[:, 0:(g1 - g0) * 128])),
                             reads=[pb], writes=[uTb])

                def M_pe(ec):
                    uTt, uTb = uT[ec % 2]
                    mm_banks[ec] = []
                    for tg in range(NQG):
                        pt, pb = M_BANKS[tg]
                        mm_acc(pt[:, 0:QG], pb, [(uTt[:, dc, :], h2T[:, dc, tg * QG:(tg + 1) * QG]) for dc in range(DC)], [uTb, b_h2T])
                        mm_banks[ec].append((tg, pt, pb))

                def M_act(ec):
                    wg, wgb = wgo[ec % 2]
                    for (tg, pt, pb) in mm_banks.pop(ec):
                        S.op("act", lambda e: e.activation(out=wg[:, tg * QG:(tg + 1) * QG], in_=pt[:, 0:QG], func=AF.Gelu),
                             reads=[pb], writes=[wgb])
                    S.dma("sp", WgT[ec * 128:(ec + 1) * 128, :], wg[:], reads=[wgb])

                slices = [(T_pe, T_act, 0)]
                for j in range(NEC):
                    if j + 1 < NEC:
                        slices.append((T_pe, T_act, j + 1))
                    slices.append((M_pe, M_act, j))
                u_load(0)
                n_steps = NTL * 32
                per_step = -(-len(slices) // n_steps)
                sl_state = {"next": 0, "pending": []}

                def slices_pe():
                    for _ in range(per_step):
                        k = sl_state["next"]
                        if k < len(slices):
                            pe_fn, act_fn, ec = slices[k]
                            pend = sl_state["pending"]
                            same = T_act if pe_fn is T_pe else M_act
                            idx = [i for i, (f_, e_) in enumerate(pend) if f_ is same or (pe_fn is M_pe and f_ is T_act and e_ == ec)]
                            if idx:
                                slices_act(idx[-1] + 1)
                            pe_fn(ec)
                            sl_state["pending"].append((act_fn, ec))
                            sl_state["next"] = k + 1

                def slices_act(upto=None):
                    pend = sl_state["pending"]
                    n = len(pend) if upto is None else min(upto, len(pend))
                    for _ in range(n):
                        act_fn, ec = pend.pop(0)
                        act_fn(ec)

                s_sb, b_s = sb(st, "s_sb", [128, HP, 128], F32)
                tmpm, b_tmpm = sb(st, "tmpm", [128, 128], F32)
                t16, b_t16 = sb(st, "t16", [128, HP, 16], F32)
                cand, b_cand = sb(st, "cand", [128, 16, 16], F32)
                cand2, b_cand2 = sb(st, "cand2", [128, 256], F32)
                c8, b_c8 = sb(st, "c8", [128, 16], F32)
                nm, b_nm = sb(st, "nm", [128, HP], F32)
                e16, b_e16 = sb(st, "e16", [128, 2, 16], F32)
                p16, b_p16 = sb(st, "p16", [128, 16, 16], F32)
                jz, b_jz = sb(st, "jz", [128, 256], F32)
                Zs, b_Zs = sb(st, "Zs", [128, PH], F32)
                lnz, b_lnz = sb(st, "lnz", [128, PH], F32)
                negc, b_negc = sb(st, "negc", [128, PH], F32)
                thr, b_thr = sb(st, "thr", [128, PH], F32)
                B1, b_B1 = sb(st, "B1", [128, PH, 128], F32)
                R1, b_R1 = sb(st, "R1", [128, PH, 128], F32)
                NPT = 12
                Pex = [sb(st, f"Pex{i}", [128, 512], BF16) for i in range(NPT)]
                Tmk = [sb(st, f"Tmk{i}", [128, 512], BF16) for i in range(NPT)]
                wtok = [sb(st, f"wtok{i}", [128, 512], BF16) for i in range(2)]
                WTs = [sb(st, f"WTs{i}", [128, 16, 128], BF16) for i in range(2)]
                PTOK = psum(reserve=True)
                PTR = psum(reserve=True)
                WdT_v = WdT.rearrange("(i1 i2) t -> i2 i1 t", i2=128)
                ipx = 0
                for tt in range(NTL):
                    ts_ = slice(tt * 128, (tt + 1) * 128)
                    for g0 in range(0, HP, 4):
                        pt, pb = (PTOK, PTR)[(g0 // 4) % 2]
                        for hp in range(g0, g0 + 4):
                            S.op("pe", lambda e: e.matmul(pt[:, (hp - g0) * 128:(hp - g0 + 1) * 128], lhsT=qT[:, hp, ts_],
                                                          rhs=kT[:, hp, :], start=True, stop=True),
                                 reads=[b_qT, b_kT], writes=[pb])
                        S.op("act", lambda e: e.copy(out=s_sb[:, g0:g0 + 4, :], in_=r3(pt[:, 0:512])), reads=[pb], writes=[b_s])
                    for hp in range(HP):
                        S.op("dve", lambda e: e.max(out=t16[:, hp, 0:8], in_=s_sb[:, hp, :]), reads=[b_s], writes=[b_t16])
                        S.op("dve", lambda e: e.match_replace(out=tmpm[:], in_to_replace=t16[:, hp, 0:8], in_values=s_sb[:, hp, :],
                                                              imm_value=NEG), reads=[b_s, b_t16], writes=[b_tmpm])
                        S.op("dve", lambda e: e.max(out=t16[:, hp, 8:16], in_=tmpm[:]), reads=[b_tmpm], writes=[b_t16])
                    S.op("dve", lambda e: e.tensor_scalar(out=nm[:], in0=t16[:, :, 0], scalar1=-1.0, scalar2=None, op0=ALU.mult),
                         reads=[b_t16], writes=[b_nm])
                    for h in range(PH):
                        for b in range(16):
                            S.op("dve", lambda e: e.tensor_scalar(out=cand[:, :, b], in0=t16[:, 2 * h, :],
                                                                  scalar1=t16[:, 2 * h + 1, b:b + 1], scalar2=None, op0=ALU.add),
                                 reads=[b_t16], writes=[b_cand])
                        cflat = cand[:].rearrange("p a b -> p (a b)")
                        S.op("dve", lambda e: e.max(out=c8[:, 0:8], in_=cflat), reads=[b_cand], writes=[b_c8])
                        S.op("dve", lambda e: e.match_replace(out=cand2[:], in_to_replace=c8[:, 0:8], in_values=cflat,
                                                              imm_value=NEG), reads=[b_cand, b_c8], writes=[b_cand2])
                        S.op("dve", lambda e: e.max(out=c8[:, 8:16], in_=cand2[:]), reads=[b_cand2], writes=[b_c8])
                        S.op("dve", lambda e: e.tensor_copy(out=thr[:, h:h + 1], in_=c8[:, 15:16]), reads=[b_c8], writes=[b_thr])
                        for p_ in range(2):
                            S.op("act", lambda e: e.activation(out=e16[:, p_, :], in_=t16[:, 2 * h + p_, :], func=AF.Exp,
                                                               bias=nm[:, 2 * h + p_:2 * h + p_ + 1]),
                                 reads=[b_t16, b_nm], writes=[b_e16])
                        for b in range(16):
                            S.op("dve", lambda e: e.tensor_scalar(out=p16[:, :, b], in0=e16[:, 0, :], scalar1=e16[:, 1, b:b + 1],
                                                                  scalar2=None, op0=ALU.mult), reads=[b_e16], writes=[b_p16])
                        S.op("dve", lambda e: e.scalar_tensor_tensor(out=jz[:], in0=cflat, scalar=thr[:, h:h + 1],
                                                                     in1=p16[:].rearrange("p a b -> p (a b)"), op0=ALU.is_ge,
                                                                     op1=ALU.mult, accum_out=Zs[:, h:h + 1]),
                             reads=[b_cand, b_thr, b_p16], writes=[b_jz, b_Zs])
                        S.op("dve", lambda e: e.tensor_scalar(out=R1[:, h, :], in0=s_sb[:, 2 * h, :], scalar1=-1.0,
                                                              scalar2=thr[:, h:h + 1], op0=ALU.mult, op1=ALU.add),
                             reads=[b_s, b_thr], writes=[b_R1])
                    S.op("act", lambda e: e.activation(out=lnz[:], in_=Zs[:], func=AF.Ln), reads=[b_Zs], writes=[b_lnz])
                    nm2 = nm[:].rearrange("p (h two) -> p h two", two=2)
                    S.op("dve", lambda e: e.tensor_tensor(out=negc[:], in0=nm2[:, :, 0], in1=nm2[:, :, 1], op=ALU.add),
                         reads=[b_nm], writes=[b_negc])
                    S.op("dve", lambda e: e.tensor_tensor(out=negc[:], in0=negc[:], in1=lnz[:], op=ALU.subtract),
                         reads=[b_negc, b_lnz], writes=[b_negc])
                    for h in range(PH):
                        S.op("dve", lambda e: e.tensor_scalar(out=B1[:, h, :], in0=s_sb[:, 2 * h, :], scalar1=negc[:, h:h + 1],
                                                              scalar2=None, op0=ALU.add), reads=[b_s, b_negc], writes=[b_B1])
                    def tr_pe(gp):
                        wk, wkb = wtok[gp % 2]
                        pv = pbf(PTR[0])
                        for ii in range(4):
                            S.op("pe", lambda e: e.transpose(out=pv[:, ii * 128:(ii + 1) * 128], in_=wk[:, ii * 128:(ii + 1) * 128],
                                                             identity=ident_b[:]), reads=[wkb, b_idb], writes=[PTR[1]])

                    def tr_act(gp):
                        wts, wtb = WTs[(gp // 4) % 2]
                        pv = pbf(PTR[0])
                        q4 = (gp % 4) * 4
                        S.op("act", lambda e: e.copy(out=wts[:, q4:q4 + 4, :], in_=r3(pv[:, 0:512])), reads=[PTR[1]], writes=[wtb])
                        if gp % 4 == 3:
                            g16 = gp // 4
                            S.dma("sp", WdT_v[:, g16 * 16:(g16 + 1) * 16, ts_], wts[:, :, :], reads=[wtb])

                    for g in range(32):
                        prev_pending = len(sl_state["pending"])
                        slices_pe()
                        if g > 0:
                            tr_pe(g - 1)
                        for h in range(PH):
                            px, pxb = Pex[ipx % NPT]; tm_, tmb = Tmk[ipx % NPT]; ipx += 1
                            for ii in range(4):
                                i1 = g * 4 + ii
                                cs = slice(ii * 128, (ii + 1) * 128)
                                S.op("act", lambda e: e.activation(out=px[:, cs], in_=s_sb[:, 2 * h + 1, :], func=AF.Exp,
                                                                   bias=B1[:, h, i1:i1 + 1]), reads=[b_s, b_B1], writes=[pxb])
                                S.op("dve", lambda e: e.scalar_tensor_tensor(out=tm_[:, cs], in0=s_sb[:, 2 * h + 1, :],
                                                                             scalar=R1[:, h, i1:i1 + 1], in1=px[:, cs],
                                                                             op0=ALU.is_ge, op1=ALU.mult),
                                     reads=[b_s, b_R1, pxb], writes=[tmb])
                            S.op("pe", lambda e: e.matmul(PTOK[0][:, 0:512], lhsT=ident_b[:], rhs=tm_[:, 0:512],
                                                          start=(h == 0), stop=(h == PH - 1)),
                                 reads=[tmb, b_idb], writes=[PTOK[1]])
                        slices_act(prev_pending)
                        if g > 0:
                            tr_act(g - 1)
                        wk, wkb = wtok[g % 2]
                        S.op("act", lambda e: e.copy(out=wk[:], in_=PTOK[0][:, 0:512]), reads=[PTOK[1]], writes=[wkb])
                    tr_pe(31)
                    tr_act(31)
                while sl_state["next"] < len(slices) or sl_state["pending"]:
                    pp = len(sl_state["pending"])
                    slices_pe()
                    slices_act(pp if pp > 0 else None)
                S.barrier()
                ps_res.clear()
        S.barrier()
        if stop == "P2a":
            return finish_dram(WgT[0:128, 0:128], 128, is_bf=True)
        with ExitStack() as ph:
            acc, b_acc = sb(ph, "acc", [128, NTL, D], F32)
            G = 8
            with ExitStack() as st:
                wgl = [sb(st, f"wgl{i}", [128, G, TL], BF16) for i in range(2)]
                wtl2, b_wtl2 = sb(st, "wtl2", [128, G, TL], BF16)
                vt = [sb(st, f"vt{i}", [128, G, DW], BF16) for i in range(2)]
                iv = 0
                NEG_ = NE // (128 * G)

                def load_wg(eg):
                    wl, wlb = wgl[eg % 2]
                    S.dma("sp", wl[:], WgT[eg * 128 * G:(eg + 1) * 128 * G, :].rearrange("(c p) t -> p c t", p=128), writes=[wlb])
                    S.dma("sp", wtl2[:], WdT[eg * 128 * G:(eg + 1) * 128 * G, :].rearrange("(c p) t -> p c t", p=128), writes=[b_wtl2])
                    S.op("dve", lambda e: e.tensor_tensor(out=wl[:], in0=wl[:], in1=wtl2[:], op=ALU.mult),
                         reads=[wlb, b_wtl2], writes=[wlb])

                load_wg(0)
                for eg in range(NEG_):
                    wl, wlb = wgl[eg % 2]
                    if eg + 1 < NEG_:
                        load_wg(eg + 1)
                    for dt in range(NDW):
                        v_, vb_ = vt[iv % 2]; iv += 1
                        cast_load(v_[:], vb_, peer_v[eg * 128 * G:(eg + 1) * 128 * G, dt * DW:(dt + 1) * DW].rearrange(
                            "(c p) n -> p c n", p=128))
                        for tt in range(NTL):
                            pt, pb = psum()
                            mm_acc(pt[:, 0:DW], pb, [(wl[:, c, tt * 128:(tt + 1) * 128], v_[:, c, :]) for c in range(G)], [wlb, vb_])
                            dst = acc[:, tt, dt * DW:(dt + 1) * DW]
                            if eg == 0:
                                S.op("act", lambda e: e.copy(out=dst, in_=pt[:, 0:DW]), reads=[pb], writes=[b_acc])
                            else:
                                S.op("dve", lambda e: e.tensor_tensor(out=dst, in0=dst, in1=pt[:, 0:DW], op=ALU.add),
                                     reads=[pb, b_acc], writes=[b_acc])
                S.barrier()
            S.barrier()
            with ExitStack() as st:
                g2b, b_g2b = bcast_rows(st, "g2b", gate2, b_modx)
                gfb, b_gfb = bcast_rows(st, "gfb", gfin, b_gfin)
                x1t = [sb(st, f"x1t{i}", [128, D], F32) for i in range(1)]
                jf, b_jf = sb(st, "jf", [128, D], BF16)
                ssf = [sb(st, f"ssf{i}", [128, 1], F32) for i in range(2)]
                rsf = [sb(st, f"rsf{i}", [128, 1], F32) for i in range(2)]
                for tt in range(NTL):
                    xt_, xb_ = x1t[0]
                    ss, bss = ssf[tt % 2]; rs, brs = rsf[tt % 2]
                    S.dma("sp", xt_[:], x1d[tt * 128:(tt + 1) * 128, :], writes=[xb_])
                    a_ = acc[:, tt, :]
                    S.op("dve", lambda e: e.tensor_tensor(out=a_, in0=a_, in1=g2b[:], op=ALU.mult), reads=[b_acc, b_g2b], writes=[b_acc])
                    S.op("dve", lambda e: e.tensor_tensor(out=a_, in0=a_, in1=xt_[:], op=ALU.add), reads=[b_acc, xb_], writes=[b_acc])
                    S.op("act", lambda e: e.activation(out=jf[:], in_=a_, func=AF.Square, accum_out=ss[:]),
                         reads=[b_acc], writes=[b_jf, bss])
                    rstd_from_ss(ss[:], bss, rs[:], brs, D)
                    S.op("dve", lambda e: e.scalar_tensor_tensor(out=a_, in0=a_, scalar=rs[:, 0:1], in1=gfb[:], op0=ALU.mult,
                                                                 op1=ALU.mult), reads=[b_acc, brs, b_gfb], writes=[b_acc])
                    S.dma("sp", out_d[tt * 128:(tt + 1) * 128, :], a_, reads=[b_acc])
                S.barrier()
        S.final_wait("sp")
        S.barrier()
    print("built: instrs", S.n_instr, "sems", S.nsem)
    return nc


def rope_apply(S, src, b_src, rt, b_rt, tp, b_tp, dst, b_dst, src_off=0, rt_idx=None):
    x = src[:, src_off:src_off + 64].rearrange("p (a h f) -> p a h f", a=2, h=2)
    r = (rt[:, rt_idx, :] if rt_idx is not None else rt[:, :]).rearrange("p (a s f) -> p a s f", a=2, s=2)
    x1 = x[:, :, 0, :]; x2 = x[:, :, 1, :]
    cs = r[:, :, 0, :]; sn = r[:, :, 1, :]
    t = tp[:].rearrange("p k (a f) -> p k a f", a=2)
    o = dst.rearrange("p (a h f) -> p a h f", a=2, h=2)
    rd = [b_src, b_rt]
    S.op("dve", lambda e: e.tensor_tensor(out=t[:, 0], in0=x1, in1=cs, op=ALU.mult), reads=rd, writes=[b_tp])
    S.op("dve", lambda e: e.tensor_tensor(out=t[:, 1], in0=x2, in1=sn, op=ALU.mult), reads=rd, writes=[b_tp])
    S.op("dve", lambda e: e.tensor_tensor(out=t[:, 2], in0=x2, in1=cs, op=ALU.mult), reads=rd, writes=[b_tp])
    S.op("dve", lambda e: e.tensor_tensor(out=t[:, 3], in0=x1, in1=sn, op=ALU.mult), reads=rd, writes=[b_tp])
    S.op("dve", lambda e: e.tensor_tensor(out=o[:, :, 0, :], in0=t[:, 0], in1=t[:, 1], op=ALU.subtract), reads=[b_tp], writes=[b_dst])
    S.op("dve", lambda e: e.tensor_tensor(out=o[:, :, 1, :], in0=t[:, 2], in1=t[:, 3], op=ALU.add), reads=[b_tp], writes=[b_dst])


def rope_table(T, grid_w=64, theta=10000.0):
    rows = T // grid_w
    row = np.repeat(np.arange(rows), grid_w).astype(np.float32)
    col = np.tile(np.arange(grid_w), rows).astype(np.float32)
    half = 32
    freqs = (np.float32(theta) ** (-np.arange(0, half, 2, dtype=np.float32) / np.float32(half))).astype(np.float32)
    ar = row[:, None] * freqs
    ac = col[:, None] * freqs
    return np.concatenate([np.cos(ar), np.sin(ar), np.cos(ac), np.sin(ac)], axis=1).astype(np.float32)


_NC_CACHE = {}


def run(cfg, inputs, stop=None):
    key = tuple(sorted(cfg.items())) + (stop,)
    if key not in _NC_CACHE:
        _NC_CACHE[key] = build(cfg, stop)
    nc = _NC_CACHE[key]
    D = cfg["D"]; T = cfg["T"]; TL = T // NCORES
    f = lambda a: np.ascontiguousarray(np.asarray(a, dtype=np.float32))
    x = f(inputs["x"])[0]
    rope = rope_table(T)
    shared = {
        "x_full": x, "ctx": f(inputs["ctx"])[0], "c": f(inputs["c"])[0], "c_ctx": f(inputs["c_ctx"]),
        "w_ada": f(inputs["w_ada"])[0], "b_ada": f(inputs["b_ada"])[0], "norm1_g": f(inputs["norm1_g"])[0],
        "w_in": f(inputs["w_in"])[0], "q_norm_g": f(inputs["q_norm_g"])[0], "w_uq": f(inputs["w_uq"])[0],
        "kv_norm_g": f(inputs["kv_norm_g"])[0], "w_ukv": f(inputs["w_ukv"])[0], "conv_w": f(inputs["conv_w"])[0],
        "conv_b": f(inputs["conv_b"])[0], "w_o": f(inputs["w_o"])[0], "norm2_g": f(inputs["norm2_g"])[0],
        "peer_wq": f(inputs["peer_wq"])[0], "peer_keys": f(inputs["peer_keys"])[0], "peer_u": f(inputs["peer_u"])[0],
        "peer_v": f(inputs["peer_v"])[0], "final_norm_g": f(inputs["final_norm_g"]), "rope_all": rope,
        "ident": np.eye(128, dtype=np.float32),
    }
    in_maps = []
    for c in range(NCORES):
        m = dict(shared)
        m["x_own"] = np.ascontiguousarray(x[c * TL:(c + 1) * TL])
        halo = np.zeros((2, D), np.float32)
        hm = np.zeros((128, 2), np.float32)
        if c > 0:
            halo[0] = x[c * TL - 1]; hm[:, 0] = 1.0
        if c < NCORES - 1:
            halo[1] = x[(c + 1) * TL]; hm[:, 1] = 1.0
        m["x_halo"] = halo
        m["hmask"] = hm
        m["rope_own"] = np.ascontiguousarray(rope[c * TL:(c + 1) * TL])
        in_maps.append(m)
    res = run_bass_kernel_spmd(nc, in_maps, core_ids=list(range(NCORES)))
    out = np.concatenate([np.asarray(res.results[c]["out"], dtype=np.float32) for c in range(NCORES)], axis=0)
    return out[None]


def kernel(**inputs):
    return run(CFG_FULL, inputs)
```

```python
import math
from contextlib import ExitStack

import numpy as np
import concourse.bass as bass
import concourse.mybir as mybir
from concourse.bass_utils import run_bass_kernel_spmd

F32 = mybir.dt.float32
BF16 = mybir.dt.bfloat16
AF = mybir.ActivationFunctionType
ALU = mybir.AluOpType
AX = mybir.AxisListType

NCORES = 8
EPS = 1e-6
NEG = -1.0e30

CFG_FULL = dict(D=4096, T=8192, TC=256, H=16, QL=768, KVL=512, CC=2048, PH=8)


class Buf:
    __slots__ = ("w", "r", "excl")

    def __init__(self, excl=False):
        self.w = None
        self.r = []
        self.excl = excl


class Sched:
    NS = 8
    EPOCH = 60000

    def __init__(self, nc, stack):
        self.nc = nc
        self.stack = stack
        self.eng = {"pe": nc.tensor, "act": nc.scalar, "dve": nc.vector, "pool": nc.gpsimd, "sp": nc.sync}
        self.sem = {}
        self.cnt = {}
        self.nsem = 0
        self.seen = {k: {} for k in self.eng}
        self.all_sems = {}
        for k in self.eng:
            self._new_eng_sem(k)
        self.dsem = {}
        self.dval = {}
        self.dslot = {}
        for q in ("sp", "pool", "act"):
            self.dsem[q] = [self._mksem(f"d_{q}{i}") for i in range(self.NS)]
            self.dval[q] = [0] * self.NS
            self.dslot[q] = 0
        self.n_instr = 0

    def _mksem(self, name):
        s = self.stack.enter_context(self.nc.semaphore(f"{name}_{self.nsem}"))
        self.nsem += 1
        self.all_sems[id(s)] = s
        return s

    def _new_eng_sem(self, k):
        self.sem[k] = self._mksem(f"e_{k}")
        self.cnt[k] = 0

    def _wait(self, e, ev):
        sem, val, src = ev
        if e == "pe" and src == "pe":
            return
        seen = self.seen[e]
        key = id(sem)
        if seen.get(key, 0) >= val:
            return
        self.eng[e].wait_ge(sem, val)
        seen[key] = val

    def _deps(self, e, reads, writes):
        for b in reads:
            if b.w is not None:
                self._wait(e, b.w)
            if b.excl:
                for ev in b.r:
                    self._wait(e, ev)
        for b in writes:
            if b.w is not None:
                self._wait(e, b.w)
            for ev in b.r:
                self._wait(e, ev)

    def _record(self, ev, reads, writes):
        for b in reads:
            if b.excl:
                b.w = ev
                b.r = []
            else:
                b.r.append(ev)
                if len(b.r) > 24:
                    b.r = b.r[-24:]
        for b in writes:
            b.w = ev
            b.r = []

    def op(self, e, fn, reads=(), writes=()):
        if self.cnt[e] >= self.EPOCH:
            self._new_eng_sem(e)
        self._deps(e, reads, writes)
        ins = fn(self.eng[e])
        self.cnt[e] += 1
        ins.then_inc(self.sem[e], 1)
        self.n_instr += 1
        ev = (self.sem[e], self.cnt[e], e)
        self._record(ev, reads, writes)
        return ev

    def dma(self, q, out, in_, reads=(), writes=(), **kw):
        self._deps(q, reads, writes)
        slot = self.dslot[q]
        self.dslot[q] = (slot + 1) % self.NS
        sem = self.dsem[q][slot]
        prev = self.dval[q][slot]
        if prev > 0:
            self._wait(q, (sem, prev, "dma"))
        ins = self.eng[q].dma_start(out=out, in_=in_, **kw)
        ins.then_inc(sem, 16)
        self.dval[q][slot] = prev + 16
        self.n_instr += 1
        ev = (sem, prev + 16, "dma")
        self._record(ev, reads, writes)
        return ev

    def barrier(self):
        evs = []
        for k in self.eng:
            if self.cnt[k] > 0:
                evs.append((self.sem[k], self.cnt[k], k))
        for q in self.dsem:
            for i in range(self.NS):
                if self.dval[q][i] > 0:
                    evs.append((self.dsem[q][i], self.dval[q][i], "dma"))
        for e in self.eng:
            for ev in evs:
                if ev[2] == e and e != "pe":
                    continue
                sem, val, src = ev
                seen = self.seen[e]
                if seen.get(id(sem), 0) >= val:
                    continue
                self.eng[e].wait_ge(sem, val)
                seen[id(sem)] = val

    def final_wait(self, e="sp"):
        for q in self.dsem:
            for i in range(self.NS):
                if self.dval[q][i] > 0:
                    self._wait(e, (self.dsem[q][i], self.dval[q][i], "dma"))


def build(cfg, stop=None):
    D = cfg["D"]; T = cfg["T"]; TC = cfg["TC"]; H = cfg["H"]; QL = cfg["QL"]; KVL = cfg["KVL"]
    CC = cfg["CC"]; PH = cfg["PH"]
    TL = T // NCORES
    DC = D // 128
    NT = T // 128
    NTC = TC // 128
    NTL = TL // 128
    KT = T + TC
    NKC = KT // 128
    CCH = CC // 128
    MIX = H * 128 + CC
    MC = MIX // 128
    INC = QL + KVL + 64 + 3 * CC
    O_CKV = QL
    O_CONV = QL + KVL + 64
    QC = QL // 128
    KC4 = KVL // 128
    NE = 16384
    HP = PH * 2
    QG = min(512, TL)
    NQG = TL // QG
    DW = min(512, D)
    NDW = D // DW
    SCALE = 1.0 / math.sqrt(192.0)
    assert H % 2 == 0 and KVL <= 512 and 6 * DC * 2 <= 512

    nc = bass.Bass("TRN2", target_bir_lowering=False)

    def din(name, shape, dt=F32):
        return nc.dram_tensor(name, list(shape), dt, kind="ExternalInput").ap()

    x_full = din("x_full", [T, D]); x_own = din("x_own", [TL, D]); x_halo = din("x_halo", [2, D])
    hmask = din("hmask", [128, 2]); ctx = din("ctx", [TC, D])
    c_in = din("c", [D]); cc_in = din("c_ctx", [D])
    w_ada = din("w_ada", [D, 6 * D]); b_ada = din("b_ada", [6 * D])
    norm1_g = din("norm1_g", [D]); w_in = din("w_in", [D, INC]); q_norm_g = din("q_norm_g", [QL])
    w_uq = din("w_uq", [QL, H * 192]); kv_norm_g = din("kv_norm_g", [KVL]); w_ukv = din("w_ukv", [KVL, H * 256])
    conv_w = din("conv_w", [3, CC]); conv_b = din("conv_b", [CC]); w_o = din("w_o", [MIX, D])
    norm2_g = din("norm2_g", [D]); peer_wq = din("peer_wq", [D, HP * 128])
    peer_keys = din("peer_keys", [PH, 2, 128, 128]); peer_u = din("peer_u", [NE, D]); peer_v = din("peer_v", [NE, D])
    final_g = din("final_norm_g", [D]); rope_all = din("rope_all", [T, 64]); rope_own = din("rope_own", [TL, 64])
    ident_in = din("ident", [128, 128])
    out_d = nc.dram_tensor("out", [TL, D], F32, kind="ExternalOutput").ap()
    x1d = nc.dram_tensor("x1d", [TL, D], F32).ap()
    convd = nc.dram_tensor("convd", [CC, TL], BF16).ap()
    attd = nc.dram_tensor("attd", [H * 128, TL], BF16).ap()
    WdT = nc.dram_tensor("WdT", [NE, TL], BF16).ap()
    WgT = nc.dram_tensor("WgT", [NE, TL], BF16).ap()

    with ExitStack() as top:
        S = Sched(nc, top)
        top.enter_context(nc.allow_non_contiguous_dma(reason="small per-feature vectors"))
        top.enter_context(nc.allow_low_precision(reason="bf16 matmul operands"))

        def sb(stack, name, shape, dt):
            t = stack.enter_context(nc.sbuf_tensor(name, list(shape), dt))
            return t, Buf()

        PSF = []; PSB = []
        for i in range(8):
            t = top.enter_context(nc.psum_tensor(f"ps{i}", [128, 512], F32))
            PSF.append(t); PSB.append(Buf(excl=True))
        ps_rr = [0]
        ps_res = set()

        def psum(reserve=False):
            assert len(ps_res) < 8, "all PSUM banks reserved"
            while True:
                i = ps_rr[0]; ps_rr[0] = (i + 1) % 8
                if i not in ps_res:
                    break
            if reserve:
                ps_res.add(i)
            return PSF[i], PSB[i]

        dbg_stack = ExitStack()
        mid = ExitStack()
        kvs = ExitStack()

        def finish(tile_ap=None, buf=None, n=None, is_bf=False, close=()):
            if tile_ap is not None:
                if is_bf:
                    tmp, tb = sb(dbg_stack, "dbgtmp", [128, n], F32)
                    S.op("dve", lambda e: e.tensor_copy(out=tmp[:], in_=tile_ap), reads=[buf], writes=[tb])
                    S.dma("sp", out_d[0:128, 0:n], tmp[:], reads=[tb])
                else:
                    S.dma("sp", out_d[0:128, 0:n], tile_ap, reads=[buf])
            S.final_wait("sp")
            S.barrier()
            print("built (stop=%s): instrs %d sems %d" % (stop, S.n_instr, S.nsem))
            dbg_stack.close()
            for st_ in close:
                st_.close()
            kvs.close(); mid.close()
            return nc

        def finish_dram(src_ap, n, is_bf=False, close=()):
            t_, tb_ = sb(dbg_stack, "dbgd", [128, n], BF16 if is_bf else F32)
            S.dma("sp", t_[:], src_ap, writes=[tb_])
            return finish(t_[:], tb_, n, is_bf=is_bf, close=close)

        def r3(ap, b=128):
            return ap.rearrange("p (a b) -> p a b", b=b)

        def pbf(pt):
            return pt[:].bitcast(BF16)

        ident_f, b_idf = sb(top, "ident_f", [128, 128], F32)
        ident_b, b_idb = sb(top, "ident_b", [128, 128], BF16)
        ones_f, b_ones = sb(top, "ones_f", [128, 128], F32)
        modx, b_modx = sb(top, "modx", [128, 6 * DC], F32)
        modc, b_modc = sb(top, "modc", [128, 2 * DC], F32)
        gs1, b_gs1 = sb(top, "gs1", [128, DC], F32)
        gs1c, b_gs1c = sb(top, "gs1c", [128, DC], F32)
        gs2, b_gs2 = sb(top, "gs2", [128, DC], F32)
        gq, b_gq = sb(top, "gq", [128, QC], F32)
        gkv, b_gkv = sb(top, "gkv", [128, KC4], F32)
        gfin, b_gfin = sb(top, "gfin", [128, DC], F32)
        epsb, b_eps = sb(top, "epsb", [128, 1], F32)

        S.dma("sp", ident_f[:], ident_in[:, :], writes=[b_idf])
        S.op("dve", lambda e: e.tensor_copy(out=ident_b[:], in_=ident_f[:]), reads=[b_idf], writes=[b_idb])
        S.op("dve", lambda e: e.memset(ones_f[:], 1.0), writes=[b_ones])
        S.op("dve", lambda e: e.memset(epsb[:], EPS), writes=[b_eps])

        def vec_pc(dst, buf, src, n):
            S.dma("sp", dst, src.rearrange("(c p) -> p c", p=128), writes=[buf])

        vec_pc(gq[:], b_gq, q_norm_g, QC)
        vec_pc(gkv[:], b_gkv, kv_norm_g, KC4)
        vec_pc(gfin[:], b_gfin, final_g, DC)

        def rstd_from_ss(ss, b_ss, rs, b_rs, n):
            S.op("act", lambda e: e.activation(out=rs, in_=ss, func=AF.Sqrt, bias=epsb[:ss.shape[0], :], scale=1.0 / n),
                 reads=[b_ss, b_eps], writes=[b_rs])
            S.op("dve", lambda e: e.reciprocal(out=rs, in_=rs), reads=[b_rs], writes=[b_rs])

        def bcast_rows(stack, name, vec, b_vec):
            dst, b_dst = sb(stack, name, [128, D], F32)
            with ExitStack() as st:
                dg, b_dg = sb(st, name + "_dg", [128, 2, 128], F32)
                for dc in range(DC):
                    k = dc % 2
                    S.op("dve", lambda e: e.tensor_scalar(out=dg[:, k, :], in0=ident_f[:], scalar1=vec[:, dc:dc + 1],
                                                          scalar2=None, op0=ALU.mult),
                         reads=[b_idf, b_vec], writes=[b_dg])
                    pt, pb = psum()
                    S.op("pe", lambda e: e.matmul(pt[:, 0:128], lhsT=ones_f[:], rhs=dg[:, k, :], start=True, stop=True),
                         reads=[b_ones, b_dg], writes=[pb])
                    S.op("act", lambda e: e.copy(out=dst[:, dc * 128:(dc + 1) * 128], in_=pt[:, 0:128]),
                         reads=[pb], writes=[b_dst])
                S.barrier()
            return dst, b_dst

        with ExitStack() as ph:
            cin, b_cin = sb(ph, "cin", [128, 2, DC], F32)
            scv, b_scv = sb(ph, "scv", [128, DC, 2], F32)
            bt, b_bt = sb(ph, "bt", [128, 6 * DC], F32)
            g1t, b_g1t = sb(ph, "g1t", [128, DC], F32)
            g2t, b_g2t = sb(ph, "g2t", [128, DC], F32)
            BW = 256
            NBLK = 6 * D // BW
            wad = [sb(ph, f"wad{i}", [128, DC, BW], F32) for i in range(3)]
            wab = [sb(ph, f"wab{i}", [128, DC, BW], BF16) for i in range(2)]
            scb, b_scb = sb(ph, "scb", [128, DC, 2], BF16)
            S.dma("sp", cin[:, 0, :], c_in.rearrange("(c p) -> p c", p=128), writes=[b_cin])
            S.dma("sp", cin[:, 1, :], cc_in.rearrange("(c p) -> p c", p=128), writes=[b_cin])
            for j in range(6):
                S.dma("sp", bt[:, j * DC:(j + 1) * DC], b_ada[j * D:(j + 1) * D].rearrange("(c p) -> p c", p=128),
                      writes=[b_bt])
            vec_pc(g1t[:], b_g1t, norm1_g, DC)
            vec_pc(g2t[:], b_g2t, norm2_g, DC)
            for i in range(2):
                S.op("act", lambda e: e.activation(out=scv[:, :, i], in_=cin[:, i, :], func=AF.Silu),
                     reads=[b_cin], writes=[b_scv])
            S.op("dve", lambda e: e.tensor_copy(out=scb[:], in_=scv[:]), reads=[b_scv], writes=[b_scb])
            mps, mpb = psum()
            HDC = max(1, DC // 2)
            for blk in range(NBLK):
                wt, wb = wad[blk % 3]
                wtb, wbb = wab[blk % 2]
                S.dma("sp", wt[:], w_ada[:, blk * BW:(blk + 1) * BW].rearrange("(kc p) n -> p kc n", p=128), writes=[wb])
                if DC > 1:
                    S.op("dve", lambda e: e.tensor_copy(out=wtb[:, 0:HDC, :], in_=wt[:, 0:HDC, :]), reads=[wb], writes=[wbb])
                    S.op("act", lambda e: e.copy(out=wtb[:, HDC:DC, :], in_=wt[:, HDC:DC, :]), reads=[wb], writes=[wbb])
                else:
                    S.op("dve", lambda e: e.tensor_copy(out=wtb[:], in_=wt[:]), reads=[wb], writes=[wbb])
                for j in range(BW // 128):
                    col = blk * (BW // 128) + j
                    for kc in range(DC):
                        S.op("pe", lambda e: e.matmul(mps[:, col * 2:col * 2 + 2], lhsT=wtb[:, kc, j * 128:(j + 1) * 128],
                                                      rhs=scb[:, kc, :], start=(kc == 0), stop=(kc == DC - 1)),
                             reads=[wbb, b_scb], writes=[mpb])
            mview = mps[:, 0:12 * DC].rearrange("p (c two) -> p c two", two=2)
            S.op("dve", lambda e: e.tensor_tensor(out=modx[:], in0=mview[:, :, 0], in1=bt[:], op=ALU.add),
                 reads=[mpb, b_bt], writes=[b_modx])
            S.op("dve", lambda e: e.tensor_tensor(out=modc[:], in0=mview[:, 0:2 * DC, 1], in1=bt[:, 0:2 * DC], op=ALU.add),
                 reads=[mpb, b_bt], writes=[b_modc])
            S.op("dve", lambda e: e.scalar_tensor_tensor(out=gs1[:], in0=modx[:, DC:2 * DC], scalar=1.0, in1=g1t[:],
                                                         op0=ALU.add, op1=ALU.mult), reads=[b_modx, b_g1t], writes=[b_gs1])
            S.op("dve", lambda e: e.scalar_tensor_tensor(out=gs1c[:], in0=modc[:, DC:2 * DC], scalar=1.0, in1=g1t[:],
                                                         op0=ALU.add, op1=ALU.mult), reads=[b_modc, b_g1t], writes=[b_gs1c])
            S.op("dve", lambda e: e.scalar_tensor_tensor(out=gs2[:], in0=modx[:, 4 * DC:5 * DC], scalar=1.0, in1=g2t[:],
                                                         op0=ALU.add, op1=ALU.mult), reads=[b_modx, b_g2t], writes=[b_gs2])
            S.barrier()
        sh1 = modx[:, 0:DC]; sh1c = modc[:, 0:DC]; sh2 = modx[:, 3 * DC:4 * DC]
        gate1 = modx[:, 2 * DC:3 * DC]; gate2 = modx[:, 5 * DC:6 * DC]
        if stop == "A":
            return finish(modx[:, :], b_modx, 6 * DC)

        class NormT:
            def __init__(self, stack, tag):
                self.xt = [sb(stack, f"{tag}_x{i}", [128, D], F32) for i in range(2)]
                self.xs = [sb(stack, f"{tag}_xs0", [128, D], BF16)] * 2
                self.junk = self.xs[0]
                self.ss = [sb(stack, f"{tag}_ss{i}", [128, 1], F32) for i in range(2)]
                self.rs = [sb(stack, f"{tag}_rs{i}", [128, 1], F32) for i in range(2)]
                self.k = 0

            def run(self, src, n, gsv, b_gsv, shv, b_shv, dst, b_dst):
                k = self.k; self.k ^= 1
                xt, bx = self.xt[k]; xs, bxs = self.xs[k]; jk, bj = self.junk
                ss, bss = self.ss[k]; rs, brs = self.rs[k]
                S.dma("sp", xt[:n, :], src, writes=[bx])
                S.op("act", lambda e: e.activation(out=jk[:n, :], in_=xt[:n, :], func=AF.Square, accum_out=ss[:n, :]),
                     reads=[bx], writes=[bj, bss])
                rstd_from_ss(ss[:n, :], bss, rs[:n, :], brs, D)
                S.op("dve", lambda e: e.tensor_scalar(out=xs[:n, :], in0=xt[:n, :], scalar1=rs[:n, 0:1], scalar2=None,
                                                      op0=ALU.mult), reads=[bx, brs], writes=[bxs])
                for g0 in range(0, DC, 8):
                    g1 = min(DC, g0 + 8)
                    pt, pb = psum()
                    pv = pbf(pt)
                    for dc in range(g0, g1):
                        S.op("pe", lambda e: e.transpose(out=pv[:, (dc - g0) * 128:(dc - g0) * 128 + n],
                                                         in_=xs[:n, dc * 128:(dc + 1) * 128], identity=ident_b[:n, :n]),
                             reads=[bxs, b_idb], writes=[pb])
                    for dc in range(g0, g1):
                        src_ps = pv[:, (dc - g0) * 128:(dc - g0) * 128 + n]
                        if dc % 2 == 0:
                            S.op("act", lambda e: e.activation(out=dst[:, dc, :], in_=src_ps, func=AF.Identity,
                                                               bias=shv[:, dc:dc + 1], scale=gsv[:, dc:dc + 1]),
                                 reads=[pb, b_gsv, b_shv], writes=[b_dst])
                        else:
                            S.op("dve", lambda e: e.tensor_scalar(out=dst[:, dc, :], in0=src_ps, scalar1=gsv[:, dc:dc + 1],
                                                                  scalar2=shv[:, dc:dc + 1], op0=ALU.mult, op1=ALU.add),
                                 reads=[pb, b_gsv, b_shv], writes=[b_dst])

        def cast_load(dst, b_dst, src):
            S.dma("pool", dst, src, writes=[b_dst])

        def mm_acc(pt_ap, pb, pairs, reads):
            n = len(pairs)
            for i, (l, r) in enumerate(pairs):
                S.op("pe", lambda e: e.matmul(pt_ap, lhsT=l, rhs=r, start=(i == 0), stop=(i == n - 1)),
                     reads=reads, writes=[pb])

        cqT, b_cqT = sb(kvs, "cqT", [128, QC, TL], BF16)

        with ExitStack() as ph:
            hxo, b_hxo = sb(ph, "hxo", [128, DC, TL], BF16)
            hxh, b_hxh = sb(ph, "hxh", [128, DC, 2], BF16)
            with ExitStack() as st:
                nt_ = NormT(st, "nC")
                for tt in range(NTL):
                    nt_.run(x_own[tt * 128:(tt + 1) * 128, :], 128, gs1, b_gs1, sh1, b_modx,
                            hxo[:, :, tt * 128:(tt + 1) * 128], b_hxo)
                nt_.run(x_halo[:, :], 2, gs1, b_gs1, sh1, b_modx, hxh[:, :, :], b_hxh)
                S.barrier()
            with ExitStack() as st:
                wqa, b_wqa = sb(st, "wqa", [128, DC, QL], BF16)
                for dc in range(DC):
                    cast_load(wqa[:, dc, :], b_wqa, w_in[dc * 128:(dc + 1) * 128, 0:QL])
                cqn, b_cqn = sb(st, "cqn", [128, QL], BF16)
                jq, b_jq = sb(st, "jq", [128, 512], BF16)
                ssq, b_ssq = sb(st, "ssq", [128, 4], F32)
                rsq, b_rsq = sb(st, "rsq", [128, 1], F32)
                pieces = [(o, min(512, QL - o)) for o in range(0, QL, 512)]
                for tt in range(NTL):
                    pts = []
                    for pi, (o, w) in enumerate(pieces):
                        pt, pb = psum()
                        mm_acc(pt[:, 0:w], pb, [(hxo[:, dc, tt * 128:(tt + 1) * 128], wqa[:, dc, o:o + w]) for dc in range(DC)],
                               [b_hxo, b_wqa])
                        S.op("act", lambda e: e.activation(out=jq[:, 0:w], in_=pt[:, 0:w], func=AF.Square,
                                                           accum_out=ssq[:, pi:pi + 1]), reads=[pb], writes=[b_jq, b_ssq])
                        pts.append((pt, pb))
                    if len(pieces) > 1:
                        S.op("dve", lambda e: e.tensor_reduce(out=ssq[:, 3:4], in_=ssq[:, 0:len(pieces)], axis=AX.X, op=ALU.add),
                             reads=[b_ssq], writes=[b_ssq])
                        sst = ssq[:, 3:4]
                    else:
                        sst = ssq[:, 0:1]
                    rstd_from_ss(sst, b_ssq, rsq[:], b_rsq, QL)
                    for (o, w), (pt, pb) in zip(pieces, pts):
                        S.op("dve", lambda e: e.tensor_scalar(out=cqn[:, o:o + w], in0=pt[:, 0:w], scalar1=rsq[:, 0:1],
                                                              scalar2=None, op0=ALU.mult), reads=[pb, b_rsq], writes=[b_cqn])
                    pt, pb = psum()
                    pv = pbf(pt)
                    for cc in range(QC):
                        S.op("pe", lambda e: e.transpose(out=pv[:, cc * 128:(cc + 1) * 128], in_=cqn[:, cc * 128:(cc + 1) * 128],
                                                         identity=ident_b[:]), reads=[b_cqn, b_idb], writes=[pb])
                    for cc in range(QC):
                        S.op("act", lambda e: e.activation(out=cqT[:, cc, tt * 128:(tt + 1) * 128], in_=pv[:, cc * 128:(cc + 1) * 128],
                                                           func=AF.Identity, scale=gq[:, cc:cc + 1]),
                             reads=[pb, b_gq], writes=[b_cqT])
                S.barrier()
            with ExitStack() as st:
                cw, b_cw = sb(st, "cw", [128, 3, CCH], F32)
                cb, b_cb = sb(st, "cb", [128, CCH], F32)
                hm, b_hm = sb(st, "hm", [128, 2], F32)
                for r in range(3):
                    S.dma("sp", cw[:, r, :], conv_w[r, :].rearrange("(j p) -> p j", p=128), writes=[b_cw])
                S.dma("sp", cb[:], conv_b.rearrange("(j p) -> p j", p=128), writes=[b_cb])
                S.dma("sp", hm[:], hmask[:, :], writes=[b_hm])
                wc = [[sb(st, f"wc{k}_{r}", [128, DC, 128], BF16) for r in range(3)] for k in range(2)]
                u_t, b_u = sb(st, "u_t", [128, TL + 2], F32)
                xin_s, b_xin = sb(st, "xin_s", [128, TL + 2], F32)
                gb_s, b_gbs = sb(st, "gb_s", [128, TL], F32)
                y_t, b_y = sb(st, "y_t", [128, TL], F32)
                cv = [sb(st, f"cv{k}", [128, TL], BF16) for k in range(2)]
                for j in range(CCH):
                    ws = wc[j % 2]
                    for r in range(3):
                        c0 = O_CONV + r * CC + j * 128
                        cast_load(ws[r][0][:], ws[r][1], w_in[:, c0:c0 + 128].rearrange("(dc p) n -> p dc n", p=128))
                    for tg in range(NQG):
                        pt, pb = psum()
                        mm_acc(pt[:, 0:QG], pb, [(ws[0][0][:, dc, :], hxo[:, dc, tg * QG:(tg + 1) * QG]) for dc in range(DC)],
                               [ws[0][1], b_hxo])
                        S.op("act", lambda e: e.copy(out=xin_s[:, 1 + tg * QG:1 + (tg + 1) * QG], in_=pt[:, 0:QG]),
                             reads=[pb], writes=[b_xin])
                    pt, pb = psum()
                    mm_acc(pt[:, 0:2], pb, [(ws[0][0][:, dc, :], hxh[:, dc, :]) for dc in range(DC)], [ws[0][1], b_hxh])
                    S.op("act", lambda e: e.copy(out=xin_s[:, 0:1], in_=pt[:, 0:1]), reads=[pb], writes=[b_xin])
                    S.op("act", lambda e: e.copy(out=xin_s[:, TL + 1:TL + 2], in_=pt[:, 1:2]), reads=[pb], writes=[b_xin])
                    for tg in range(NQG):
                        pt, pb = psum()
                        mm_acc(pt[:, 0:QG], pb, [(ws[1][0][:, dc, :], hxo[:, dc, tg * QG:(tg + 1) * QG]) for dc in range(DC)],
                               [ws[1][1], b_hxo])
                        S.op("act", lambda e: e.copy(out=gb_s[:, tg * QG:(tg + 1) * QG], in_=pt[:, 0:QG]),
                             reads=[pb], writes=[b_gbs])
                    for tg in range(NQG):
                        pt, pb = psum()
                        mm_acc(pt[:, 0:QG], pb, [(ws[2][0][:, dc, :], hxo[:, dc, tg * QG:(tg + 1) * QG]) for dc in range(DC)],
                               [ws[2][1], b_hxo])
                        S.op("dve", lambda e: e.tensor_tensor(out=u_t[:, 1 + tg * QG:1 + (tg + 1) * QG], in0=pt[:, 0:QG],
                                                              in1=xin_s[:, 1 + tg * QG:1 + (tg + 1) * QG], op=ALU.mult),
                             reads=[pb, b_xin], writes=[b_u])
                    pt, pb = psum()
                    mm_acc(pt[:, 0:2], pb, [(ws[2][0][:, dc, :], hxh[:, dc, :]) for dc in range(DC)], [ws[2][1], b_hxh])
                    for (hc, uc) in ((0, 0), (1, TL + 1)):
                        S.op("dve", lambda e: e.scalar_tensor_tensor(out=u_t[:, uc:uc + 1], in0=pt[:, hc:hc + 1],
                                                                     scalar=hm[:, hc:hc + 1], in1=xin_s[:, uc:uc + 1],
                                                                     op0=ALU.mult, op1=ALU.mult),
                             reads=[pb, b_hm, b_xin], writes=[b_u])
                    S.op("dve", lambda e: e.tensor_scalar(out=y_t[:], in0=u_t[:, 1:TL + 1], scalar1=cw[:, 1, j:j + 1],
                                                          scalar2=cb[:, j:j + 1], op0=ALU.mult, op1=ALU.add),
                         reads=[b_u, b_cw, b_cb], writes=[b_y])
                    S.op("dve", lambda e: e.scalar_tensor_tensor(out=y_t[:], in0=u_t[:, 0:TL], scalar=cw[:, 0, j:j + 1],
                                                                 in1=y_t[:], op0=ALU.mult, op1=ALU.add),
                         reads=[b_u, b_cw, b_y], writes=[b_y])
                    S.op("dve", lambda e: e.scalar_tensor_tensor(out=y_t[:], in0=u_t[:, 2:TL + 2], scalar=cw[:, 2, j:j + 1],
                                                                 in1=y_t[:], op0=ALU.mult, op1=ALU.add),
                         reads=[b_u, b_cw, b_y], writes=[b_y])
                    cvt, cvb = cv[j % 2]
                    S.op("dve", lambda e: e.tensor_tensor(out=cvt[:], in0=y_t[:], in1=gb_s[:], op=ALU.mult),
                         reads=[b_y, b_gbs], writes=[cvb])
                    S.dma("sp", convd[j * 128:(j + 1) * 128, :], cvt[:], reads=[cvb])
                S.barrier()
            S.barrier()

        if stop == "C":
            return finish(cqT[:, 0, 0:128], b_cqT, 128, is_bf=True)
        ckvT, b_ckvT = sb(kvs, "ckvT", [128, KC4, KT], BF16)
        krT, b_krT = sb(kvs, "krT", [128, KT], BF16)
        with ExitStack() as ph:
            nt_ = NormT(ph, "nB")
            wkv, b_wkv = sb(ph, "wkv", [128, DC, KVL + 64], BF16)
            for dc in range(DC):
                cast_load(wkv[:, dc, :], b_wkv, w_in[dc * 128:(dc + 1) * 128, O_CKV:O_CKV + KVL + 64])
            hxt = [sb(ph, f"hxt{i}", [128, DC, 128], BF16) for i in range(2)]
            jk2, b_jk2 = sb(ph, "jk2", [128, 512], BF16)
            ss2 = [sb(ph, f"ss2_{i}", [128, 1], F32) for i in range(2)]
            rs2 = [sb(ph, f"rs2_{i}", [128, 1], F32) for i in range(2)]
            ckn = [sb(ph, f"ckn{i}", [128, KVL], BF16) for i in range(2)]
            krs = [sb(ph, f"krs{i}", [128, 64], F32) for i in range(2)]
            rtb = [sb(ph, f"rtb{i}", [128, 64], F32) for i in range(2)]
            tmpr = [sb(ph, f"tmpr{i}", [128, 4, 32], F32) for i in range(2)]
            krd = [sb(ph, f"krd{i}", [128, 128], BF16) for i in range(2)]
            def front(i):
                k = i % 2
                is_x = i < NT
                src = x_full[i * 128:(i + 1) * 128, :] if is_x else ctx[(i - NT) * 128:(i - NT + 1) * 128, :]
                hx, bhx = hxt[k]
                if is_x:
                    nt_.run(src, 128, gs1, b_gs1, sh1, b_modx, hx[:, :, :], bhx)
                else:
                    nt_.run(src, 128, gs1c, b_gs1c, sh1c, b_modc, hx[:, :, :], bhx)

            front(0)
            for i in range(NT + NTC):
                k = i % 2
                is_x = i < NT
                hx, bhx = hxt[k]
                if i + 1 < NT + NTC:
                    front(i + 1)
                pa, pab = psum()
                mm_acc(pa[:, 0:KVL], pab, [(hx[:, dc, :], wkv[:, dc, 0:KVL]) for dc in range(DC)], [bhx, b_wkv])
                pk, pkb = psum()
                mm_acc(pk[:, 0:64], pkb, [(hx[:, dc, :], wkv[:, dc, KVL:KVL + 64]) for dc in range(DC)], [bhx, b_wkv])
                ss, bss = ss2[k]; rs, brs = rs2[k]; cn, bcn = ckn[k]
                S.op("act", lambda e: e.activation(out=jk2[:, 0:KVL], in_=pa[:, 0:KVL], func=AF.Square, accum_out=ss[:]),
                     reads=[pab], writes=[b_jk2, bss])
                rstd_from_ss(ss[:], bss, rs[:], brs, KVL)
                S.op("dve", lambda e: e.tensor_scalar(out=cn[:], in0=pa[:, 0:KVL], scalar1=rs[:, 0:1], scalar2=None,
                                                      op0=ALU.mult), reads=[pab, brs], writes=[bcn])
                kd, bkd = krd[k]
                if is_x:
                    kr, bkr = krs[k]; rt, brt = rtb[k]; tp, btp = tmpr[k]
                    S.dma("sp", rt[:], rope_all[i * 128:(i + 1) * 128, :], writes=[brt])
                    S.op("act", lambda e: e.copy(out=kr[:], in_=pk[:, 0:64]), reads=[pkb], writes=[bkr])
                    rope_apply(S, kr, bkr, rt, brt, tp, btp, kd[:, 0:64], bkd)
                    S.op("dve", lambda e: e.tensor_copy(out=kd[:, 64:128], in_=kd[:, 0:64]), reads=[bkd], writes=[bkd])
                else:
                    S.op("act", lambda e: e.copy(out=kd[:, 0:64], in_=pk[:, 0:64]), reads=[pkb], writes=[bkd])
                    S.op("act", lambda e: e.copy(out=kd[:, 64:128], in_=pk[:, 0:64]), reads=[pkb], writes=[bkd])
                pt, pb = psum()
                pv = pbf(pt)
                for cc in range(KC4):
                    S.op("pe", lambda e: e.transpose(out=pv[:, cc * 128:(cc + 1) * 128], in_=cn[:, cc * 128:(cc + 1) * 128],
                                                     identity=ident_b[:]), reads=[bcn, b_idb], writes=[pb])
                S.op("pe", lambda e: e.transpose(out=pv[:, KC4 * 128:(KC4 + 1) * 128], in_=kd[:], identity=ident_b[:]),
                     reads=[bkd, b_idb], writes=[pb])
                for cc in range(KC4):
                    S.op("act", lambda e: e.activation(out=ckvT[:, cc, i * 128:(i + 1) * 128], in_=pv[:, cc * 128:(cc + 1) * 128],
                                                       func=AF.Identity, scale=gkv[:, cc:cc + 1]),
                         reads=[pb, b_gkv], writes=[b_ckvT])
                S.op("dve", lambda e: e.tensor_copy(out=krT[:, i * 128:(i + 1) * 128], in_=pv[:, KC4 * 128:(KC4 + 1) * 128]),
                     reads=[pb], writes=[b_krT])
            S.barrier()

        if stop == "B":
            return finish(ckvT[:, 0, 0:128], b_ckvT, 128, is_bf=True)
        with ExitStack() as ph:
            attT, b_attT = sb(ph, "attT", [128, H, TL], BF16)
            ones_b, b_onb = sb(ph, "ones_b", [128, 128], BF16)
            S.op("dve", lambda e: e.memset(ones_b[:], 1.0), writes=[b_onb])
            wuk, b_wuk = sb(ph, "wuk", [128, KC4, 128], BF16)
            wuv, b_wuv = sb(ph, "wuv", [128, KC4, 128], BF16)
            wqn, b_wqn = sb(ph, "wqn", [128, QC, 128], BF16)
            wqr, b_wqr = sb(ph, "wqr", [128, QC, 128], BF16)
            knh, b_knh = sb(ph, "knh", [128, KT], BF16)
            vh, b_vh = sb(ph, "vh", [128, NKC, 128], BF16)
            qnh, b_qnh = sb(ph, "qnh", [128, TL], BF16)
            qrT, b_qrT = sb(ph, "qrT", [128, TL], BF16)
            qrs, b_qrs = sb(ph, "qrs", [128, 128], F32)
            qrr, b_qrr = sb(ph, "qrr", [128, 128], BF16)
            rto, b_rto = sb(ph, "rto", [128, NTL, 64], F32)
            tpq, b_tpq = sb(ph, "tpq", [128, 4, 32], F32)
            NPB = 3
            Pt = [sb(ph, f"Pt{i}", [128, QG], BF16) for i in range(NPB)]
            accs, b_accs = sb(ph, "accs", [128, QG], F32)
            rsum, b_rsum = sb(ph, "rsum", [128, QG], F32)
            for tt in range(NTL):
                S.dma("sp", rto[:, tt, :], rope_own[tt * 128:(tt + 1) * 128, :], writes=[b_rto])
            SB = [psum(reserve=True) for _ in range(NPB)]
            OB = psum(reserve=True)
            for h in range(H):
                cast_load(wuk[:], b_wuk, w_ukv[:, h * 256:h * 256 + 128].rearrange("(c p) n -> p c n", p=128))
                cast_load(wuv[:], b_wuv, w_ukv[:, h * 256 + 128:h * 256 + 256].rearrange("(c p) n -> p c n", p=128))
                cast_load(wqn[:], b_wqn, w_uq[:, h * 192:h * 192 + 128].rearrange("(c p) n -> p c n", p=128))
                if h % 2 == 0:
                    for hh in range(2):
                        c0 = (h + hh) * 192 + 128
                        cast_load(wqr[:, :, hh * 64:(hh + 1) * 64], b_wqr,
                                  w_uq[:, c0:c0 + 64].rearrange("(c p) n -> p c n", p=128))
                    for tt in range(NTL):
                        pt, pb = psum()
                        mm_acc(pt[:, 0:128], pb, [(cqT[:, cc, tt * 128:(tt + 1) * 128], wqr[:, cc, :]) for cc in range(QC)],
                               [b_cqT, b_wqr])
                        S.op("act", lambda e: e.copy(out=qrs[:], in_=pt[:, 0:128]), reads=[pb], writes=[b_qrs])
                        for hh in range(2):
                            rope_apply(S, qrs, b_qrs, rto, b_rto, tpq, b_tpq, qrr[:, hh * 64:(hh + 1) * 64], b_qrr,
                                       src_off=hh * 64, rt_idx=tt)
                        pt2, pb2 = psum()
                        pv2 = pbf(pt2)
                        S.op("pe", lambda e: e.transpose(out=pv2[:, 0:128], in_=qrr[:], identity=ident_b[:]),
                             reads=[b_qrr, b_idb], writes=[pb2])
                        S.op("act", lambda e: e.activation(out=qrT[:, tt * 128:(tt + 1) * 128], in_=pv2[:, 0:128], func=AF.Copy,
                                                           scale=SCALE), reads=[pb2], writes=[b_qrT])
                for tg in range(NQG):
                    pt, pb = psum()
                    mm_acc(pt[:, 0:QG], pb, [(wqn[:, cc, :], cqT[:, cc, tg * QG:(tg + 1) * QG]) for cc in range(QC)],
                           [b_wqn, b_cqT])
                    S.op("act", lambda e: e.activation(out=qnh[:, tg * QG:(tg + 1) * QG], in_=pt[:, 0:QG], func=AF.Copy,
                                                       scale=SCALE), reads=[pb], writes=[b_qnh])
                for k0 in range(0, KT, 512):
                    w = min(512, KT - k0)
                    pt, pb = psum()
                    mm_acc(pt[:, 0:w], pb, [(wuk[:, cc, :], ckvT[:, cc, k0:k0 + w]) for cc in range(KC4)], [b_wuk, b_ckvT])
                    if (k0 // 512) % 2 == 0:
                        S.op("dve", lambda e: e.tensor_copy(out=knh[:, k0:k0 + w], in_=pt[:, 0:w]), reads=[pb], writes=[b_knh])
                    else:
                        S.op("act", lambda e: e.copy(out=knh[:, k0:k0 + w], in_=pt[:, 0:w]), reads=[pb], writes=[b_knh])
                for kc0 in range(0, NKC, 4):
                    n4 = min(4, NKC - kc0)
                    pt, pb = psum()
                    for q4 in range(n4):
                        kc = kc0 + q4
                        mm_acc(pt[:, q4 * 128:(q4 + 1) * 128], pb,
                               [(ckvT[:, cc, kc * 128:(kc + 1) * 128], wuv[:, cc, :]) for cc in range(KC4)], [b_ckvT, b_wuv])
                    if (kc0 // 4) % 2 == 0:
                        S.op("act", lambda e: e.copy(out=vh[:, kc0:kc0 + n4, :], in_=r3(pt[:, 0:n4 * 128])), reads=[pb], writes=[b_vh])
                    else:
                        S.op("dve", lambda e: e.tensor_copy(out=vh[:, kc0:kc0 + n4, :], in_=r3(pt[:, 0:n4 * 128])),
                             reads=[pb], writes=[b_vh])
                r0 = 64 * (h % 2)
                for qg in range(NQG):
                    qs = slice(qg * QG, (qg + 1) * QG)

                    def s_mm(kc):
                        st_, sbb = SB[kc % NPB]
                        S.op("pe", lambda e: e.matmul(st_[:, 0:QG], lhsT=knh[:, kc * 128:(kc + 1) * 128], rhs=qnh[:, qs],
                                                      start=True, stop=False), reads=[b_knh, b_qnh], writes=[sbb])
                        S.op("pe", lambda e: e.matmul(st_[:, 0:QG], lhsT=krT[r0:r0 + 64, kc * 128:(kc + 1) * 128],
                                                      rhs=qrT[r0:r0 + 64, qs], start=False, stop=True),
                             reads=[b_krT, b_qrT], writes=[sbb])

                    s_mm(0)
                    if NKC > 1:
                        s_mm(1)
                    for kc in range(NKC):
                        st_, sbb = SB[kc % NPB]
                        p_t, p_b = Pt[kc % NPB]
                        S.op("act", lambda e: e.activation(out=p_t[:], in_=st_[:, 0:QG], func=AF.Exp), reads=[sbb], writes=[p_b])
                        S.op("pe", lambda e: e.matmul(OB[0][:, 0:QG], lhsT=vh[:, kc, :], rhs=p_t[:], start=(kc == 0),
                                                      stop=(kc == NKC - 1)), reads=[b_vh, p_b], writes=[OB[1]])
                        if kc == 0:
                            S.op("dve", lambda e: e.tensor_copy(out=accs[:], in_=p_t[:]), reads=[p_b], writes=[b_accs])
                        else:
                            S.op("dve", lambda e: e.tensor_tensor(out=accs[:], in0=accs[:], in1=p_t[:], op=ALU.add),
                                 reads=[p_b, b_accs], writes=[b_accs])
                        if kc + 2 < NKC:
                            s_mm(kc + 2)
                    pt, pb = psum()
                    S.op("pe", lambda e: e.matmul(pt[:, 0:QG], lhsT=ones_f[:], rhs=accs[:], start=True, stop=True),
                         reads=[b_ones, b_accs], writes=[pb])
                    S.op("dve", lambda e: e.reciprocal(out=rsum[:], in_=pt[:, 0:QG]), reads=[pb], writes=[b_rsum])
                    S.op("dve", lambda e: e.tensor_tensor(out=attT[:, h, qs], in0=OB[0][:, 0:QG], in1=rsum[:], op=ALU.mult),
                         reads=[OB[1], b_rsum], writes=[b_attT])
            S.dma("sp", attd.rearrange("(h p) t -> p h t", p=128), attT[:], reads=[b_attT])
            S.barrier()
            ps_res.clear()
            if stop == "D":
                return finish(attT[:, 0, 0:128], b_attT, 128, is_bf=True, close=[ph])
        S.barrier()
        kvs.close()

        with ExitStack() as ph:
            g1b, b_g1b = bcast_rows(ph, "g1b", gate1, b_modx)
            if stop == "E0":
                return finish(g1b[:, 0:D], b_g1b, D, close=[ph])
            mixT, b_cvT = sb(ph, "mixT", [128, MC, TL], BF16)
            b_attT = b_cvT
            S.dma("sp", mixT[:, 0:H, :], attd.rearrange("(h p) t -> p h t", p=128), writes=[b_cvT])
            S.dma("sp", mixT[:, H:MC, :], convd.rearrange("(j p) t -> p j t", p=128), writes=[b_cvT])
            cvT = mixT[:, H:MC, :]
            if stop == "E1":
                return finish(cvT[:, 0, 0:128], b_cvT, 128, is_bf=True, close=[ph])
            wo = [sb(ph, f"wo{i}", [128, MC, DW], BF16) for i in range(2)]
            xp = [sb(ph, f"xp{i}", [128, DW], F32) for i in range(3)]
            tq = [sb(ph, f"tq{i}", [128, DW], F32) for i in range(3)]
            it = 0
            for dt in range(NDW):
                wt, wb = wo[dt % 2]
                for mc in range(MC):
                    cast_load(wt[:, mc, :], wb, w_o[mc * 128:(mc + 1) * 128, dt * DW:(dt + 1) * DW])
                for tt in range(NTL):
                    ts_ = slice(tt * 128, (tt + 1) * 128)
                    pt, pb = psum()
                    pairs = [(mixT[:, mc, ts_], wt[:, mc, :]) for mc in range(MC)]
                    mm_acc(pt[:, 0:DW], pb, pairs, [b_attT, b_cvT, wb])
                    xt_, xb_ = xp[it % 3]; tt_, tb_ = tq[it % 3]; it += 1
                    S.dma("sp", xt_[:], x_own[ts_, dt * DW:(dt + 1) * DW], writes=[xb_])
                    S.op("dve", lambda e: e.tensor_tensor(out=tt_[:], in0=pt[:, 0:DW], in1=g1b[:, dt * DW:(dt + 1) * DW], op=ALU.mult),
                         reads=[pb, b_g1b], writes=[tb_])
                    S.op("dve", lambda e: e.tensor_tensor(out=tt_[:], in0=tt_[:], in1=xt_[:], op=ALU.add),
                         reads=[tb_, xb_], writes=[tb_])
                    S.dma("sp", x1d[ts_, dt * DW:(dt + 1) * DW], tt_[:], reads=[tb_])
            S.barrier()
        S.barrier()

        if stop == "E":
            return finish_dram(x1d[0:128, 0:DW], DW)
        mid.close()
        with ExitStack() as peer:
            h2T, b_h2T = sb(peer, "h2T", [128, DC, TL], BF16)
            with ExitStack() as st:
                nt_ = NormT(st, "nF")
                for tt in range(NTL):
                    nt_.run(x1d[tt * 128:(tt + 1) * 128, :], 128, gs2, b_gs2, sh2, b_modx, h2T[:, :, tt * 128:(tt + 1) * 128], b_h2T)
                S.barrier()
            if stop == "F":
                return finish(h2T[:, 0, 0:128], b_h2T, 128, is_bf=True, close=[peer])
            with ExitStack() as st:
                kT, b_kT = sb(st, "kT", [128, HP, 128], BF16)
                qT, b_qT = sb(st, "qT", [128, HP, TL], BF16)
                with ExitStack() as st2:
                    kf, b_kf = sb(st2, "kf", [128, HP, 128], F32)
                    kb_, b_kb = sb(st2, "kb", [128, HP, 128], BF16)
                    S.dma("sp", kf[:], peer_keys.rearrange("h p k d -> k (h p) d"), writes=[b_kf])
                    S.op("dve", lambda e: e.tensor_copy(out=kb_[:], in_=kf[:]), reads=[b_kf], writes=[b_kb])
                    for g0 in range(0, HP, 8):
                        pt, pb = psum()
                        pv = pbf(pt)
                        for hp in range(g0, min(HP, g0 + 8)):
                            S.op("pe", lambda e: e.transpose(out=pv[:, (hp - g0) * 128:(hp - g0 + 1) * 128], in_=kb_[:, hp, :],
                                                             identity=ident_b[:]), reads=[b_kb, b_idb], writes=[pb])
                        n8 = min(HP, g0 + 8) - g0
                        S.op("act", lambda e: e.copy(out=kT[:, g0:g0 + n8, :], in_=r3(pv[:, 0:n8 * 128])), reads=[pb], writes=[b_kT])
                    wq = [sb(st2, f"wq{i}", [128, DC, 128], BF16) for i in range(2)]
                    for hp in range(HP):
                        wt, wb = wq[hp % 2]
                        cast_load(wt[:], wb, peer_wq[:, hp * 128:(hp + 1) * 128].rearrange("(dc p) n -> p dc n", p=128))
                        for tg in range(NQG):
                            pt, pb = psum()
                            mm_acc(pt[:, 0:QG], pb, [(wt[:, dc, :], h2T[:, dc, tg * QG:(tg + 1) * QG]) for dc in range(DC)], [wb, b_h2T])
                            S.op("act", lambda e: e.copy(out=qT[:, hp, tg * QG:(tg + 1) * QG], in_=pt[:, 0:QG]), reads=[pb], writes=[b_qT])
                    S.barrier()
                ub = [sb(st, f"ub{i}", [128, D], BF16) for i in range(2)]
                uT = [sb(st, f"uT{i}", [128, DC, 128], BF16) for i in range(2)]
                wgo = [sb(st, f"wgo{i}", [128, TL], BF16) for i in range(2)]
                LW = min(2048, D)

                NEC = NE // 128

                def u_load(ec):
                    ut, utb = ub[ec % 2]
                    cast_load(ut[:].rearrange("p (a n) -> p a n", n=LW), utb,
                              peer_u[ec * 128:(ec + 1) * 128, :].rearrange("p (a n) -> p a n", n=LW))

                tr_banks = {}
                mm_banks = {}
                T_BANKS = [psum(reserve=True) for _ in range(-(-DC // 8))]
                M_BANKS = [psum(reserve=True) for _ in range(NQG)]

                def T_pe(ec):
                    if ec + 1 < NEC:
                        u_load(ec + 1)
                    ut, utb = ub[ec % 2]
                    tr_banks[ec] = []
                    for g0 in range(0, DC, 8):
                        g1 = min(DC, g0 + 8)
                        pt, pb = T_BANKS[g0 // 8]
                        pv = pbf(pt)
                        for dc in range(g0, g1):
                            S.op("pe", lambda e: e.transpose(out=pv[:, (dc - g0) * 128:(dc - g0 + 1) * 128],
                                                             in_=ut[:, dc * 128:(dc + 1) * 128], identity=ident_b[:]),
                                 reads=[utb, b_idb], writes=[pb])
                        tr_banks[ec].append((g0, g1, pv, pb))

                def T_act(ec):
                    uTt, uTb = uT[ec % 2]
                    for (g0, g1, pv, pb) in tr_banks.pop(ec):
                        S.op("act", lambda e: e.copy(out=uTt[:, g0:g1, :], in_=r3(pv[:, 0:(g1 - g0) * 128])),
                             reads=[pb], writes=[uTb])

                def M_pe(ec):
                    uTt, uTb = uT[ec % 2]
                    mm_banks[ec] = []
                    for tg in range(NQG):
                        pt, pb = M_BANKS[tg]
                        mm_acc(pt[:, 0:QG], pb, [(uTt[:, dc, :], h2T[:, dc, tg * QG:(tg + 1) * QG]) for dc in range(DC)], [uTb, b_h2T])
                        mm_banks[ec].append((tg, pt, pb))

                def M_act(ec):
                    wg, wgb = wgo[ec % 2]
                    for (tg, pt, pb) in mm_banks.pop(ec):
                        S.op("act", lambda e: e.activation(out=wg[:, tg * QG:(tg + 1) * QG], in_=pt[:, 0:QG], func=AF.Gelu),
                             reads=[pb], writes=[wgb])
                    S.dma("sp", WgT[ec * 128:(ec + 1) * 128, :], wg[:], reads=[wgb])

                slices = [(T_pe, T_act, 0)]
                for j in range(NEC):
                    if j + 1 < NEC:
                        slices.append((T_pe, T_act, j + 1))
                    slices.append((M_pe, M_act, j))
                u_load(0)
                n_steps = NTL * 32
                per_step = -(-len(slices) // n_steps)
                sl_state = {"next": 0, "pending": []}

                def slices_pe():
                    for _ in range(per_step):
                        k = sl_state["next"]
                        if k < len(slices):
                            pe_fn, act_fn, ec = slices[k]
                            pend = sl_state["pending"]
                            same = T_act if pe_fn is T_pe else M_act
                            idx = [i for i, (f_, e_) in enumerate(pend) if f_ is same or (pe_fn is M_pe and f_ is T_act and e_ == ec)]
                            if idx:
                                slices_act(idx[-1] + 1)
                            pe_fn(ec)
                            sl_state["pending"].append((act_fn, ec))
                            sl_state["next"] = k + 1

                def slices_act(upto=None):
                    pend = sl_state["pending"]
                    n = len(pend) if upto is None else min(upto, len(pend))
                    for _ in range(n):
                        act_fn, ec = pend.pop(0)
                        act_fn(ec)

                s_sb, b_s = sb(st, "s_sb", [128, HP, 128], F32)
                tmpm, b_tmpm = sb(st, "tmpm", [128, 128], F32)
                t16, b_t16 = sb(st, "t16", [128, HP, 16], F32)
                cand, b_cand = sb(st, "cand", [128, 16, 16], F32)
                cand2, b_cand2 = sb(st, "cand2", [128, 256], F32)
                c8, b_c8 = sb(st, "c8", [128, 16], F32)
                nm, b_nm = sb(st, "nm", [128, HP], F32)
                e16, b_e16 = sb(st, "e16", [128, 2, 16], F32)
                p16, b_p16 = sb(st, "p16", [128, 16, 16], F32)
                jz, b_jz = sb(st, "jz", [128, 256], F32)
                Zs, b_Zs = sb(st, "Zs", [128, PH], F32)
                lnz, b_lnz = sb(st, "lnz", [128, PH], F32)
                negc, b_negc = sb(st, "negc", [128, PH], F32)
                thr, b_thr = sb(st, "thr", [128, PH], F32)
                B1, b_B1 = sb(st, "B1", [128, PH, 128], F32)
                R1, b_R1 = sb(st, "R1", [128, PH, 128], F32)
                NPT = 12
                Pex = [(sb(st, f"Pex{i}", [128, 512], BF16)[0], [Buf() for _ in range(4)]) for i in range(NPT)]
                Tmk = [(sb(st, f"Tmk{i}", [128, 512], BF16)[0], [Buf()] * 4) for i in range(NPT)]
                wtok = [sb(st, f"wtok{i}", [128, 512], BF16) for i in range(2)]
                WTs = [sb(st, f"WTs{i}", [128, 16, 128], BF16) for i in range(2)]
                PTOK = psum(reserve=True)
                PTR = psum(reserve=True)
                WdT_v = WdT.rearrange("(i1 i2) t -> i2 i1 t", i2=128)
                ipx = 0
                for tt in range(NTL):
                    ts_ = slice(tt * 128, (tt + 1) * 128)
                    for g0 in range(0, HP, 4):
                        pt, pb = (PTOK, PTR)[(g0 // 4) % 2]
                        for hp in range(g0, g0 + 4):
                            S.op("pe", lambda e: e.matmul(pt[:, (hp - g0) * 128:(hp - g0 + 1) * 128], lhsT=qT[:, hp, ts_],
                                                          rhs=kT[:, hp, :], start=True, stop=True),
                                 reads=[b_qT, b_kT], writes=[pb])
                        S.op("act", lambda e: e.copy(out=s_sb[:, g0:g0 + 4, :], in_=r3(pt[:, 0:512])), reads=[pb], writes=[b_s])
                    for hp in range(HP):
                        S.op("dve", lambda e: e.max(out=t16[:, hp, 0:8], in_=s_sb[:, hp, :]), reads=[b_s], writes=[b_t16])
                        S.op("dve", lambda e: e.match_replace(out=tmpm[:], in_to_replace=t16[:, hp, 0:8], in_values=s_sb[:, hp, :],
                                                              imm_value=NEG), reads=[b_s, b_t16], writes=[b_tmpm])
                        S.op("dve", lambda e: e.max(out=t16[:, hp, 8:16], in_=tmpm[:]), reads=[b_tmpm], writes=[b_t16])
                    S.op("dve", lambda e: e.tensor_scalar(out=nm[:], in0=t16[:, :, 0], scalar1=-1.0, scalar2=None, op0=ALU.mult),
                         reads=[b_t16], writes=[b_nm])
                    for h in range(PH):
                        for b in range(16):
                            S.op("dve", lambda e: e.tensor_scalar(out=cand[:, :, b], in0=t16[:, 2 * h, :],
                                                                  scalar1=t16[:, 2 * h + 1, b:b + 1], scalar2=None, op0=ALU.add),
                                 reads=[b_t16], writes=[b_cand])
                        cflat = cand[:].rearrange("p a b -> p (a b)")
                        S.op("dve", lambda e: e.max(out=c8[:, 0:8], in_=cflat), reads=[b_cand], writes=[b_c8])
                        S.op("dve", lambda e: e.match_replace(out=cand2[:], in_to_replace=c8[:, 0:8], in_values=cflat,
                                                              imm_value=NEG), reads=[b_cand, b_c8], writes=[b_cand2])
                        S.op("dve", lambda e: e.max(out=c8[:, 8:16], in_=cand2[:]), reads=[b_cand2], writes=[b_c8])
                        S.op("dve", lambda e: e.tensor_copy(out=thr[:, h:h + 1], in_=c8[:, 15:16]), reads=[b_c8], writes=[b_thr])
                        for p_ in range(2):
                            S.op("act", lambda e: e.activation(out=e16[:, p_, :], in_=t16[:, 2 * h + p_, :], func=AF.Exp,
                                                               bias=nm[:, 2 * h + p_:2 * h + p_ + 1]),
                                 reads=[b_t16, b_nm], writes=[b_e16])
                        for b in range(16):
                            S.op("dve", lambda e: e.tensor_scalar(out=p16[:, :, b], in0=e16[:, 0, :], scalar1=e16[:, 1, b:b + 1],
                                                                  scalar2=None, op0=ALU.mult), reads=[b_e16], writes=[b_p16])
                        S.op("dve", lambda e: e.scalar_tensor_tensor(out=jz[:], in0=cflat, scalar=thr[:, h:h + 1],
                                                                     in1=p16[:].rearrange("p a b -> p (a b)"), op0=ALU.is_ge,
                                                                     op1=ALU.mult, accum_out=Zs[:, h:h + 1]),
                             reads=[b_cand, b_thr, b_p16], writes=[b_jz, b_Zs])
                        S.op("dve", lambda e: e.tensor_scalar(out=R1[:, h, :], in0=s_sb[:, 2 * h, :], scalar1=-1.0,
                                                              scalar2=thr[:, h:h + 1], op0=ALU.mult, op1=ALU.add),
                             reads=[b_s, b_thr], writes=[b_R1])
                    S.op("act", lambda e: e.activation(out=lnz[:], in_=Zs[:], func=AF.Ln), reads=[b_Zs], writes=[b_lnz])
                    nm2 = nm[:].rearrange("p (h two) -> p h two", two=2)
                    S.op("dve", lambda e: e.tensor_tensor(out=negc[:], in0=nm2[:, :, 0], in1=nm2[:, :, 1], op=ALU.add),
                         reads=[b_nm], writes=[b_negc])
                    S.op("dve", lambda e: e.tensor_tensor(out=negc[:], in0=negc[:], in1=lnz[:], op=ALU.subtract),
                         reads=[b_negc, b_lnz], writes=[b_negc])
                    for h in range(PH):
                        S.op("dve", lambda e: e.tensor_scalar(out=B1[:, h, :], in0=s_sb[:, 2 * h, :], scalar1=negc[:, h:h + 1],
                                                              scalar2=None, op0=ALU.add), reads=[b_s, b_negc], writes=[b_B1])
                    def tr_pe(gp):
                        wk, wkb = wtok[gp % 2]
                        pv = pbf(PTR[0])
                        for ii in range(4):
                            S.op("pe", lambda e: e.transpose(out=pv[:, ii * 128:(ii + 1) * 128], in_=wk[:, ii * 128:(ii + 1) * 128],
                                                             identity=ident_b[:]), reads=[wkb, b_idb], writes=[PTR[1]])

                    def tr_act(gp):
                        wts, wtb = WTs[(gp // 4) % 2]
                        pv = pbf(PTR[0])
                        q4 = (gp % 4) * 4
                        S.op("act", lambda e: e.copy(out=wts[:, q4:q4 + 4, :], in_=r3(pv[:, 0:512])), reads=[PTR[1]], writes=[wtb])
                        if gp % 4 == 3:
                            g16 = gp // 4
                            S.dma("sp", WdT_v[:, g16 * 16:(g16 + 1) * 16, ts_], wts[:, :, :], reads=[wtb])

                    for g in range(32):
                        prev_pending = len(sl_state["pending"])
                        slices_pe()
                        if g > 0:
                            tr_pe(g - 1)
                        for h in range(PH):
                            px, pxb = Pex[ipx % NPT]; tm_, tmb = Tmk[ipx % NPT]; ipx += 1
                            for ii in range(4):
                                i1 = g * 4 + ii
                                cs = slice(ii * 128, (ii + 1) * 128)
                                S.op("act", lambda e: e.activation(out=px[:, cs], in_=s_sb[:, 2 * h + 1, :], func=AF.Exp,
                                                                   bias=B1[:, h, i1:i1 + 1]), reads=[b_s, b_B1], writes=[pxb[ii]])
                                S.op("dve", lambda e: e.scalar_tensor_tensor(out=tm_[:, cs], in0=s_sb[:, 2 * h + 1, :],
                                                                             scalar=R1[:, h, i1:i1 + 1], in1=px[:, cs],
                                                                             op0=ALU.is_ge, op1=ALU.mult),
                                     reads=[b_s, b_R1, pxb[ii]], writes=[tmb[ii]])
                            S.op("pe", lambda e: e.matmul(PTOK[0][:, 0:512], lhsT=ident_b[:], rhs=tm_[:, 0:512],
                                                          start=(h == 0), stop=(h == PH - 1)),
                                 reads=tmb + [b_idb], writes=[PTOK[1]])
                        slices_act(prev_pending)
                        if g > 0:
                            tr_act(g - 1)
                        wk, wkb = wtok[g % 2]
                        S.op("act", lambda e: e.copy(out=wk[:], in_=PTOK[0][:, 0:512]), reads=[PTOK[1]], writes=[wkb])
                    tr_pe(31)
                    tr_act(31)
                while sl_state["next"] < len(slices) or sl_state["pending"]:
                    pp = len(sl_state["pending"])
                    slices_pe()
                    slices_act(pp if pp > 0 else None)
                S.barrier()
                ps_res.clear()
        S.barrier()
        if stop == "P2a":
            return finish_dram(WgT[0:128, 0:128], 128, is_bf=True)
        with ExitStack() as ph:
            acc, b_acc = sb(ph, "acc", [128, NTL, D], F32)
            G = 8
            with ExitStack() as st:
                wgl = [sb(st, f"wgl{i}", [128, G, TL], BF16) for i in range(2)]
                wtl2, b_wtl2 = sb(st, "wtl2", [128, G, TL], BF16)
                vt = [sb(st, f"vt{i}", [128, G, DW], BF16) for i in range(2)]
                iv = 0
                NEG_ = NE // (128 * G)

                def load_wg(eg):
                    wl, wlb = wgl[eg % 2]
                    S.dma("sp", wl[:], WgT[eg * 128 * G:(eg + 1) * 128 * G, :].rearrange("(c p) t -> p c t", p=128), writes=[wlb])
                    S.dma("sp", wtl2[:], WdT[eg * 128 * G:(eg + 1) * 128 * G, :].rearrange("(c p) t -> p c t", p=128), writes=[b_wtl2])
                    S.op("dve", lambda e: e.tensor_tensor(out=wl[:], in0=wl[:], in1=wtl2[:], op=ALU.mult),
                         reads=[wlb, b_wtl2], writes=[wlb])

                load_wg(0)
                for eg in range(NEG_):
                    wl, wlb = wgl[eg % 2]
                    if eg + 1 < NEG_:
                        load_wg(eg + 1)
                    for dt in range(NDW):
                        v_, vb_ = vt[iv % 2]; iv += 1
                        cast_load(v_[:], vb_, peer_v[eg * 128 * G:(eg + 1) * 128 * G, dt * DW:(dt + 1) * DW].rearrange(
                            "(c p) n -> p c n", p=128))
                        for tt in range(NTL):
                            pt, pb = psum()
                            mm_acc(pt[:, 0:DW], pb, [(wl[:, c, tt * 128:(tt + 1) * 128], v_[:, c, :]) for c in range(G)], [wlb, vb_])
                            dst = acc[:, tt, dt * DW:(dt + 1) * DW]
                            if eg == 0:
                                S.op("act", lambda e: e.copy(out=dst, in_=pt[:, 0:DW]), reads=[pb], writes=[b_acc])
                            else:
                                S.op("dve", lambda e: e.tensor_tensor(out=dst, in0=dst, in1=pt[:, 0:DW], op=ALU.add),
                                     reads=[pb, b_acc], writes=[b_acc])
                S.barrier()
            S.barrier()
            with ExitStack() as st:
                g2b, b_g2b = bcast_rows(st, "g2b", gate2, b_modx)
                gfb, b_gfb = bcast_rows(st, "gfb", gfin, b_gfin)
                x1t = [sb(st, f"x1t{i}", [128, D], F32) for i in range(1)]
                jf, b_jf = sb(st, "jf", [128, D], BF16)
                ssf = [sb(st, f"ssf{i}", [128, 1], F32) for i in range(2)]
                rsf = [sb(st, f"rsf{i}", [128, 1], F32) for i in range(2)]
                for tt in range(NTL):
                    xt_, xb_ = x1t[0]
                    ss, bss = ssf[tt % 2]; rs, brs = rsf[tt % 2]
                    S.dma("sp", xt_[:], x1d[tt * 128:(tt + 1) * 128, :], writes=[xb_])
                    a_ = acc[:, tt, :]
                    S.op("dve", lambda e: e.tensor_tensor(out=a_, in0=a_, in1=g2b[:], op=ALU.mult), reads=[b_acc, b_g2b], writes=[b_acc])
                    S.op("dve", lambda e: e.tensor_tensor(out=a_, in0=a_, in1=xt_[:], op=ALU.add), reads=[b_acc, xb_], writes=[b_acc])
                    S.op("act", lambda e: e.activation(out=jf[:], in_=a_, func=AF.Square, accum_out=ss[:]),
                         reads=[b_acc], writes=[b_jf, bss])
                    rstd_from_ss(ss[:], bss, rs[:], brs, D)
                    S.op("dve", lambda e: e.scalar_tensor_tensor(out=a_, in0=a_, scalar=rs[:, 0:1], in1=gfb[:], op0=ALU.mult,
                                                                 op1=ALU.mult), reads=[b_acc, brs, b_gfb], writes=[b_acc])
                    S.dma("sp", out_d[tt * 128:(tt + 1) * 128, :], a_, reads=[b_acc])
                S.barrier()
        S.final_wait("sp")
        S.barrier()
    print("built: instrs", S.n_instr, "sems", S.nsem)
    return nc


def rope_apply(S, src, b_src, rt, b_rt, tp, b_tp, dst, b_dst, src_off=0, rt_idx=None):
    x = src[:, src_off:src_off + 64].rearrange("p (a h f) -> p a h f", a=2, h=2)
    r = (rt[:, rt_idx, :] if rt_idx is not None else rt[:, :]).rearrange("p (a s f) -> p a s f", a=2, s=2)
    x1 = x[:, :, 0, :]; x2 = x[:, :, 1, :]
    cs = r[:, :, 0, :]; sn = r[:, :, 1, :]
    t = tp[:].rearrange("p k (a f) -> p k a f", a=2)
    o = dst.rearrange("p (a h f) -> p a h f", a=2, h=2)
    rd = [b_src, b_rt]
    S.op("dve", lambda e: e.tensor_tensor(out=t[:, 0], in0=x1, in1=cs, op=ALU.mult), reads=rd, writes=[b_tp])
    S.op("dve", lambda e: e.tensor_tensor(out=t[:, 1], in0=x2, in1=sn, op=ALU.mult), reads=rd, writes=[b_tp])
    S.op("dve", lambda e: e.tensor_tensor(out=t[:, 2], in0=x2, in1=cs, op=ALU.mult), reads=rd, writes=[b_tp])
    S.op("dve", lambda e: e.tensor_tensor(out=t[:, 3], in0=x1, in1=sn, op=ALU.mult), reads=rd, writes=[b_tp])
    S.op("dve", lambda e: e.tensor_tensor(out=o[:, :, 0, :], in0=t[:, 0], in1=t[:, 1], op=ALU.subtract), reads=[b_tp], writes=[b_dst])
    S.op("dve", lambda e: e.tensor_tensor(out=o[:, :, 1, :], in0=t[:, 2], in1=t[:, 3], op=ALU.add), reads=[b_tp], writes=[b_dst])


def rope_table(T, grid_w=64, theta=10000.0):
    rows = T // grid_w
    row = np.repeat(np.arange(rows), grid_w).astype(np.float32)
    col = np.tile(np.arange(grid_w), rows).astype(np.float32)
    half = 32
    freqs = (np.float32(theta) ** (-np.arange(0, half, 2, dtype=np.float32) / np.float32(half))).astype(np.float32)
    ar = row[:, None] * freqs
    ac = col[:, None] * freqs
    return np.concatenate([np.cos(ar), np.sin(ar), np.cos(ac), np.sin(ac)], axis=1).astype(np.float32)


_NC_CACHE = {}


def run(cfg, inputs, stop=None):
    key = tuple(sorted(cfg.items())) + (stop,)
    if key not in _NC_CACHE:
        _NC_CACHE[key] = build(cfg, stop)
    nc = _NC_CACHE[key]
    D = cfg["D"]; T = cfg["T"]; TL = T // NCORES
    f = lambda a: np.ascontiguousarray(np.asarray(a, dtype=np.float32))
    x = f(inputs["x"])[0]
    rope = rope_table(T)
    shared = {
        "x_full": x, "ctx": f(inputs["ctx"])[0], "c": f(inputs["c"])[0], "c_ctx": f(inputs["c_ctx"]),
        "w_ada": f(inputs["w_ada"])[0], "b_ada": f(inputs["b_ada"])[0], "norm1_g": f(inputs["norm1_g"])[0],
        "w_in": f(inputs["w_in"])[0], "q_norm_g": f(inputs["q_norm_g"])[0], "w_uq": f(inputs["w_uq"])[0],
        "kv_norm_g": f(inputs["kv_norm_g"])[0], "w_ukv": f(inputs["w_ukv"])[0], "conv_w": f(inputs["conv_w"])[0],
        "conv_b": f(inputs["conv_b"])[0], "w_o": f(inputs["w_o"])[0], "norm2_g": f(inputs["norm2_g"])[0],
        "peer_wq": f(inputs["peer_wq"])[0], "peer_keys": f(inputs["peer_keys"])[0], "peer_u": f(inputs["peer_u"])[0],
        "peer_v": f(inputs["peer_v"])[0], "final_norm_g": f(inputs["final_norm_g"]), "rope_all": rope,
        "ident": np.eye(128, dtype=np.float32),
    }
    in_maps = []
    for c in range(NCORES):
        m = dict(shared)
        m["x_own"] = np.ascontiguousarray(x[c * TL:(c + 1) * TL])
        halo = np.zeros((2, D), np.float32)
        hm = np.zeros((128, 2), np.float32)
        if c > 0:
            halo[0] = x[c * TL - 1]; hm[:, 0] = 1.0
        if c < NCORES - 1:
            halo[1] = x[(c + 1) * TL]; hm[:, 1] = 1.0
        m["x_halo"] = halo
        m["hmask"] = hm
        m["rope_own"] = np.ascontiguousarray(rope[c * TL:(c + 1) * TL])
        in_maps.append(m)
    res = run_bass_kernel_spmd(nc, in_maps, core_ids=list(range(NCORES)))
    out = np.concatenate([np.asarray(res.results[c]["out"], dtype=np.float32) for c in range(NCORES)], axis=0)
    return out[None]


def kernel(**inputs):
    return run(CFG_FULL, inputs)
```

```python
import math
from contextlib import ExitStack

import numpy as np
import concourse.bass as bass
import concourse.mybir as mybir
from concourse.bass_utils import run_bass_kernel_spmd

F32 = mybir.dt.float32
BF16 = mybir.dt.bfloat16
AF = mybir.ActivationFunctionType
ALU = mybir.AluOpType
AX = mybir.AxisListType

NCORES = 8
EPS = 1e-6
NEG = -1.0e30

CFG_FULL = dict(D=4096, T=8192, TC=256, H=16, QL=768, KVL=512, CC=2048, PH=8)


class Buf:
    __slots__ = ("w", "r", "excl")

    def __init__(self, excl=False):
        self.w = None
        self.r = []
        self.excl = excl


class Sched:
    NS = 8
    EPOCH = 60000

    def __init__(self, nc, stack):
        self.nc = nc
        self.stack = stack
        self.eng = {"pe": nc.tensor, "act": nc.scalar, "dve": nc.vector, "pool": nc.gpsimd, "sp": nc.sync}
        self.sem = {}
        self.cnt = {}
        self.nsem = 0
        self.seen = {k: {} for k in self.eng}
        self.all_sems = {}
        for k in self.eng:
            self._new_eng_sem(k)
        self.dsem = {}
        self.dval = {}
        self.dslot = {}
        for q in ("sp", "pool", "act"):
            self.dsem[q] = [self._mksem(f"d_{q}{i}") for i in range(self.NS)]
            self.dval[q] = [0] * self.NS
            self.dslot[q] = 0
        self.n_instr = 0

    def _mksem(self, name):
        s = self.stack.enter_context(self.nc.semaphore(f"{name}_{self.nsem}"))
        self.nsem += 1
        self.all_sems[id(s)] = s
        return s

    def _new_eng_sem(self, k):
        self.sem[k] = self._mksem(f"e_{k}")
        self.cnt[k] = 0

    def _wait(self, e, ev):
        sem, val, src = ev
        if e == "pe" and src == "pe":
            return
        seen = self.seen[e]
        key = id(sem)
        if seen.get(key, 0) >= val:
            return
        self.eng[e].wait_ge(sem, val)
        seen[key] = val

    def _deps(self, e, reads, writes):
        for b in reads:
            if b.w is not None:
                self._wait(e, b.w)
            if b.excl:
                for ev in b.r:
                    self._wait(e, ev)
        for b in writes:
            if b.w is not None:
                self._wait(e, b.w)
            for ev in b.r:
                self._wait(e, ev)

    def _record(self, ev, reads, writes):
        for b in reads:
            if b.excl:
                b.w = ev
                b.r = []
            else:
                b.r.append(ev)
                if len(b.r) > 24:
                    b.r = b.r[-24:]
        for b in writes:
            b.w = ev
            b.r = []

    def op(self, e, fn, reads=(), writes=()):
        if self.cnt[e] >= self.EPOCH:
            self._new_eng_sem(e)
        self._deps(e, reads, writes)
        ins = fn(self.eng[e])
        self.cnt[e] += 1
        ins.then_inc(self.sem[e], 1)
        self.n_instr += 1
        ev = (self.sem[e], self.cnt[e], e)
        self._record(ev, reads, writes)
        return ev

    def dma(self, q, out, in_, reads=(), writes=(), **kw):
        self._deps(q, reads, writes)
        slot = self.dslot[q]
        self.dslot[q] = (slot + 1) % self.NS
        sem = self.dsem[q][slot]
        prev = self.dval[q][slot]
        if prev > 0:
            self._wait(q, (sem, prev, "dma"))
        ins = self.eng[q].dma_start(out=out, in_=in_, **kw)
        ins.then_inc(sem, 16)
        self.dval[q][slot] = prev + 16
        self.n_instr += 1
        ev = (sem, prev + 16, "dma")
        self._record(ev, reads, writes)
        return ev

    def barrier(self):
        evs = []
        for k in self.eng:
            if self.cnt[k] > 0:
                evs.append((self.sem[k], self.cnt[k], k))
        for q in self.dsem:
            for i in range(self.NS):
                if self.dval[q][i] > 0:
                    evs.append((self.dsem[q][i], self.dval[q][i], "dma"))
        for e in self.eng:
            for ev in evs:
                if ev[2] == e and e != "pe":
                    continue
                sem, val, src = ev
                seen = self.seen[e]
                if seen.get(id(sem), 0) >= val:
                    continue
                self.eng[e].wait_ge(sem, val)
                seen[id(sem)] = val

    def final_wait(self, e="sp"):
        for q in self.dsem:
            for i in range(self.NS):
                if self.dval[q][i] > 0:
                    self._wait(e, (self.dsem[q][i], self.dval[q][i], "dma"))


def build(cfg, stop=None):
    D = cfg["D"]; T = cfg["T"]; TC = cfg["TC"]; H = cfg["H"]; QL = cfg["QL"]; KVL = cfg["KVL"]
    CC = cfg["CC"]; PH = cfg["PH"]
    TL = T // NCORES
    DC = D // 128
    NT = T // 128
    NTC = TC // 128
    NTL = TL // 128
    KT = T + TC
    NKC = KT // 128
    CCH = CC // 128
    MIX = H * 128 + CC
    MC = MIX // 128
    INC = QL + KVL + 64 + 3 * CC
    O_CKV = QL
    O_CONV = QL + KVL + 64
    QC = QL // 128
    KC4 = KVL // 128
    NE = 16384
    HP = PH * 2
    QG = min(512, TL)
    NQG = TL // QG
    DW = min(512, D)
    NDW = D // DW
    SCALE = 1.0 / math.sqrt(192.0)
    assert H % 2 == 0 and KVL <= 512 and 6 * DC * 2 <= 512

    nc = bass.Bass("TRN2", target_bir_lowering=False)

    def din(name, shape, dt=F32):
        return nc.dram_tensor(name, list(shape), dt, kind="ExternalInput").ap()

    x_full = din("x_full", [T, D]); x_own = din("x_own", [TL, D]); x_halo = din("x_halo", [2, D])
    hmask = din("hmask", [128, 2]); ctx = din("ctx", [TC, D])
    c_in = din("c", [D]); cc_in = din("c_ctx", [D])
    w_ada = din("w_ada", [D, 6 * D]); b_ada = din("b_ada", [6 * D])
    norm1_g = din("norm1_g", [D]); w_in = din("w_in", [D, INC]); q_norm_g = din("q_norm_g", [QL])
    w_uq = din("w_uq", [QL, H * 192]); kv_norm_g = din("kv_norm_g", [KVL]); w_ukv = din("w_ukv", [KVL, H * 256])
    conv_w = din("conv_w", [3, CC]); conv_b = din("conv_b", [CC]); w_o = din("w_o", [MIX, D])
    norm2_g = din("norm2_g", [D]); peer_wq = din("peer_wq", [D, HP * 128])
    peer_keys = din("peer_keys", [PH, 2, 128, 128]); peer_u = din("peer_u", [NE, D]); peer_v = din("peer_v", [NE, D])
    final_g = din("final_norm_g", [D]); rope_all = din("rope_all", [T, 64]); rope_own = din("rope_own", [TL, 64])
    ident_in = din("ident", [128, 128])
    out_d = nc.dram_tensor("out", [TL, D], F32, kind="ExternalOutput").ap()
    x1d = nc.dram_tensor("x1d", [TL, D], F32).ap()
    convd = nc.dram_tensor("convd", [CC, TL], BF16).ap()
    attd = nc.dram_tensor("attd", [H * 128, TL], BF16).ap()
    WdT = nc.dram_tensor("WdT", [NE, TL], BF16).ap()
    WgT = nc.dram_tensor("WgT", [NE, TL], BF16).ap()

    with ExitStack() as top:
        S = Sched(nc, top)
        top.enter_context(nc.allow_non_contiguous_dma(reason="small per-feature vectors"))
        top.enter_context(nc.allow_low_precision(reason="bf16 matmul operands"))

        def sb(stack, name, shape, dt):
            t = stack.enter_context(nc.sbuf_tensor(name, list(shape), dt))
            return t, Buf()

        PSF = []; PSB = []
        for i in range(8):
            t = top.enter_context(nc.psum_tensor(f"ps{i}", [128, 512], F32))
            PSF.append(t); PSB.append(Buf(excl=True))
        ps_rr = [0]
        ps_res = set()

        def psum(reserve=False):
            assert len(ps_res) < 8, "all PSUM banks reserved"
            while True:
                i = ps_rr[0]; ps_rr[0] = (i + 1) % 8
                if i not in ps_res:
                    break
            if reserve:
                ps_res.add(i)
            return PSF[i], PSB[i]

        dbg_stack = ExitStack()
        mid = ExitStack()
        kvs = ExitStack()

        def finish(tile_ap=None, buf=None, n=None, is_bf=False, close=()):
            if tile_ap is not None:
                if is_bf:
                    tmp, tb = sb(dbg_stack, "dbgtmp", [128, n], F32)
                    S.op("dve", lambda e: e.tensor_copy(out=tmp[:], in_=tile_ap), reads=[buf], writes=[tb])
                    S.dma("sp", out_d[0:128, 0:n], tmp[:], reads=[tb])
                else:
                    S.dma("sp", out_d[0:128, 0:n], tile_ap, reads=[buf])
            S.final_wait("sp")
            S.barrier()
            print("built (stop=%s): instrs %d sems %d" % (stop, S.n_instr, S.nsem))
            dbg_stack.close()
            for st_ in close:
                st_.close()
            kvs.close(); mid.close()
            return nc

        def finish_dram(src_ap, n, is_bf=False, close=()):
            t_, tb_ = sb(dbg_stack, "dbgd", [128, n], BF16 if is_bf else F32)
            S.dma("sp", t_[:], src_ap, writes=[tb_])
            return finish(t_[:], tb_, n, is_bf=is_bf, close=close)

        def r3(ap, b=128):
            return ap.rearrange("p (a b) -> p a b", b=b)

        def pbf(pt):
            return pt[:].bitcast(BF16)

        ident_f, b_idf = sb(top, "ident_f", [128, 128], F32)
        ident_b, b_idb = sb(top, "ident_b", [128, 128], BF16)
        ones_f, b_ones = sb(top, "ones_f", [128, 128], F32)
        modx, b_modx = sb(top, "modx", [128, 6 * DC], F32)
        modc, b_modc = sb(top, "modc", [128, 2 * DC], F32)
        gs1, b_gs1 = sb(top, "gs1", [128, DC], F32)
        gs1c, b_gs1c = sb(top, "gs1c", [128, DC], F32)
        gs2, b_gs2 = sb(top, "gs2", [128, DC], F32)
        gq, b_gq = sb(top, "gq", [128, QC], F32)
        gkv, b_gkv = sb(top, "gkv", [128, KC4], F32)
        gfin, b_gfin = sb(top, "gfin", [128, DC], F32)
        epsb, b_eps = sb(top, "epsb", [128, 1], F32)

        S.dma("sp", ident_f[:], ident_in[:, :], writes=[b_idf])
        S.op("dve", lambda e: e.tensor_copy(out=ident_b[:], in_=ident_f[:]), reads=[b_idf], writes=[b_idb])
        S.op("dve", lambda e: e.memset(ones_f[:], 1.0), writes=[b_ones])
        S.op("dve", lambda e: e.memset(epsb[:], EPS), writes=[b_eps])

        def vec_pc(dst, buf, src, n):
            S.dma("sp", dst, src.rearrange("(c p) -> p c", p=128), writes=[buf])

        vec_pc(gq[:], b_gq, q_norm_g, QC)
        vec_pc(gkv[:], b_gkv, kv_norm_g, KC4)
        vec_pc(gfin[:], b_gfin, final_g, DC)

        def rstd_from_ss(ss, b_ss, rs, b_rs, n):
            S.op("act", lambda e: e.activation(out=rs, in_=ss, func=AF.Sqrt, bias=epsb[:ss.shape[0], :], scale=1.0 / n),
                 reads=[b_ss, b_eps], writes=[b_rs])
            S.op("dve", lambda e: e.reciprocal(out=rs, in_=rs), reads=[b_rs], writes=[b_rs])

        def bcast_rows(stack, name, vec, b_vec):
            dst, b_dst = sb(stack, name, [128, D], F32)
            with ExitStack() as st:
                dg, b_dg = sb(st, name + "_dg", [128, 2, 128], F32)
                for dc in range(DC):
                    k = dc % 2
                    S.op("dve", lambda e: e.tensor_scalar(out=dg[:, k, :], in0=ident_f[:], scalar1=vec[:, dc:dc + 1],
                                                          scalar2=None, op0=ALU.mult),
                         reads=[b_idf, b_vec], writes=[b_dg])
                    pt, pb = psum()
                    S.op("pe", lambda e: e.matmul(pt[:, 0:128], lhsT=ones_f[:], rhs=dg[:, k, :], start=True, stop=True),
                         reads=[b_ones, b_dg], writes=[pb])
                    S.op("act", lambda e: e.copy(out=dst[:, dc * 128:(dc + 1) * 128], in_=pt[:, 0:128]),
                         reads=[pb], writes=[b_dst])
                S.barrier()
            return dst, b_dst

        with ExitStack() as ph:
            cin, b_cin = sb(ph, "cin", [128, 2, DC], F32)
            scv, b_scv = sb(ph, "scv", [128, DC, 2], F32)
            bt, b_bt = sb(ph, "bt", [128, 6 * DC], F32)
            g1t, b_g1t = sb(ph, "g1t", [128, DC], F32)
            g2t, b_g2t = sb(ph, "g2t", [128, DC], F32)
            BW = 256
            NBLK = 6 * D // BW
            wad = [sb(ph, f"wad{i}", [128, DC, BW], F32) for i in range(3)]
            wab = [sb(ph, f"wab{i}", [128, DC, BW], BF16) for i in range(2)]
            scb, b_scb = sb(ph, "scb", [128, DC, 2], BF16)
            S.dma("sp", cin[:, 0, :], c_in.rearrange("(c p) -> p c", p=128), writes=[b_cin])
            S.dma("sp", cin[:, 1, :], cc_in.rearrange("(c p) -> p c", p=128), writes=[b_cin])
            for j in range(6):
                S.dma("sp", bt[:, j * DC:(j + 1) * DC], b_ada[j * D:(j + 1) * D].rearrange("(c p) -> p c", p=128),
                      writes=[b_bt])
            vec_pc(g1t[:], b_g1t, norm1_g, DC)
            vec_pc(g2t[:], b_g2t, norm2_g, DC)
            for i in range(2):
                S.op("act", lambda e: e.activation(out=scv[:, :, i], in_=cin[:, i, :], func=AF.Silu),
                     reads=[b_cin], writes=[b_scv])
            S.op("dve", lambda e: e.tensor_copy(out=scb[:], in_=scv[:]), reads=[b_scv], writes=[b_scb])
            mps, mpb = psum()
            HDC = max(1, DC // 2)
            for blk in range(NBLK):
                wt, wb = wad[blk % 3]
                wtb, wbb = wab[blk % 2]
                S.dma("sp", wt[:], w_ada[:, blk * BW:(blk + 1) * BW].rearrange("(kc p) n -> p kc n", p=128), writes=[wb])
                if DC > 1:
                    S.op("dve", lambda e: e.tensor_copy(out=wtb[:, 0:HDC, :], in_=wt[:, 0:HDC, :]), reads=[wb], writes=[wbb])
                    S.op("act", lambda e: e.copy(out=wtb[:, HDC:DC, :], in_=wt[:, HDC:DC, :]), reads=[wb], writes=[wbb])
                else:
                    S.op("dve", lambda e: e.tensor_copy(out=wtb[:], in_=wt[:]), reads=[wb], writes=[wbb])
                for j in range(BW // 128):
                    col = blk * (BW // 128) + j
                    for kc in range(DC):
                        S.op("pe", lambda e: e.matmul(mps[:, col * 2:col * 2 + 2], lhsT=wtb[:, kc, j * 128:(j + 1) * 128],
                                                      rhs=scb[:, kc, :], start=(kc == 0), stop=(kc == DC - 1)),
                             reads=[wbb, b_scb], writes=[mpb])
            mview = mps[:, 0:12 * DC].rearrange("p (c two) -> p c two", two=2)
            S.op("dve", lambda e: e.tensor_tensor(out=modx[:], in0=mview[:, :, 0], in1=bt[:], op=ALU.add),
                 reads=[mpb, b_bt], writes=[b_modx])
            S.op("dve", lambda e: e.tensor_tensor(out=modc[:], in0=mview[:, 0:2 * DC, 1], in1=bt[:, 0:2 * DC], op=ALU.add),
                 reads=[mpb, b_bt], writes=[b_modc])
            S.op("dve", lambda e: e.scalar_tensor_tensor(out=gs1[:], in0=modx[:, DC:2 * DC], scalar=1.0, in1=g1t[:],
                                                         op0=ALU.add, op1=ALU.mult), reads=[b_modx, b_g1t], writes=[b_gs1])
            S.op("dve", lambda e: e.scalar_tensor_tensor(out=gs1c[:], in0=modc[:, DC:2 * DC], scalar=1.0, in1=g1t[:],
                                                         op0=ALU.add, op1=ALU.mult), reads=[b_modc, b_g1t], writes=[b_gs1c])
            S.op("dve", lambda e: e.scalar_tensor_tensor(out=gs2[:], in0=modx[:, 4 * DC:5 * DC], scalar=1.0, in1=g2t[:],
                                                         op0=ALU.add, op1=ALU.mult), reads=[b_modx, b_g2t], writes=[b_gs2])
            S.barrier()
        sh1 = modx[:, 0:DC]; sh1c = modc[:, 0:DC]; sh2 = modx[:, 3 * DC:4 * DC]
        gate1 = modx[:, 2 * DC:3 * DC]; gate2 = modx[:, 5 * DC:6 * DC]
        if stop == "A":
            return finish(modx[:, :], b_modx, 6 * DC)

        class NormT:
            def __init__(self, stack, tag):
                self.xt = [sb(stack, f"{tag}_x{i}", [128, D], F32) for i in range(2)]
                self.xs = [sb(stack, f"{tag}_xs0", [128, D], BF16)] * 2
                self.junk = self.xs[0]
                self.ss = [sb(stack, f"{tag}_ss{i}", [128, 1], F32) for i in range(2)]
                self.rs = [sb(stack, f"{tag}_rs{i}", [128, 1], F32) for i in range(2)]
                self.k = 0

            def run(self, src, n, gsv, b_gsv, shv, b_shv, dst, b_dst):
                k = self.k; self.k ^= 1
                xt, bx = self.xt[k]; xs, bxs = self.xs[k]; jk, bj = self.junk
                ss, bss = self.ss[k]; rs, brs = self.rs[k]
                S.dma("sp", xt[:n, :], src, writes=[bx])
                S.op("act", lambda e: e.activation(out=jk[:n, :], in_=xt[:n, :], func=AF.Square, accum_out=ss[:n, :]),
                     reads=[bx], writes=[bj, bss])
                rstd_from_ss(ss[:n, :], bss, rs[:n, :], brs, D)
                S.op("dve", lambda e: e.tensor_scalar(out=xs[:n, :], in0=xt[:n, :], scalar1=rs[:n, 0:1], scalar2=None,
                                                      op0=ALU.mult), reads=[bx, brs], writes=[bxs])
                for g0 in range(0, DC, 8):
                    g1 = min(DC, g0 + 8)
                    pt, pb = psum()
                    pv = pbf(pt)
                    for dc in range(g0, g1):
                        S.op("pe", lambda e: e.transpose(out=pv[:, (dc - g0) * 128:(dc - g0) * 128 + n],
                                                         in_=xs[:n, dc * 128:(dc + 1) * 128], identity=ident_b[:n, :n]),
                             reads=[bxs, b_idb], writes=[pb])
                    for dc in range(g0, g1):
                        src_ps = pv[:, (dc - g0) * 128:(dc - g0) * 128 + n]
                        if dc % 2 == 0:
                            S.op("act", lambda e: e.activation(out=dst[:, dc, :], in_=src_ps, func=AF.Identity,
                                                               bias=shv[:, dc:dc + 1], scale=gsv[:, dc:dc + 1]),
                                 reads=[pb, b_gsv, b_shv], writes=[b_dst])
                        else:
                            S.op("dve", lambda e: e.tensor_scalar(out=dst[:, dc, :], in0=src_ps, scalar1=gsv[:, dc:dc + 1],
                                                                  scalar2=shv[:, dc:dc + 1], op0=ALU.mult, op1=ALU.add),
                                 reads=[pb, b_gsv, b_shv], writes=[b_dst])

        def cast_load(dst, b_dst, src):
            S.dma("pool", dst, src, writes=[b_dst])

        def mm_acc(pt_ap, pb, pairs, reads):
            n = len(pairs)
            for i, (l, r) in enumerate(pairs):
                S.op("pe", lambda e: e.matmul(pt_ap, lhsT=l, rhs=r, start=(i == 0), stop=(i == n - 1)),
                     reads=reads, writes=[pb])

        cqT, b_cqT = sb(kvs, "cqT", [128, QC, TL], BF16)

        with ExitStack() as ph:
            hxo, b_hxo = sb(ph, "hxo", [128, DC, TL], BF16)
            hxh, b_hxh = sb(ph, "hxh", [128, DC, 2], BF16)
            with ExitStack() as st:
                nt_ = NormT(st, "nC")
                for tt in range(NTL):
                    nt_.run(x_own[tt * 128:(tt + 1) * 128, :], 128, gs1, b_gs1, sh1, b_modx,
                            hxo[:, :, tt * 128:(tt + 1) * 128], b_hxo)
                nt_.run(x_halo[:, :], 2, gs1, b_gs1, sh1, b_modx, hxh[:, :, :], b_hxh)
                S.barrier()
            with ExitStack() as st:
                wqa, b_wqa = sb(st, "wqa", [128, DC, QL], BF16)
                for dc in range(DC):
                    cast_load(wqa[:, dc, :], b_wqa, w_in[dc * 128:(dc + 1) * 128, 0:QL])
                cqn, b_cqn = sb(st, "cqn", [128, QL], BF16)
                jq, b_jq = sb(st, "jq", [128, 512], BF16)
                ssq, b_ssq = sb(st, "ssq", [128, 4], F32)
                rsq, b_rsq = sb(st, "rsq", [128, 1], F32)
                pieces = [(o, min(512, QL - o)) for o in range(0, QL, 512)]
                for tt in range(NTL):
                    pts = []
                    for pi, (o, w) in enumerate(pieces):
                        pt, pb = psum()
                        mm_acc(pt[:, 0:w], pb, [(hxo[:, dc, tt * 128:(tt + 1) * 128], wqa[:, dc, o:o + w]) for dc in range(DC)],
                               [b_hxo, b_wqa])
                        S.op("act", lambda e: e.activation(out=jq[:, 0:w], in_=pt[:, 0:w], func=AF.Square,
                                                           accum_out=ssq[:, pi:pi + 1]), reads=[pb], writes=[b_jq, b_ssq])
                        pts.append((pt, pb))
                    if len(pieces) > 1:
                        S.op("dve", lambda e: e.tensor_reduce(out=ssq[:, 3:4], in_=ssq[:, 0:len(pieces)], axis=AX.X, op=ALU.add),
                             reads=[b_ssq], writes=[b_ssq])
                        sst = ssq[:, 3:4]
                    else:
                        sst = ssq[:, 0:1]
                    rstd_from_ss(sst, b_ssq, rsq[:], b_rsq, QL)
                    for (o, w), (pt, pb) in zip(pieces, pts):
                        S.op("dve", lambda e: e.tensor_scalar(out=cqn[:, o:o + w], in0=pt[:, 0:w], scalar1=rsq[:, 0:1],
                                                              scalar2=None, op0=ALU.mult), reads=[pb, b_rsq], writes=[b_cqn])
                    pt, pb = psum()
                    pv = pbf(pt)
                    for cc in range(QC):
                        S.op("pe", lambda e: e.transpose(out=pv[:, cc * 128:(cc + 1) * 128], in_=cqn[:, cc * 128:(cc + 1) * 128],
                                                         identity=ident_b[:]), reads=[b_cqn, b_idb], writes=[pb])
                    for cc in range(QC):
                        S.op("act", lambda e: e.activation(out=cqT[:, cc, tt * 128:(tt + 1) * 128], in_=pv[:, cc * 128:(cc + 1) * 128],
                                                           func=AF.Identity, scale=gq[:, cc:cc + 1]),
                             reads=[pb, b_gq], writes=[b_cqT])
                S.barrier()
            with ExitStack() as st:
                cw, b_cw = sb(st, "cw", [128, 3, CCH], F32)
                cb, b_cb = sb(st, "cb", [128, CCH], F32)
                hm, b_hm = sb(st, "hm", [128, 2], F32)
                for r in range(3):
                    S.dma("sp", cw[:, r, :], conv_w[r, :].rearrange("(j p) -> p j", p=128), writes=[b_cw])
                S.dma("sp", cb[:], conv_b.rearrange("(j p) -> p j", p=128), writes=[b_cb])
                S.dma("sp", hm[:], hmask[:, :], writes=[b_hm])
                wc = [[sb(st, f"wc{k}_{r}", [128, DC, 128], BF16) for r in range(3)] for k in range(2)]
                u_t, b_u = sb(st, "u_t", [128, TL + 2], F32)
                xin_s, b_xin = sb(st, "xin_s", [128, TL + 2], F32)
                gb_s, b_gbs = sb(st, "gb_s", [128, TL], F32)
                y_t, b_y = sb(st, "y_t", [128, TL], F32)
                cv = [sb(st, f"cv{k}", [128, TL], BF16) for k in range(2)]
                for j in range(CCH):
                    ws = wc[j % 2]
                    for r in range(3):
                        c0 = O_CONV + r * CC + j * 128
                        cast_load(ws[r][0][:], ws[r][1], w_in[:, c0:c0 + 128].rearrange("(dc p) n -> p dc n", p=128))
                    for tg in range(NQG):
                        pt, pb = psum()
                        mm_acc(pt[:, 0:QG], pb, [(ws[0][0][:, dc, :], hxo[:, dc, tg * QG:(tg + 1) * QG]) for dc in range(DC)],
                               [ws[0][1], b_hxo])
                        S.op("act", lambda e: e.copy(out=xin_s[:, 1 + tg * QG:1 + (tg + 1) * QG], in_=pt[:, 0:QG]),
                             reads=[pb], writes=[b_xin])
                    pt, pb = psum()
                    mm_acc(pt[:, 0:2], pb, [(ws[0][0][:, dc, :], hxh[:, dc, :]) for dc in range(DC)], [ws[0][1], b_hxh])
                    S.op("act", lambda e: e.copy(out=xin_s[:, 0:1], in_=pt[:, 0:1]), reads=[pb], writes=[b_xin])
                    S.op("act", lambda e: e.copy(out=xin_s[:, TL + 1:TL + 2], in_=pt[:, 1:2]), reads=[pb], writes=[b_xin])
                    for tg in range(NQG):
                        pt, pb = psum()
                        mm_acc(pt[:, 0:QG], pb, [(ws[1][0][:, dc, :], hxo[:, dc, tg * QG:(tg + 1) * QG]) for dc in range(DC)],
                               [ws[1][1], b_hxo])
                        S.op("act", lambda e: e.copy(out=gb_s[:, tg * QG:(tg + 1) * QG], in_=pt[:, 0:QG]),
                             reads=[pb], writes=[b_gbs])
                    for tg in range(NQG):
                        pt, pb = psum()
                        mm_acc(pt[:, 0:QG], pb, [(ws[2][0][:, dc, :], hxo[:, dc, tg * QG:(tg + 1) * QG]) for dc in range(DC)],
                               [ws[2][1], b_hxo])
                        S.op("dve", lambda e: e.tensor_tensor(out=u_t[:, 1 + tg * QG:1 + (tg + 1) * QG], in0=pt[:, 0:QG],
                                                              in1=xin_s[:, 1 + tg * QG:1 + (tg + 1) * QG], op=ALU.mult),
                             reads=[pb, b_xin], writes=[b_u])
                    pt, pb = psum()
                    mm_acc(pt[:, 0:2], pb, [(ws[2][0][:, dc, :], hxh[:, dc, :]) for dc in range(DC)], [ws[2][1], b_hxh])
                    for (hc, uc) in ((0, 0), (1, TL + 1)):
                        S.op("dve", lambda e: e.scalar_tensor_tensor(out=u_t[:, uc:uc + 1], in0=pt[:, hc:hc + 1],
                                                                     scalar=hm[:, hc:hc + 1], in1=xin_s[:, uc:uc + 1],
                                                                     op0=ALU.mult, op1=ALU.mult),
                             reads=[pb, b_hm, b_xin], writes=[b_u])
                    S.op("dve", lambda e: e.tensor_scalar(out=y_t[:], in0=u_t[:, 1:TL + 1], scalar1=cw[:, 1, j:j + 1],
                                                          scalar2=cb[:, j:j + 1], op0=ALU.mult, op1=ALU.add),
                         reads=[b_u, b_cw, b_cb], writes=[b_y])
                    S.op("dve", lambda e: e.scalar_tensor_tensor(out=y_t[:], in0=u_t[:, 0:TL], scalar=cw[:, 0, j:j + 1],
                                                                 in1=y_t[:], op0=ALU.mult, op1=ALU.add),
                         reads=[b_u, b_cw, b_y], writes=[b_y])
                    S.op("dve", lambda e: e.scalar_tensor_tensor(out=y_t[:], in0=u_t[:, 2:TL + 2], scalar=cw[:, 2, j:j + 1],
                                                                 in1=y_t[:], op0=ALU.mult, op1=ALU.add),
                         reads=[b_u, b_cw, b_y], writes=[b_y])
                    cvt, cvb = cv[j % 2]
                    S.op("dve", lambda e: e.tensor_tensor(out=cvt[:], in0=y_t[:], in1=gb_s[:], op=ALU.mult),
                         reads=[b_y, b_gbs], writes=[cvb])
                    S.dma("sp", convd[j * 128:(j + 1) * 128, :], cvt[:], reads=[cvb])
                S.barrier()
            S.barrier()

        if stop == "C":
            return finish(cqT[:, 0, 0:128], b_cqT, 128, is_bf=True)
        ckvT, b_ckvT = sb(kvs, "ckvT", [128, KC4, KT], BF16)
        krT, b_krT = sb(kvs, "krT", [128, KT], BF16)
        with ExitStack() as ph:
            nt_ = NormT(ph, "nB")
            wkv, b_wkv = sb(ph, "wkv", [128, DC, KVL + 64], BF16)
            for dc in range(DC):
                cast_load(wkv[:, dc, :], b_wkv, w_in[dc * 128:(dc + 1) * 128, O_CKV:O_CKV + KVL + 64])
            hxt = [sb(ph, f"hxt{i}", [128, DC, 128], BF16) for i in range(2)]
            jk2, b_jk2 = sb(ph, "jk2", [128, 512], BF16)
            ss2 = [sb(ph, f"ss2_{i}", [128, 1], F32) for i in range(2)]
            rs2 = [sb(ph, f"rs2_{i}", [128, 1], F32) for i in range(2)]
            ckn = [sb(ph, f"ckn{i}", [128, KVL], BF16) for i in range(2)]
            krs = [sb(ph, f"krs{i}", [128, 64], F32) for i in range(2)]
            rtb = [sb(ph, f"rtb{i}", [128, 64], F32) for i in range(2)]
            tmpr = [sb(ph, f"tmpr{i}", [128, 4, 32], F32) for i in range(2)]
            krd = [sb(ph, f"krd{i}", [128, 128], BF16) for i in range(2)]
            def front(i):
                k = i % 2
                is_x = i < NT
                src = x_full[i * 128:(i + 1) * 128, :] if is_x else ctx[(i - NT) * 128:(i - NT + 1) * 128, :]
                hx, bhx = hxt[k]
                if is_x:
                    nt_.run(src, 128, gs1, b_gs1, sh1, b_modx, hx[:, :, :], bhx)
                else:
                    nt_.run(src, 128, gs1c, b_gs1c, sh1c, b_modc, hx[:, :, :], bhx)

            front(0)
            for i in range(NT + NTC):
                k = i % 2
                is_x = i < NT
                hx, bhx = hxt[k]
                if i + 1 < NT + NTC:
                    front(i + 1)
                pa, pab = psum()
                mm_acc(pa[:, 0:KVL], pab, [(hx[:, dc, :], wkv[:, dc, 0:KVL]) for dc in range(DC)], [bhx, b_wkv])
                pk, pkb = psum()
                mm_acc(pk[:, 0:64], pkb, [(hx[:, dc, :], wkv[:, dc, KVL:KVL + 64]) for dc in range(DC)], [bhx, b_wkv])
                ss, bss = ss2[k]; rs, brs = rs2[k]; cn, bcn = ckn[k]
                S.op("act", lambda e: e.activation(out=jk2[:, 0:KVL], in_=pa[:, 0:KVL], func=AF.Square, accum_out=ss[:]),
                     reads=[pab], writes=[b_jk2, bss])
                rstd_from_ss(ss[:], bss, rs[:], brs, KVL)
                S.op("dve", lambda e: e.tensor_scalar(out=cn[:], in0=pa[:, 0:KVL], scalar1=rs[:, 0:1], scalar2=None,
                                                      op0=ALU.mult), reads=[pab, brs], writes=[bcn])
                kd, bkd = krd[k]
                if is_x:
                    kr, bkr = krs[k]; rt, brt = rtb[k]; tp, btp = tmpr[k]
                    S.dma("sp", rt[:], rope_all[i * 128:(i + 1) * 128, :], writes=[brt])
                    S.op("act", lambda e: e.copy(out=kr[:], in_=pk[:, 0:64]), reads=[pkb], writes=[bkr])
                    rope_apply(S, kr, bkr, rt, brt, tp, btp, kd[:, 0:64], bkd)
                    S.op("dve", lambda e: e.tensor_copy(out=kd[:, 64:128], in_=kd[:, 0:64]), reads=[bkd], writes=[bkd])
                else:
                    S.op("act", lambda e: e.copy(out=kd[:, 0:64], in_=pk[:, 0:64]), reads=[pkb], writes=[bkd])
                    S.op("act", lambda e: e.copy(out=kd[:, 64:128], in_=pk[:, 0:64]), reads=[pkb], writes=[bkd])
                pt, pb = psum()
                pv = pbf(pt)
                for cc in range(KC4):
                    S.op("pe", lambda e: e.transpose(out=pv[:, cc * 128:(cc + 1) * 128], in_=cn[:, cc * 128:(cc + 1) * 128],
                                                     identity=ident_b[:]), reads=[bcn, b_idb], writes=[pb])
                S.op("pe", lambda e: e.transpose(out=pv[:, KC4 * 128:(KC4 + 1) * 128], in_=kd[:], identity=ident_b[:]),
                     reads=[bkd, b_idb], writes=[pb])
                for cc in range(KC4):
                    S.op("act", lambda e: e.activation(out=ckvT[:, cc, i * 128:(i + 1) * 128], in_=pv[:, cc * 128:(cc + 1) * 128],
                                                       func=AF.Identity, scale=gkv[:, cc:cc + 1]),
                         reads=[pb, b_gkv], writes=[b_ckvT])
                S.op("dve", lambda e: e.tensor_copy(out=krT[:, i * 128:(i + 1) * 128], in_=pv[:, KC4 * 128:(KC4 + 1) * 128]),
                     reads=[pb], writes=[b_krT])
            S.barrier()

        if stop == "B":
            return finish(ckvT[:, 0, 0:128], b_ckvT, 128, is_bf=True)
        with ExitStack() as ph:
            attT, b_attT = sb(ph, "attT", [128, H, TL], BF16)
            ones_b, b_onb = sb(ph, "ones_b", [128, 128], BF16)
            S.op("dve", lambda e: e.memset(ones_b[:], 1.0), writes=[b_onb])
            wuk, b_wuk = sb(ph, "wuk", [128, KC4, 128], BF16)
            wuv, b_wuv = sb(ph, "wuv", [128, KC4, 128], BF16)
            wqn, b_wqn = sb(ph, "wqn", [128, QC, 128], BF16)
            wqr, b_wqr = sb(ph, "wqr", [128, QC, 128], BF16)
            knh, b_knh = sb(ph, "knh", [128, KT], BF16)
            vh, b_vh = sb(ph, "vh", [128, NKC, 128], BF16)
            qnh, b_qnh = sb(ph, "qnh", [128, TL], BF16)
            qrT, b_qrT = sb(ph, "qrT", [128, TL], BF16)
            qrs, b_qrs = sb(ph, "qrs", [128, 128], F32)
            qrr, b_qrr = sb(ph, "qrr", [128, 128], BF16)
            rto, b_rto = sb(ph, "rto", [128, NTL, 64], F32)
            tpq, b_tpq = sb(ph, "tpq", [128, 4, 32], F32)
            NPB = 3
            Pt = [sb(ph, f"Pt{i}", [128, QG], BF16) for i in range(NPB)]
            accs, b_accs = sb(ph, "accs", [128, QG], F32)
            rsum, b_rsum = sb(ph, "rsum", [128, QG], F32)
            for tt in range(NTL):
                S.dma("sp", rto[:, tt, :], rope_own[tt * 128:(tt + 1) * 128, :], writes=[b_rto])
            SB = [psum(reserve=True) for _ in range(NPB)]
            OB = psum(reserve=True)
            for h in range(H):
                cast_load(wuk[:], b_wuk, w_ukv[:, h * 256:h * 256 + 128].rearrange("(c p) n -> p c n", p=128))
                cast_load(wuv[:], b_wuv, w_ukv[:, h * 256 + 128:h * 256 + 256].rearrange("(c p) n -> p c n", p=128))
                cast_load(wqn[:], b_wqn, w_uq[:, h * 192:h * 192 + 128].rearrange("(c p) n -> p c n", p=128))
                if h % 2 == 0:
                    for hh in range(2):
                        c0 = (h + hh) * 192 + 128
                        cast_load(wqr[:, :, hh * 64:(hh + 1) * 64], b_wqr,
                                  w_uq[:, c0:c0 + 64].rearrange("(c p) n -> p c n", p=128))
                    for tt in range(NTL):
                        pt, pb = psum()
                        mm_acc(pt[:, 0:128], pb, [(cqT[:, cc, tt * 128:(tt + 1) * 128], wqr[:, cc, :]) for cc in range(QC)],
                               [b_cqT, b_wqr])
                        S.op("act", lambda e: e.copy(out=qrs[:], in_=pt[:, 0:128]), reads=[pb], writes=[b_qrs])
                        for hh in range(2):
                            rope_apply(S, qrs, b_qrs, rto, b_rto, tpq, b_tpq, qrr[:, hh * 64:(hh + 1) * 64], b_qrr,
                                       src_off=hh * 64, rt_idx=tt)
                        pt2, pb2 = psum()
                        pv2 = pbf(pt2)
                        S.op("pe", lambda e: e.transpose(out=pv2[:, 0:128], in_=qrr[:], identity=ident_b[:]),
                             reads=[b_qrr, b_idb], writes=[pb2])
                        S.op("act", lambda e: e.activation(out=qrT[:, tt * 128:(tt + 1) * 128], in_=pv2[:, 0:128], func=AF.Copy,
                                                           scale=SCALE), reads=[pb2], writes=[b_qrT])
                for tg in range(NQG):
                    pt, pb = psum()
                    mm_acc(pt[:, 0:QG], pb, [(wqn[:, cc, :], cqT[:, cc, tg * QG:(tg + 1) * QG]) for cc in range(QC)],
                           [b_wqn, b_cqT])
                    S.op("act", lambda e: e.activation(out=qnh[:, tg * QG:(tg + 1) * QG], in_=pt[:, 0:QG], func=AF.Copy,
                                                       scale=SCALE), reads=[pb], writes=[b_qnh])
                for k0 in range(0, KT, 512):
                    w = min(512, KT - k0)
                    pt, pb = psum()
                    mm_acc(pt[:, 0:w], pb, [(wuk[:, cc, :], ckvT[:, cc, k0:k0 + w]) for cc in range(KC4)], [b_wuk, b_ckvT])
                    if (k0 // 512) % 2 == 0:
                        S.op("dve", lambda e: e.tensor_copy(out=knh[:, k0:k0 + w], in_=pt[:, 0:w]), reads=[pb], writes=[b_knh])
                    else:
                        S.op("act", lambda e: e.copy(out=knh[:, k0:k0 + w], in_=pt[:, 0:w]), reads=[pb], writes=[b_knh])
                for kc0 in range(0, NKC, 4):
                    n4 = min(4, NKC - kc0)
                    pt, pb = psum()
                    for q4 in range(n4):
                        kc = kc0 + q4
                        mm_acc(pt[:, q4 * 128:(q4 + 1) * 128], pb,
                               [(ckvT[:, cc, kc * 128:(kc + 1) * 128], wuv[:, cc, :]) for cc in range(KC4)], [b_ckvT, b_wuv])
                    if (kc0 // 4) % 2 == 0:
                        S.op("act", lambda e: e.copy(out=vh[:, kc0:kc0 + n4, :], in_=r3(pt[:, 0:n4 * 128])), reads=[pb], writes=[b_vh])
                    else:
                        S.op("dve", lambda e: e.tensor_copy(out=vh[:, kc0:kc0 + n4, :], in_=r3(pt[:, 0:n4 * 128])),
                             reads=[pb], writes=[b_vh])
                r0 = 64 * (h % 2)
                for qg in range(NQG):
                    qs = slice(qg * QG, (qg + 1) * QG)

                    def s_mm(kc):
                        st_, sbb = SB[kc % NPB]
                        S.op("pe", lambda e: e.matmul(st_[:, 0:QG], lhsT=knh[:, kc * 128:(kc + 1) * 128], rhs=qnh[:, qs],
                                                      start=True, stop=False), reads=[b_knh, b_qnh], writes=[sbb])
                        S.op("pe", lambda e: e.matmul(st_[:, 0:QG], lhsT=krT[r0:r0 + 64, kc * 128:(kc + 1) * 128],
                                                      rhs=qrT[r0:r0 + 64, qs], start=False, stop=True),
                             reads=[b_krT, b_qrT], writes=[sbb])

                    s_mm(0)
                    if NKC > 1:
                        s_mm(1)
                    for kc in range(NKC):
                        st_, sbb = SB[kc % NPB]
                        p_t, p_b = Pt[kc % NPB]
                        S.op("act", lambda e: e.activation(out=p_t[:], in_=st_[:, 0:QG], func=AF.Exp), reads=[sbb], writes=[p_b])
                        S.op("pe", lambda e: e.matmul(OB[0][:, 0:QG], lhsT=vh[:, kc, :], rhs=p_t[:], start=(kc == 0),
                                                      stop=(kc == NKC - 1)), reads=[b_vh, p_b], writes=[OB[1]])
                        if kc == 0:
                            S.op("dve", lambda e: e.tensor_copy(out=accs[:], in_=p_t[:]), reads=[p_b], writes=[b_accs])
                        else:
                            S.op("dve", lambda e: e.tensor_tensor(out=accs[:], in0=accs[:], in1=p_t[:], op=ALU.add),
                                 reads=[p_b, b_accs], writes=[b_accs])
                        if kc + 2 < NKC:
                            s_mm(kc + 2)
                    pt, pb = psum()
                    S.op("pe", lambda e: e.matmul(pt[:, 0:QG], lhsT=ones_f[:], rhs=accs[:], start=True, stop=True),
                         reads=[b_ones, b_accs], writes=[pb])
                    S.op("dve", lambda e: e.reciprocal(out=rsum[:], in_=pt[:, 0:QG]), reads=[pb], writes=[b_rsum])
                    S.op("dve", lambda e: e.tensor_tensor(out=attT[:, h, qs], in0=OB[0][:, 0:QG], in1=rsum[:], op=ALU.mult),
                         reads=[OB[1], b_rsum], writes=[b_attT])
            S.dma("sp", attd.rearrange("(h p) t -> p h t", p=128), attT[:], reads=[b_attT])
            S.barrier()
            ps_res.clear()
            if stop == "D":
                return finish(attT[:, 0, 0:128], b_attT, 128, is_bf=True, close=[ph])
        S.barrier()
        kvs.close()

        with ExitStack() as ph:
            g1b, b_g1b = bcast_rows(ph, "g1b", gate1, b_modx)
            if stop == "E0":
                return finish(g1b[:, 0:D], b_g1b, D, close=[ph])
            mixT, b_cvT = sb(ph, "mixT", [128, MC, TL], BF16)
            b_attT = b_cvT
            S.dma("sp", mixT[:, 0:H, :], attd.rearrange("(h p) t -> p h t", p=128), writes=[b_cvT])
            S.dma("sp", mixT[:, H:MC, :], convd.rearrange("(j p) t -> p j t", p=128), writes=[b_cvT])
            cvT = mixT[:, H:MC, :]
            if stop == "E1":
                return finish(cvT[:, 0, 0:128], b_cvT, 128, is_bf=True, close=[ph])
            wo = [sb(ph, f"wo{i}", [128, MC, DW], BF16) for i in range(2)]
            xp = [sb(ph, f"xp{i}", [128, DW], F32) for i in range(3)]
            tq = [sb(ph, f"tq{i}", [128, DW], F32) for i in range(3)]
            it = 0
            for dt in range(NDW):
                wt, wb = wo[dt % 2]
                for mc in range(MC):
                    cast_load(wt[:, mc, :], wb, w_o[mc * 128:(mc + 1) * 128, dt * DW:(dt + 1) * DW])
                for tt in range(NTL):
                    ts_ = slice(tt * 128, (tt + 1) * 128)
                    pt, pb = psum()
                    pairs = [(mixT[:, mc, ts_], wt[:, mc, :]) for mc in range(MC)]
                    mm_acc(pt[:, 0:DW], pb, pairs, [b_attT, b_cvT, wb])
                    xt_, xb_ = xp[it % 3]; tt_, tb_ = tq[it % 3]; it += 1
                    S.dma("sp", xt_[:], x_own[ts_, dt * DW:(dt + 1) * DW], writes=[xb_])
                    S.op("dve", lambda e: e.tensor_tensor(out=tt_[:], in0=pt[:, 0:DW], in1=g1b[:, dt * DW:(dt + 1) * DW], op=ALU.mult),
                         reads=[pb, b_g1b], writes=[tb_])
                    S.op("dve", lambda e: e.tensor_tensor(out=tt_[:], in0=tt_[:], in1=xt_[:], op=ALU.add),
                         reads=[tb_, xb_], writes=[tb_])
                    S.dma("sp", x1d[ts_, dt * DW:(dt + 1) * DW], tt_[:], reads=[tb_])
            S.barrier()
        S.barrier()

        if stop == "E":
            return finish_dram(x1d[0:128, 0:DW], DW)
        mid.close()
        with ExitStack() as peer:
            h2T, b_h2T = sb(peer, "h2T", [128, DC, TL], BF16)
            with ExitStack() as st:
                nt_ = NormT(st, "nF")
                for tt in range(NTL):
                    nt_.run(x1d[tt * 128:(tt + 1) * 128, :], 128, gs2, b_gs2, sh2, b_modx, h2T[:, :, tt * 128:(tt + 1) * 128], b_h2T)
                S.barrier()
            if stop == "F":
                return finish(h2T[:, 0, 0:128], b_h2T, 128, is_bf=True, close=[peer])
            with ExitStack() as st:
                kT, b_kT = sb(st, "kT", [128, HP, 128], BF16)
                qT, b_qT = sb(st, "qT", [128, HP, TL], BF16)
                with ExitStack() as st2:
                    kf, b_kf = sb(st2, "kf", [128, HP, 128], F32)
                    kb_, b_kb = sb(st2, "kb", [128, HP, 128], BF16)
                    S.dma("sp", kf[:], peer_keys.rearrange("h p k d -> k (h p) d"), writes=[b_kf])
                    S.op("dve", lambda e: e.tensor_copy(out=kb_[:], in_=kf[:]), reads=[b_kf], writes=[b_kb])
                    for g0 in range(0, HP, 8):
                        pt, pb = psum()
                        pv = pbf(pt)
                        for hp in range(g0, min(HP, g0 + 8)):
                            S.op("pe", lambda e: e.transpose(out=pv[:, (hp - g0) * 128:(hp - g0 + 1) * 128], in_=kb_[:, hp, :],
                                                             identity=ident_b[:]), reads=[b_kb, b_idb], writes=[pb])
                        n8 = min(HP, g0 + 8) - g0
                        S.op("act", lambda e: e.copy(out=kT[:, g0:g0 + n8, :], in_=r3(pv[:, 0:n8 * 128])), reads=[pb], writes=[b_kT])
                    wq = [sb(st2, f"wq{i}", [128, DC, 128], BF16) for i in range(2)]
                    for hp in range(HP):
                        wt, wb = wq[hp % 2]
                        cast_load(wt[:], wb, peer_wq[:, hp * 128:(hp + 1) * 128].rearrange("(dc p) n -> p dc n", p=128))
                        for tg in range(NQG):
                            pt, pb = psum()
                            mm_acc(pt[:, 0:QG], pb, [(wt[:, dc, :], h2T[:, dc, tg * QG:(tg + 1) * QG]) for dc in range(DC)], [wb, b_h2T])
                            S.op("act", lambda e: e.copy(out=qT[:, hp, tg * QG:(tg + 1) * QG], in_=pt[:, 0:QG]), reads=[pb], writes=[b_qT])
                    S.barrier()
                ub = [sb(st, f"ub{i}", [128, D], BF16) for i in range(2)]
                uT = [sb(st, f"uT{i}", [128, DC, 128], BF16) for i in range(2)]
                wgo = [sb(st, f"wgo{i}", [128, TL], BF16) for i in range(2)]
                LW = min(2048, D)

                NEC = NE // 128

                def u_load(ec):
                    ut, utb = ub[ec % 2]
                    cast_load(ut[:].rearrange("p (a n) -> p a n", n=LW), utb,
                              peer_u[ec * 128:(ec + 1) * 128, :].rearrange("p (a n) -> p a n", n=LW))

                tr_banks = {}
                mm_banks = {}
                T_BANKS = [psum(reserve=True) for _ in range(-(-DC // 8))]
                M_BANKS = [psum(reserve=True) for _ in range(NQG)]

                def T_pe(ec):
                    if ec + 1 < NEC:
                        u_load(ec + 1)
                    ut, utb = ub[ec % 2]
                    tr_banks[ec] = []
                    for g0 in range(0, DC, 8):
                        g1 = min(DC, g0 + 8)
                        pt, pb = T_BANKS[g0 // 8]
                        pv = pbf(pt)
                        for dc in range(g0, g1):
                            S.op("pe", lambda e: e.transpose(out=pv[:, (dc - g0) * 128:(dc - g0 + 1) * 128],
                                                             in_=ut[:, dc * 128:(dc + 1) * 128], identity=ident_b[:]),
                                 reads=[utb, b_idb], writes=[pb])
                        tr_banks[ec].append((g0, g1, pv, pb))

                def T_act(ec):
                    uTt, uTb = uT[ec % 2]
                    for (g0, g1, pv, pb) in tr_banks.pop(ec):
                        S.op("act", lambda e: e.copy(out=uTt[:, g0:g1, :], in_=r3(pv[:, 0:(g1 - g0) * 128])),
                             reads=[pb], writes=[uTb])

                def M_pe(ec):
                    uTt, uTb = uT[ec % 2]
                    mm_banks[ec] = []
                    for tg in range(NQG):
                        pt, pb = M_BANKS[tg]
                        mm_acc(pt[:, 0:QG], pb, [(uTt[:, dc, :], h2T[:, dc, tg * QG:(tg + 1) * QG]) for dc in range(DC)], [uTb, b_h2T])
                        mm_banks[ec].append((tg, pt, pb))

                def M_act(ec):
                    wg, wgb = wgo[ec % 2]
                    for (tg, pt, pb) in mm_banks.pop(ec):
                        S.op("act", lambda e: e.activation(out=wg[:, tg * QG:(tg + 1) * QG], in_=pt[:, 0:QG], func=AF.Gelu),
                             reads=[pb], writes=[wgb])
                    S.dma("sp", WgT[ec * 128:(ec + 1) * 128, :], wg[:], reads=[wgb])

                slices = [(T_pe, T_act, 0)]
                for j in range(NEC):
                    if j + 1 < NEC:
                        slices.append((T_pe, T_act, j + 1))
                    slices.append((M_pe, M_act, j))
                u_load(0)
                n_steps = NTL * 32
                per_step = -(-len(slices) // n_steps)
                sl_state = {"next": 0, "pending": []}

                def slices_pe():
                    for _ in range(per_step):
                        k = sl_state["next"]
                        if k < len(slices):
                            pe_fn, act_fn, ec = slices[k]
                            pend = sl_state["pending"]
                            same = T_act if pe_fn is T_pe else M_act
                            idx = [i for i, (f_, e_) in enumerate(pend) if f_ is same or (pe_fn is M_pe and f_ is T_act and e_ == ec)]
                            if idx:
                                slices_act(idx[-1] + 1)
                            pe_fn(ec)
                            sl_state["pending"].append((act_fn, ec))
                            sl_state["next"] = k + 1

                def slices_act(upto=None):
                    pend = sl_state["pending"]
                    n = len(pend) if upto is None else min(upto, len(pend))
                    for _ in range(n):
                        act_fn, ec = pend.pop(0)
                        act_fn(ec)

                s_sb, b_s = sb(st, "s_sb", [128, HP, 128], F32)
                tmpm, b_tmpm = sb(st, "tmpm", [128, 128], F32)
                t16, b_t16 = sb(st, "t16", [128, HP, 16], F32)
                cand, b_cand = sb(st, "cand", [128, 16, 16], F32)
                cand2, b_cand2 = sb(st, "cand2", [128, 256], F32)
                c8, b_c8 = sb(st, "c8", [128, 16], F32)
                nm, b_nm = sb(st, "nm", [128, HP], F32)
                e16, b_e16 = sb(st, "e16", [128, 2, 16], F32)
                p16, b_p16 = sb(st, "p16", [128, 16, 16], F32)
                jz, b_jz = sb(st, "jz", [128, 256], F32)
                Zs, b_Zs = sb(st, "Zs", [128, PH], F32)
                lnz, b_lnz = sb(st, "lnz", [128, PH], F32)
                negc, b_negc = sb(st, "negc", [128, PH], F32)
                thr, b_thr = sb(st, "thr", [128, PH], F32)
                B1, b_B1 = sb(st, "B1", [128, PH, 128], F32)
                R1, b_R1 = sb(st, "R1", [128, PH, 128], F32)
                NPT = 12
                Pex = [(sb(st, f"Pex{i}", [128, 512], BF16)[0], [Buf() for _ in range(4)]) for i in range(NPT)]
                Tmk = [(sb(st, f"Tmk{i}", [128, 512], BF16)[0], [Buf()] * 4) for i in range(NPT)]
                wtok = [sb(st, f"wtok{i}", [128, 512], BF16) for i in range(2)]
                WTs = [sb(st, f"WTs{i}", [128, 16, 128], BF16) for i in range(2)]
                PTOK = psum(reserve=True)
                PTR = psum(reserve=True)
                WdT_v = WdT.rearrange("(i1 i2) t -> i2 i1 t", i2=128)
                ipx = 0
                for tt in range(NTL):
                    ts_ = slice(tt * 128, (tt + 1) * 128)
                    for g0 in range(0, HP, 4):
                        pt, pb = (PTOK, PTR)[(g0 // 4) % 2]
                        for hp in range(g0, g0 + 4):
                            S.op("pe", lambda e: e.matmul(pt[:, (hp - g0) * 128:(hp - g0 + 1) * 128], lhsT=qT[:, hp, ts_],
                                                          rhs=kT[:, hp, :], start=True, stop=True),
                                 reads=[b_qT, b_kT], writes=[pb])
                        S.op("act", lambda e: e.copy(out=s_sb[:, g0:g0 + 4, :], in_=r3(pt[:, 0:512])), reads=[pb], writes=[b_s])
                    for hp in range(HP):
                        S.op("dve", lambda e: e.max(out=t16[:, hp, 0:8], in_=s_sb[:, hp, :]), reads=[b_s], writes=[b_t16])
                        S.op("dve", lambda e: e.match_replace(out=tmpm[:], in_to_replace=t16[:, hp, 0:8], in_values=s_sb[:, hp, :],
                                                              imm_value=NEG), reads=[b_s, b_t16], writes=[b_tmpm])
                        S.op("dve", lambda e: e.max(out=t16[:, hp, 8:16], in_=tmpm[:]), reads=[b_tmpm], writes=[b_t16])
                    S.op("dve", lambda e: e.tensor_scalar(out=nm[:], in0=t16[:, :, 0], scalar1=-1.0, scalar2=None, op0=ALU.mult),
                         reads=[b_t16], writes=[b_nm])
                    for h in range(PH):
                        for b in range(16):
                            S.op("dve", lambda e: e.tensor_scalar(out=cand[:, :, b], in0=t16[:, 2 * h, :],
                                                                  scalar1=t16[:, 2 * h + 1, b:b + 1], scalar2=None, op0=ALU.add),
                                 reads=[b_t16], writes=[b_cand])
                        cflat = cand[:].rearrange("p a b -> p (a b)")
                        S.op("dve", lambda e: e.max(out=c8[:, 0:8], in_=cflat), reads=[b_cand], writes=[b_c8])
                        S.op("dve", lambda e: e.match_replace(out=cand2[:], in_to_replace=c8[:, 0:8], in_values=cflat,
                                                              imm_value=NEG), reads=[b_cand, b_c8], writes=[b_cand2])
                        S.op("dve", lambda e: e.max(out=c8[:, 8:16], in_=cand2[:]), reads=[b_cand2], writes=[b_c8])
                        S.op("dve", lambda e: e.tensor_copy(out=thr[:, h:h + 1], in_=c8[:, 15:16]), reads=[b_c8], writes=[b_thr])
                        for p_ in range(2):
                            S.op("act", lambda e: e.activation(out=e16[:, p_, :], in_=t16[:, 2 * h + p_, :], func=AF.Exp,
                                                               bias=nm[:, 2 * h + p_:2 * h + p_ + 1]),
                                 reads=[b_t16, b_nm], writes=[b_e16])
                        for b in range(16):
                            S.op("dve", lambda e: e.tensor_scalar(out=p16[:, :, b], in0=e16[:, 0, :], scalar1=e16[:, 1, b:b + 1],
                                                                  scalar2=None, op0=ALU.mult), reads=[b_e16], writes=[b_p16])
                        S.op("dve", lambda e: e.scalar_tensor_tensor(out=jz[:], in0=cflat, scalar=thr[:, h:h + 1],
                                                                     in1=p16[:].rearrange("p a b -> p (a b)"), op0=ALU.is_ge,
                                                                     op1=ALU.mult, accum_out=Zs[:, h:h + 1]),
                             reads=[b_cand, b_thr, b_p16], writes=[b_jz, b_Zs])
                        S.op("dve", lambda e: e.tensor_scalar(out=R1[:, h, :], in0=s_sb[:, 2 * h, :], scalar1=-1.0,
                                                              scalar2=thr[:, h:h + 1], op0=ALU.mult, op1=ALU.add),
                             reads=[b_s, b_thr], writes=[b_R1])
                    S.op("act", lambda e: e.activation(out=lnz[:], in_=Zs[:], func=AF.Ln), reads=[b_Zs], writes=[b_lnz])
                    nm2 = nm[:].rearrange("p (h two) -> p h two", two=2)
                    S.op("dve", lambda e: e.tensor_tensor(out=negc[:], in0=nm2[:, :, 0], in1=nm2[:, :, 1], op=ALU.add),
                         reads=[b_nm], writes=[b_negc])
                    S.op("dve", lambda e: e.tensor_tensor(out=negc[:], in0=negc[:], in1=lnz[:], op=ALU.subtract),
                         reads=[b_negc, b_lnz], writes=[b_negc])
                    for h in range(PH):
                        S.op("dve", lambda e: e.tensor_scalar(out=B1[:, h, :], in0=s_sb[:, 2 * h, :], scalar1=negc[:, h:h + 1],
                                                              scalar2=None, op0=ALU.add), reads=[b_s, b_negc], writes=[b_B1])
                    def tr_pe(gp):
                        wk, wkb = wtok[gp % 2]
                        pv = pbf(PTR[0])
                        for ii in range(4):
                            S.op("pe", lambda e: e.transpose(out=pv[:, ii * 128:(ii + 1) * 128], in_=wk[:, ii * 128:(ii + 1) * 128],
                                                             identity=ident_b[:]), reads=[wkb, b_idb], writes=[PTR[1]])

                    def tr_act(gp):
                        wts, wtb = WTs[(gp // 4) % 2]
                        pv = pbf(PTR[0])
                        q4 = (gp % 4) * 4
                        S.op("act", lambda e: e.copy(out=wts[:, q4:q4 + 4, :], in_=r3(pv[:, 0:512])), reads=[PTR[1]], writes=[wtb])
                        if gp % 4 == 3:
                            g16 = gp // 4
                            S.dma("sp", WdT_v[:, g16 * 16:(g16 + 1) * 16, ts_], wts[:, :, :], reads=[wtb])

                    for g in range(32):
                        prev_pending = len(sl_state["pending"])
                        slices_pe()
                        if g > 0:
                            tr_pe(g - 1)
                        GH = min(4, PH)
                        for hg in range(0, PH, GH):
                            slots = {}
                            for h in range(hg, hg + GH):
                                slots[h] = (Pex[ipx % NPT], Tmk[ipx % NPT]); ipx += 1
                            for ii in range(4):
                                i1 = g * 4 + ii
                                cs = slice(ii * 128, (ii + 1) * 128)
                                for h in range(hg, hg + GH):
                                    (px, pxb), (tm_, tmb) = slots[h]
                                    S.op("act", lambda e: e.activation(out=px[:, cs], in_=s_sb[:, 2 * h + 1, :], func=AF.Exp,
                                                                       bias=B1[:, h, i1:i1 + 1]), reads=[b_s, b_B1], writes=[pxb[ii]])
                                    S.op("dve", lambda e: e.scalar_tensor_tensor(out=tm_[:, cs], in0=s_sb[:, 2 * h + 1, :],
                                                                                 scalar=R1[:, h, i1:i1 + 1], in1=px[:, cs],
                                                                                 op0=ALU.is_ge, op1=ALU.mult),
                                         reads=[b_s, b_R1, pxb[ii]], writes=[tmb[ii]])
                            for h in range(hg, hg + GH):
                                (px, pxb), (tm_, tmb) = slots[h]
                                S.op("pe", lambda e: e.matmul(PTOK[0][:, 0:512], lhsT=ident_b[:], rhs=tm_[:, 0:512],
                                                              start=(h == 0), stop=(h == PH - 1)),
                                     reads=tmb + [b_idb], writes=[PTOK[1]])
                        slices_act(prev_pending)
                        if g > 0:
                            tr_act(g - 1)
                        wk, wkb = wtok[g % 2]
                        S.op("act", lambda e: e.copy(out=wk[:], in_=PTOK[0][:, 0:512]), reads=[PTOK[1]], writes=[wkb])
                    tr_pe(31)
                    tr_act(31)
                while sl_state["next"] < len(slices) or sl_state["pending"]:
                    pp = len(sl_state["pending"])
                    slices_pe()
                    slices_act(pp if pp > 0 else None)
                S.barrier()
                ps_res.clear()
        S.barrier()
        if stop == "P2a":
            return finish_dram(WgT[0:128, 0:128], 128, is_bf=True)
        with ExitStack() as ph:
            acc, b_acc = sb(ph, "acc", [128, NTL, D], F32)
            G = 8
            with ExitStack() as st:
                wgl = [sb(st, f"wgl{i}", [128, G, TL], BF16) for i in range(2)]
                wtl2, b_wtl2 = sb(st, "wtl2", [128, G, TL], BF16)
                vt = [sb(st, f"vt{i}", [128, G, DW], BF16) for i in range(2)]
                iv = 0
                NEG_ = NE // (128 * G)

                def load_wg(eg):
                    wl, wlb = wgl[eg % 2]
                    S.dma("sp", wl[:], WgT[eg * 128 * G:(eg + 1) * 128 * G, :].rearrange("(c p) t -> p c t", p=128), writes=[wlb])
                    S.dma("sp", wtl2[:], WdT[eg * 128 * G:(eg + 1) * 128 * G, :].rearrange("(c p) t -> p c t", p=128), writes=[b_wtl2])
                    S.op("dve", lambda e: e.tensor_tensor(out=wl[:], in0=wl[:], in1=wtl2[:], op=ALU.mult),
                         reads=[wlb, b_wtl2], writes=[wlb])

                load_wg(0)
                for eg in range(NEG_):
                    wl, wlb = wgl[eg % 2]
                    if eg + 1 < NEG_:
                        load_wg(eg + 1)
                    for dt in range(NDW):
                        v_, vb_ = vt[iv % 2]; iv += 1
                        cast_load(v_[:], vb_, peer_v[eg * 128 * G:(eg + 1) * 128 * G, dt * DW:(dt + 1) * DW].rearrange(
                            "(c p) n -> p c n", p=128))
                        for tt in range(NTL):
                            pt, pb = psum()
                            mm_acc(pt[:, 0:DW], pb, [(wl[:, c, tt * 128:(tt + 1) * 128], v_[:, c, :]) for c in range(G)], [wlb, vb_])
                            dst = acc[:, tt, dt * DW:(dt + 1) * DW]
                            if eg == 0:
                                S.op("act", lambda e: e.copy(out=dst, in_=pt[:, 0:DW]), reads=[pb], writes=[b_acc])
                            else:
                                S.op("dve", lambda e: e.tensor_tensor(out=dst, in0=dst, in1=pt[:, 0:DW], op=ALU.add),
                                     reads=[pb, b_acc], writes=[b_acc])
                S.barrier()
            S.barrier()
            with ExitStack() as st:
                g2b, b_g2b = bcast_rows(st, "g2b", gate2, b_modx)
                gfb, b_gfb = bcast_rows(st, "gfb", gfin, b_gfin)
                x1t = [sb(st, f"x1t{i}", [128, D], F32) for i in range(1)]
                jf, b_jf = sb(st, "jf", [128, D], BF16)
                ssf = [sb(st, f"ssf{i}", [128, 1], F32) for i in range(2)]
                rsf = [sb(st, f"rsf{i}", [128, 1], F32) for i in range(2)]
                for tt in range(NTL):
                    xt_, xb_ = x1t[0]
                    ss, bss = ssf[tt % 2]; rs, brs = rsf[tt % 2]
                    S.dma("sp", xt_[:], x1d[tt * 128:(tt + 1) * 128, :], writes=[xb_])
                    a_ = acc[:, tt, :]
                    S.op("dve", lambda e: e.tensor_tensor(out=a_, in0=a_, in1=g2b[:], op=ALU.mult), reads=[b_acc, b_g2b], writes=[b_acc])
                    S.op("dve", lambda e: e.tensor_tensor(out=a_, in0=a_, in1=xt_[:], op=ALU.add), reads=[b_acc, xb_], writes=[b_acc])
                    S.op("act", lambda e: e.activation(out=jf[:], in_=a_, func=AF.Square, accum_out=ss[:]),
                         reads=[b_acc], writes=[b_jf, bss])
                    rstd_from_ss(ss[:], bss, rs[:], brs, D)
                    S.op("dve", lambda e: e.scalar_tensor_tensor(out=a_, in0=a_, scalar=rs[:, 0:1], in1=gfb[:], op0=ALU.mult,
                                                                 op1=ALU.mult), reads=[b_acc, brs, b_gfb], writes=[b_acc])
                    S.dma("sp", out_d[tt * 128:(tt + 1) * 128, :], a_, reads=[b_acc])
                S.barrier()
        S.final_wait("sp")
        S.barrier()
    print("built: instrs", S.n_instr, "sems", S.nsem)
    return nc


def rope_apply(S, src, b_src, rt, b_rt, tp, b_tp, dst, b_dst, src_off=0, rt_idx=None):
    x = src[:, src_off:src_off + 64].rearrange("p (a h f) -> p a h f", a=2, h=2)
    r = (rt[:, rt_idx, :] if rt_idx is not None else rt[:, :]).rearrange("p (a s f) -> p a s f", a=2, s=2)
    x1 = x[:, :, 0, :]; x2 = x[:, :, 1, :]
    cs = r[:, :, 0, :]; sn = r[:, :, 1, :]
    t = tp[:].rearrange("p k (a f) -> p k a f", a=2)
    o = dst.rearrange("p (a h f) -> p a h f", a=2, h=2)
    rd = [b_src, b_rt]
    S.op("dve", lambda e: e.tensor_tensor(out=t[:, 0], in0=x1, in1=cs, op=ALU.mult), reads=rd, writes=[b_tp])
    S.op("dve", lambda e: e.tensor_tensor(out=t[:, 1], in0=x2, in1=sn, op=ALU.mult), reads=rd, writes=[b_tp])
    S.op("dve", lambda e: e.tensor_tensor(out=t[:, 2], in0=x2, in1=cs, op=ALU.mult), reads=rd, writes=[b_tp])
    S.op("dve", lambda e: e.tensor_tensor(out=t[:, 3], in0=x1, in1=sn, op=ALU.mult), reads=rd, writes=[b_tp])
    S.op("dve", lambda e: e.tensor_tensor(out=o[:, :, 0, :], in0=t[:, 0], in1=t[:, 1], op=ALU.subtract), reads=[b_tp], writes=[b_dst])
    S.op("dve", lambda e: e.tensor_tensor(out=o[:, :, 1, :], in0=t[:, 2], in1=t[:, 3], op=ALU.add), reads=[b_tp], writes=[b_dst])


def rope_table(T, grid_w=64, theta=10000.0):
    rows = T // grid_w
    row = np.repeat(np.arange(rows), grid_w).astype(np.float32)
    col = np.tile(np.arange(grid_w), rows).astype(np.float32)
    half = 32
    freqs = (np.float32(theta) ** (-np.arange(0, half, 2, dtype=np.float32) / np.float32(half))).astype(np.float32)
    ar = row[:, None] * freqs
    ac = col[:, None] * freqs
    return np.concatenate([np.cos(ar), np.sin(ar), np.cos(ac), np.sin(ac)], axis=1).astype(np.float32)


_NC_CACHE = {}


def run(cfg, inputs, stop=None):
    key = tuple(sorted(cfg.items())) + (stop,)
    if key not in _NC_CACHE:
        _NC_CACHE[key] = build(cfg, stop)
    nc = _NC_CACHE[key]
    D = cfg["D"]; T = cfg["T"]; TL = T // NCORES
    f = lambda a: np.ascontiguousarray(np.asarray(a, dtype=np.float32))
    x = f(inputs["x"])[0]
    rope = rope_table(T)
    shared = {
        "x_full": x, "ctx": f(inputs["ctx"])[0], "c": f(inputs["c"])[0], "c_ctx": f(inputs["c_ctx"]),
        "w_ada": f(inputs["w_ada"])[0], "b_ada": f(inputs["b_ada"])[0], "norm1_g": f(inputs["norm1_g"])[0],
        "w_in": f(inputs["w_in"])[0], "q_norm_g": f(inputs["q_norm_g"])[0], "w_uq": f(inputs["w_uq"])[0],
        "kv_norm_g": f(inputs["kv_norm_g"])[0], "w_ukv": f(inputs["w_ukv"])[0], "conv_w": f(inputs["conv_w"])[0],
        "conv_b": f(inputs["conv_b"])[0], "w_o": f(inputs["w_o"])[0], "norm2_g": f(inputs["norm2_g"])[0],
        "peer_wq": f(inputs["peer_wq"])[0], "peer_keys": f(inputs["peer_keys"])[0], "peer_u": f(inputs["peer_u"])[0],
        "peer_v": f(inputs["peer_v"])[0], "final_norm_g": f(inputs["final_norm_g"]), "rope_all": rope,
        "ident": np.eye(128, dtype=np.float32),
    }
    in_maps = []
    for c in range(NCORES):
        m = dict(shared)
        m["x_own"] = np.ascontiguousarray(x[c * TL:(c + 1) * TL])
        halo = np.zeros((2, D), np.float32)
        hm = np.zeros((128, 2), np.float32)
        if c > 0:
            halo[0] = x[c * TL - 1]; hm[:, 0] = 1.0
        if c < NCORES - 1:
            halo[1] = x[(c + 1) * TL]; hm[:, 1] = 1.0
        m["x_halo"] = halo
        m["hmask"] = hm
        m["rope_own"] = np.ascontiguousarray(rope[c * TL:(c + 1) * TL])
        in_maps.append(m)
    res = run_bass_kernel_spmd(nc, in_maps, core_ids=list(range(NCORES)))
    out = np.concatenate([np.asarray(res.results[c]["out"], dtype=np.float32) for c in range(NCORES)], axis=0)
    return out[None]


def kernel(**inputs):
    return run(CFG_FULL, inputs)
```
